# Optimizing a Trainium2 kernel written in Bass

```python
import jax, jax.numpy as jnp
from jax import lax
import numpy as np

D_MODEL = 2048
BATCH = 1
SEQ = 16384
DEPTH = 1

MEM_LEN = 256
D_FF = 5632
CONV_CH = 1536
CONV_WIDTH = 31
MLA_HEADS = 12
Q_LORA = 512
KV_LORA = 512
QK_NOPE = 128
QK_ROPE = 64
QK_DIM = QK_NOPE + QK_ROPE
V_HEAD = 128
MEM_HEADS = 4
MEM_HEAD_DIM = 256
N_BRANCH = 3
ROPE_THETA = 10000.0
EPS = 1e-6
Q_BLOCK = 128
IN_SIZES = (2 * CONV_CH, Q_LORA, KV_LORA, QK_ROPE, MEM_HEADS * MEM_HEAD_DIM, N_BRANCH * D_MODEL)
IN_COLS = 2 * CONV_CH + Q_LORA + KV_LORA + QK_ROPE + MEM_HEADS * MEM_HEAD_DIM + N_BRANCH * D_MODEL

kernel_name = "hybrid_gated_conformer_mla_memory_layer"


def _split_points(sizes):
    pts, acc = [], 0
    for s in sizes[:-1]:
        acc += s
        pts.append(acc)
    return pts


def rmsnorm(x, g):
    xf = x.astype(jnp.float32)
    y = xf * lax.rsqrt(jnp.mean(xf * xf, axis=-1, keepdims=True) + EPS)
    return (y * g.astype(jnp.float32)).astype(x.dtype)


def layernorm(x, g, b):
    xf = x.astype(jnp.float32)
    mu = jnp.mean(xf, axis=-1, keepdims=True)
    xc = xf - mu
    y = xc * lax.rsqrt(jnp.mean(xc * xc, axis=-1, keepdims=True) + EPS)
    return (y * g.astype(jnp.float32) + b.astype(jnp.float32)).astype(x.dtype)


def swiglu_ffn(h, w_gu, w_down):
    g, u = jnp.split(h @ w_gu, 2, axis=-1)
    return (jax.nn.silu(g) * u) @ w_down


def rope_tables(positions):
    inv_freq = ROPE_THETA ** (-jnp.arange(0, QK_ROPE, 2, dtype=jnp.float32) / QK_ROPE)
    ang = positions.astype(jnp.float32)[..., None] * inv_freq
    return jnp.cos(ang)[:, :, None, :], jnp.sin(ang)[:, :, None, :]


def apply_rope(x, cos, sin):
    xf = x.astype(jnp.float32)
    x1, x2 = jnp.split(xf, 2, axis=-1)
    return jnp.concatenate([x1 * cos - x2 * sin, x2 * cos + x1 * sin], axis=-1).astype(x.dtype)


def causal_attention_blocked(q, k, v, scale):
    B, S, H, Dk = q.shape
    Dv = v.shape[-1]
    nb = S // Q_BLOCK
    qb = q.reshape(B, nb, Q_BLOCK, H, Dk).transpose(1, 0, 2, 3, 4)
    key_pos = jnp.arange(S)

    def one_block(args):
        i, q_i = args
        s = jnp.einsum('bqhd,bkhd->bhqk', q_i, k, preferred_element_type=jnp.float32) * scale
        q_pos = i * Q_BLOCK + jnp.arange(Q_BLOCK)
        mask = key_pos[None, :] <= q_pos[:, None]
        s = jnp.where(mask[None, None], s, -jnp.inf)
        p = jax.nn.softmax(s, axis=-1)
        return jnp.einsum('bhqk,bkhd->bqhd', p.astype(v.dtype), v)

    out = lax.map(one_block, (jnp.arange(nb), qb))
    return out.transpose(1, 0, 2, 3, 4).reshape(B, S, H, Dv)


def setup_inputs(seed: int = 0) -> dict:
    key = jax.random.key(seed)
    ks = iter(jax.random.split(key, 40))
    L = DEPTH

    def w(shape, fan_in):
        return jax.random.normal(next(ks), shape, jnp.float32) * (fan_in ** -0.5)

    def gain(shape):
        return 1.0 + 0.02 * jax.random.normal(next(ks), shape, jnp.float32)

    def bias(shape):
        return 0.01 * jax.random.normal(next(ks), shape, jnp.float32)

    x = jax.random.normal(next(ks), (BATCH, SEQ, D_MODEL), jnp.float32)
    mem = jax.random.normal(next(ks), (BATCH, MEM_LEN, D_MODEL), jnp.float32)
    positions = jnp.broadcast_to(jnp.arange(SEQ, dtype=jnp.int32)[None, :], (BATCH, SEQ))
    return {
        "x": x,
        "mem": mem,
        "positions": positions,
        "g_ffn1": gain((L, D_MODEL)),
        "w_ffn1_gu": w((L, D_MODEL, 2 * D_FF), D_MODEL),
        "w_ffn1_down": w((L, D_FF, D_MODEL), D_FF),
        "g_mix": gain((L, D_MODEL)),
        "w_in": w((L, D_MODEL, IN_COLS), D_MODEL),
        "b_gate": bias((L, N_BRANCH * D_MODEL)),
        "conv_w": w((L, CONV_WIDTH, CONV_CH), CONV_WIDTH),
        "conv_b": bias((L, CONV_CH)),
        "conv_ln_g": gain((L, CONV_CH)),
        "conv_ln_b": bias((L, CONV_CH)),
        "w_conv_out": w((L, CONV_CH, D_MODEL), CONV_CH),
        "g_q_a": gain((L, Q_LORA)),
        "w_uq": w((L, Q_LORA, MLA_HEADS * QK_DIM), Q_LORA),
        "g_kv_a": gain((L, KV_LORA)),
        "w_ukv": w((L, KV_LORA, MLA_HEADS * (QK_NOPE + V_HEAD)), KV_LORA),
        "g_qnorm": gain((L, QK_DIM)),
        "g_knorm": gain((L, QK_DIM)),
        "w_mla_out": w((L, MLA_HEADS * V_HEAD, D_MODEL), MLA_HEADS * V_HEAD),
        "g_mem": gain((L, D_MODEL)),
        "w_mem_kv": w((L, D_MODEL, 2 * MEM_HEADS * MEM_HEAD_DIM), D_MODEL),
        "g_mqnorm": gain((L, MEM_HEAD_DIM)),
        "g_mknorm": gain((L, MEM_HEAD_DIM)),
        "w_mem_out": w((L, MEM_HEADS * MEM_HEAD_DIM, D_MODEL), MEM_HEADS * MEM_HEAD_DIM),
        "w_out": w((L, D_MODEL, D_MODEL), D_MODEL),
        "g_ffn2": gain((L, D_MODEL)),
        "w_ffn2_gu": w((L, D_MODEL, 2 * D_FF), D_MODEL),
        "w_ffn2_down": w((L, D_FF, D_MODEL), D_FF),
    }


def reference(x, mem, positions, g_ffn1, w_ffn1_gu, w_ffn1_down, g_mix, w_in, b_gate,
              conv_w, conv_b, conv_ln_g, conv_ln_b, w_conv_out, g_q_a, w_uq, g_kv_a, w_ukv,
              g_qnorm, g_knorm, w_mla_out, g_mem, w_mem_kv, g_mqnorm, g_mknorm, w_mem_out,
              w_out, g_ffn2, w_ffn2_gu, w_ffn2_down):
    B, S, D = x.shape
    cos, sin = rope_tables(positions)
    split_pts = _split_points(IN_SIZES)
    mla_scale = QK_DIM ** -0.5
    mem_scale = MEM_HEAD_DIM ** -0.5

    for l in range(DEPTH):
        x = x + 0.5 * swiglu_ffn(rmsnorm(x, g_ffn1[l]), w_ffn1_gu[l], w_ffn1_down[l])

        h = rmsnorm(x, g_mix[l])
        conv_in, cq, ckv, k_pe, mq, gate_logits = jnp.split(h @ w_in[l], split_pts, axis=-1)

        a = conv_in[..., :CONV_CH] * jax.nn.sigmoid(conv_in[..., CONV_CH:])
        a = lax.conv_general_dilated(
            a, conv_w[l][:, None, :], window_strides=(1,), padding=[(CONV_WIDTH - 1, 0)],
            dimension_numbers=('NWC', 'WIO', 'NWC'), feature_group_count=CONV_CH) + conv_b[l]
        a = jax.nn.silu(layernorm(a, conv_ln_g[l], conv_ln_b[l]))
        y_a = a @ w_conv_out[l]

        q = (rmsnorm(cq, g_q_a[l]) @ w_uq[l]).reshape(B, S, MLA_HEADS, QK_DIM)
        kv = (rmsnorm(ckv, g_kv_a[l]) @ w_ukv[l]).reshape(B, S, MLA_HEADS, QK_NOPE + V_HEAD)
        k_nope, v = kv[..., :QK_NOPE], kv[..., QK_NOPE:]
        k_rot = jnp.broadcast_to(k_pe[:, :, None, :], (B, S, MLA_HEADS, QK_ROPE))
        k = jnp.concatenate([k_nope, k_rot], axis=-1)
        q = rmsnorm(q, g_qnorm[l])
        k = rmsnorm(k, g_knorm[l])
        q = jnp.concatenate([q[..., :QK_NOPE], apply_rope(q[..., QK_NOPE:], cos, sin)], axis=-1)
        k = jnp.concatenate([k[..., :QK_NOPE], apply_rope(k[..., QK_NOPE:], cos, sin)], axis=-1)
        o = causal_attention_blocked(q, k, v, mla_scale)
        y_b = o.reshape(B, S, MLA_HEADS * V_HEAD) @ w_mla_out[l]

        mk, mv = jnp.split(rmsnorm(mem, g_mem[l]) @ w_mem_kv[l], 2, axis=-1)
        mk = rmsnorm(mk.reshape(B, MEM_LEN, MEM_HEADS, MEM_HEAD_DIM), g_mknorm[l])
        mv = mv.reshape(B, MEM_LEN, MEM_HEADS, MEM_HEAD_DIM)
        mq = rmsnorm(mq.reshape(B, S, MEM_HEADS, MEM_HEAD_DIM), g_mqnorm[l])
        ms = jnp.einsum('bshd,bmhd->bhsm', mq, mk, preferred_element_type=jnp.float32) * mem_scale
        mp = jax.nn.softmax(ms, axis=-1).astype(mv.dtype)
        om = jnp.einsum('bhsm,bmhd->bshd', mp, mv).reshape(B, S, MEM_HEADS * MEM_HEAD_DIM)
        y_c = om @ w_mem_out[l]

        gates = jax.nn.sigmoid(gate_logits + b_gate[l]).reshape(B, S, N_BRANCH, D)
        merged = gates[:, :, 0] * y_a + gates[:, :, 1] * y_b + gates[:, :, 2] * y_c
        x = x + merged @ w_out[l]

        x = x + 0.5 * swiglu_ffn(rmsnorm(x, g_ffn2[l]), w_ffn2_gu[l], w_ffn2_down[l])
    return x
```

```python
import numpy as np
import ml_dtypes
from contextlib import ExitStack
import concourse.bass as bass
import concourse.mybir as mybir
from concourse.bass_utils import run_bass_kernel_spmd

F32 = mybir.dt.float32
BF16 = mybir.dt.bfloat16
I32 = mybir.dt.int32
AF = mybir.ActivationFunctionType
ALU = mybir.AluOpType

NCORE = 8
D = 2048
DC = 16
DFF = 5632
FC = 44
T = 2048
G = 512
NG = 4
NT = 16
NTA = 128
TA = NTA * 128
NGA = TA // 512
CCH = 1536
CC = 12
H = 12
EPS = 1e-6
O_CV, O_CG, O_CQ, O_CKV, O_KPE, O_MQ, O_GATE = 0, 1536, 3072, 3584, 4096, 4160, 5184
MLA_SCALE = 192 ** -0.5
MEM_SCALE = 256 ** -0.5
TWO_PI = 2.0 * np.pi
STAGE = "full"


class Sched:
    def __init__(s, nc):
        s.nc = nc
        s.E = dict(pe=nc.tensor, act=nc.scalar, dve=nc.vector, pool=nc.gpsimd, sp=nc.sync)
        s.csem = {e: nc.alloc_semaphore("c_" + e) for e in ("pe", "act", "dve")}
        s.cnt = {e: 0 for e in s.csem}
        s.dsem = {}
        s.waited = {e: {} for e in s.E}
        s.res = {}
        s.nwait = 0

    def _wait(s, eng, evs):
        best = {}
        for ev in evs:
            if ev is None:
                continue
            sem, val = ev
            k = id(sem)
            if k not in best or best[k][1] < val:
                best[k] = ev
        for k, (sem, val) in best.items():
            if eng == "pe" and sem is s.csem["pe"]:
                continue
            if s.waited[eng].get(k, 0) >= val:
                continue
            s.E[eng].wait_ge(sem, val)
            s.nwait += 1
            s.waited[eng][k] = val

    def _deps(s, reads, writes):
        evs = []
        for k in reads:
            r = s.res.get(k)
            if r:
                evs.append(r[0])
                if k.startswith("ps"):
                    evs.extend(r[1].values())
        for k in writes:
            r = s.res.get(k)
            if r:
                evs.append(r[0])
                evs.extend(r[1].values())
        return evs

    def _commit(s, ev, reads, writes):
        kk = id(ev[0])
        for k in reads:
            r = s.res.setdefault(k, [None, {}])
            if kk not in r[1] or r[1][kk][1] < ev[1]:
                r[1][kk] = ev
        for k in writes:
            s.res[k] = [ev, {}]

    def op(s, eng, fn, reads=(), writes=()):
        s._wait(eng, s._deps(reads, writes))
        ins = fn(s.E[eng])
        s.cnt[eng] += 1
        ins.then_inc(s.csem[eng], 1)
        s._commit((s.csem[eng], s.cnt[eng]), reads, writes)

    def mm(s, mms, reads, writes, start=True, stop=True):
        s._wait("pe", s._deps(reads, writes))
        pe = s.E["pe"]
        n = len(mms)
        ins = None
        for i, (o, l, r) in enumerate(mms):
            ins = pe.matmul(o, l, r, start=(start and i == 0), stop=(stop and i == n - 1))
        s.cnt["pe"] += 1
        ins.then_inc(s.csem["pe"], 1)
        s._commit((s.csem["pe"], s.cnt["pe"]), reads, writes)

    def tr(s, outs_ins, ident, reads, writes):
        s._wait("pe", s._deps(reads, writes))
        pe = s.E["pe"]
        ins = None
        for o, i in outs_ins:
            ins = pe.transpose(o, i, ident)
        s.cnt["pe"] += 1
        ins.then_inc(s.csem["pe"], 1)
        s._commit((s.csem["pe"], s.cnt["pe"]), reads, writes)

    def dma(s, q, out, in_, reads, writes, slot, **kw):
        s._wait(q, s._deps(reads, writes))
        if slot not in s.dsem:
            s.dsem[slot] = [s.nc.alloc_semaphore("d_" + slot), 0]
        d = s.dsem[slot]
        s.E[q].dma_start(out=out, in_=in_, **kw).then_inc(d[0], 16)
        d[1] += 16
        s._commit((d[0], d[1]), reads, writes)

    def all_events(s):
        evs = [(s.csem[e], s.cnt[e]) for e in s.csem if s.cnt[e] > 0]
        evs += [(d[0], d[1]) for d in s.dsem.values() if d[1] > 0]
        return evs

    def fence(s, engs=("pe", "act", "dve", "sp")):
        evs = s.all_events()
        for e in engs:
            s._wait(e, evs)

    def final(s):
        evs = s.all_events()
        for e in ("sp", "pool", "act", "dve", "pe"):
            s._wait(e, evs)


def build():
    nc = bass.Bass("TRN2", target_bir_lowering=False)
    S = Sched(nc)

    def din(name, shape, dt=F32):
        return nc.dram_tensor(name, list(shape), dt, kind="ExternalInput").ap()

    x_d = din("x", [NTA, 128, D])
    mem_d = din("mem", [256, D])
    pos_d = din("pos", [64, TA], I32)
    g1_d = din("g_ffn1", [128, DC])
    gm_d = din("g_mix", [128, DC])
    g2_d = din("g_ffn2", [128, DC])
    gmem_d = din("g_mem", [128, DC])
    bg_d = din("b_gate", [128, 48])
    cw_d = din("conv_w", [128, CC, 31])
    cb_d = din("conv_b", [128, CC])
    lng_d = din("conv_ln_g", [128, CC])
    lnb_d = din("conv_ln_b", [128, CC])
    gqa_d = din("g_q_a", [128, 4])
    gkva_d = din("g_kv_a", [128, 4])
    gqn_d = din("gqn", [128, 1])
    gqr_d = din("gqr", [64, 1])
    gkn_d = din("gkn", [128, 1])
    gkr_d = din("gkr", [64, 1])
    gmq_d = din("g_mqnorm", [128, 2])
    gmk_d = din("g_mknorm", [128, 2])
    ident_d = din("ident", [128, 128])
    rot_d = din("rot", [64, 64])
    mask_d = din("mask", [128, 8, 128])
    sel_d = din("sel", [128, 8])
    invf_d = din("invf", [64, 1])
    w1gu = din("w_ffn1_gu", [D, 2 * DFF])
    w1d = din("w_ffn1_down", [DFF, D])
    w_in = din("w_in", [D, 11328])
    w_co = din("w_conv_out", [CCH, D])
    w_uq = din("w_uq", [512, H * 192])
    w_ukv = din("w_ukv", [512, H * 256])
    w_mo = din("w_mla_out", [H * 128, D])
    w_mkv = din("w_mem_kv", [D, 2048])
    w_meo = din("w_mem_out", [1024, D])
    w_out = din("w_out", [D, D])
    w2gu = din("w_ffn2_gu", [D, 2 * DFF])
    w2d = din("w_ffn2_down", [DFF, D])
    out_d = nc.dram_tensor("out", [NT, 128, D], F32, kind="ExternalOutput").ap()

    x1d = nc.dram_tensor("x1d", [128, DC, T], F32).ap()
    kloc = nc.dram_tensor("kloc", [H * 192, TA], BF16).ap()
    vloc = nc.dram_tensor("vloc", [H * TA, 128], BF16).ap()
    tloc = nc.dram_tensor("tloc", [CCH, 512], F32).ap()
    hd = nc.dram_tensor("hd", [CCH, 512], F32).ap()
    wbf_gu = nc.dram_tensor("wbf_gu", [2 * (FC // 2), 128, DC * 256], BF16).ap()
    wbf_d = nc.dram_tensor("wbf_d", [DC, 128, FC * 128], BF16).ap()

    sb = nc.alloc_sbuf_tensor
    ident = sb("ident_s", [128, 128], F32)
    rot = sb("rot_s", [64, 64], F32)
    mask = sb("mask_s", [128, 8, 128], F32)
    sel = sb("sel_s", [128, 8], F32)
    invf = sb("invf_s", [64, 1], F32)
    g1 = sb("g1_s", [128, DC], F32)
    gm = sb("gm_s", [128, DC], F32)
    g2 = sb("g2_s", [128, DC], F32)
    gmem = sb("gmem_s", [128, DC], F32)
    bg = sb("bg_s", [128, 48], F32)
    cw = sb("cw_s", [128, CC, 31], F32)
    cb = sb("cb_s", [128, CC], F32)
    lng = sb("lng_s", [128, CC], F32)
    lnb = sb("lnb_s", [128, CC], F32)
    gqa = sb("gqa_s", [128, 4], F32)
    gkva = sb("gkva_s", [128, 4], F32)
    gqn = sb("gqn_s", [128, 1], F32)
    gqr = sb("gqr_s", [64, 1], F32)
    gkn = sb("gkn_s", [128, 1], F32)
    gkr = sb("gkr_s", [64, 1], F32)
    gmq = sb("gmq_s", [128, 2], F32)
    gmk = sb("gmk_s", [128, 2], F32)
    ones_bf = sb("ones_bf", [128, 128], BF16)
    epst = sb("epst", [128, 1], F32)
    inv = {n: sb(f"inv{n}", [128, 128], F32) for n in (2048, 512, 192, 256, 1536)}
    mkT = sb("mkT", [128, 8, 256], BF16)
    mv = sb("mv", [128, 2, 1024], BF16)
    csg = sb("csg", [64, 2, G], F32)
    cs_d = nc.dram_tensor("cs_d", [64, 2, TA], F32).ap()
    rstd = [sb(f"rstd{i}", [128, G], F32) for i in range(2)]
    wring = [sb(f"wring{i}", [128, 5632], BF16) for i in range(3)]
    ps = [nc.alloc_psum_tensor(f"ps{i}", [128, G], F32) for i in range(8)]

    consts = [(ident, ident_d), (rot, rot_d), (mask, mask_d), (sel, sel_d), (invf, invf_d), (g1, g1_d),
              (gm, gm_d), (g2, g2_d), (gmem, gmem_d), (bg, bg_d), (cw, cw_d), (cb, cb_d), (lng, lng_d),
              (lnb, lnb_d), (gqa, gqa_d), (gkva, gkva_d), (gqn, gqn_d), (gqr, gqr_d), (gkn, gkn_d),
              (gkr, gkr_d), (gmq, gmq_d), (gmk, gmk_d)]
    for ci, (t, d_) in enumerate(consts):
        S.dma("sp", t[:], d_, [], [f"const{ci}"], "const")
    S.op("dve", lambda e: e.memset(ones_bf[:], 1.0), [], ["constA"])
    S.op("dve", lambda e: e.memset(epst[:], EPS), [], ["constB"])
    for n, t in inv.items():
        S.op("dve", lambda e, t=t, n=n: e.memset(t[:], 1.0 / n), [], [f"constI{n}"])
    S.fence(("pe", "act", "dve", "sp", "pool"))
    CONST = []

    wstate = {"i": 0, "n": 3}

    def wload(w, r0, KC, c0, ncols):
        i = wstate["i"] % wstate["n"]
        wstate["i"] += 1
        view = wring[i][:, 0:KC * ncols].rearrange("p (k n) -> p k n", k=KC)
        src = w[r0:r0 + KC * 128, c0:c0 + ncols].rearrange("(k p) n -> p k n", p=128)
        S.dma("pool", view, src, [], [f"w{i}"], f"w{i}")
        return view, f"w{i}"

    def wload_bf(src, KC, ncols, rkey):
        i = wstate["i"] % wstate["n"]
        wstate["i"] += 1
        S.dma("pool", wring[i][:, 0:KC * ncols], src, [rkey], [f"w{i}"], f"w{i}")
        return wring[i][:, 0:KC * ncols].rearrange("p (k n) -> p k n", k=KC), f"w{i}"

    pstate = {}

    def pbank(lo=0, n=4):
        c = pstate.get((lo, n), 0)
        pstate[(lo, n)] = c + 1
        i = lo + c % n
        return ps[i], f"ps{i}"

    rs = {"i": 0}

    def stat_rstd(parts, n, extra_reads, eps=EPS):
        i = rs["i"] % 2
        rs["i"] += 1
        N = parts[0][0].shape[-1]
        sbank, skey = ps[4 + i], f"ps{4 + i}"
        for j, (ap, P) in enumerate(parts):
            sq, sqk = sqbuf[j % 2], f"sq{j % 2}"
            S.op("act", lambda e, ap=ap, P=P, sq=sq: e.activation(out=sq[0:P, 0:N], in_=ap, func=AF.Square),
                 extra_reads, [sqk])
            S.mm([(sbank[:, 0:N], inv[n][0:P, :], sq[0:P, 0:N])], [sqk], [skey],
                 start=(j == 0), stop=(j == len(parts) - 1))
        r, rk = rstd[i], f"rstd{i}"
        S.op("act", lambda e: e.activation(out=r[:, 0:N], in_=sbank[:, 0:N], func=AF.Sqrt, bias=epst[:, 0:1]),
             [skey], [rk])
        S.op("dve", lambda e: e.reciprocal(r[:, 0:N], r[:, 0:N]), [rk], [rk])
        return r, rk

    sqbuf = [sb(f"sqb{i}", [128, G], F32) for i in range(2)]

    def rmsnorm_fm(src, skey, C, gain, n, out, okey, N=G):
        r, rk = stat_rstd([(src[:, c, :], 128) for c in range(C)], n, [skey])
        for c in range(C):
            S.op("dve", lambda e, c=c: e.scalar_tensor_tensor(out[:, c, :], src[:, c, :], gain[:, c:c + 1],
                                                             r[:, 0:N], ALU.mult, ALU.mult),
                 [skey, rk], [okey])

    def ffn(xT, xkey, xn, xnkey, hff, wgu, wdn, pre=None):
        for m2 in range(FC // 2):
            if pre is None:
                wg, wgk = wload(wgu, 0, DC, m2 * 256, 256)
                wu, wuk = wload(wgu, 0, DC, DFF + m2 * 256, 256)
            else:
                wg, wgk = wload_bf(pre[0][2 * m2], DC, 256, f"wbfg{2 * m2}")
                wu, wuk = wload_bf(pre[0][2 * m2 + 1], DC, 256, f"wbfg{2 * m2 + 1}")
            for mm_ in range(2):
                m = m2 * 2 + mm_
                pg, pgk = pbank(0, 4)
                pu, puk = pbank(0, 4)
                S.mm([(pg[:], wg[:, k, mm_ * 128:(mm_ + 1) * 128], xn[:, k, :]) for k in range(DC)],
                     [wgk, xnkey], [pgk])
                S.mm([(pu[:], wu[:, k, mm_ * 128:(mm_ + 1) * 128], xn[:, k, :]) for k in range(DC)],
                     [wuk, xnkey], [puk])
                sg, sgk = sgbuf[m % 2], f"sg{m % 2}"
                S.op("act", lambda e, pg=pg, sg=sg: e.activation(out=sg[:], in_=pg[:], func=AF.Silu), [pgk], [sgk])
                S.op("dve", lambda e, pu=pu, sg=sg, m=m: e.tensor_tensor(hff[:, m, :], sg[:], pu[:], ALU.mult),
                     [sgk, puk], ["hff"])
        for mo in range(DC):
            if pre is None:
                wd, wdk = wload(wdn, 0, FC, mo * 128, 128)
            else:
                wd, wdk = wload_bf(pre[1][mo], FC, 128, f"wbfd{mo}")
            pa, pak = pbank(6, 2)
            S.mm([(pa[:], wd[:, k, :], hff[:, k, :]) for k in range(FC)], [wdk, "hff"], [pak])
            S.op("dve", lambda e, pa=pa, mo=mo: e.scalar_tensor_tensor(xT[:, mo, :], pa[:], 0.5, xT[:, mo, :],
                                                                      ALU.mult, ALU.add),
                 [pak, xkey], [xkey])

    sgbuf = [sb(f"sgb{i}", [128, G], F32) for i in range(2)]

    with ExitStack() as es:
        posi = es.enter_context(nc.sbuf_tensor("posi", [64, T], I32))
        ang = es.enter_context(nc.sbuf_tensor("ang", [64, T], F32))
        ang2 = es.enter_context(nc.sbuf_tensor("ang2", [64, T], F32))
        sinn = es.enter_context(nc.sbuf_tensor("sinn", [64, T], F32))
        cosn = es.enter_context(nc.sbuf_tensor("cosn", [64, T], F32))
        for ch in range(TA // T):
            S.dma("sp", posi[:], pos_d[:, ch * T:(ch + 1) * T], [], ["posi"], "const")
            S.op("dve", lambda e: e.tensor_copy(ang[:], posi[:]), ["posi"], ["ang"])
            S.op("dve", lambda e: e.tensor_scalar(ang[:], ang[:], invf[:, 0:1], None, ALU.mult), ["ang"], ["ang"])
            for tgt, tk, shift in ((sinn, "sinn", 0.0), (cosn, "cosn", np.pi / 2)):
                S.op("dve", lambda e, shift=shift: e.tensor_scalar(ang2[:], ang[:], shift, 1.0 / TWO_PI, ALU.add, ALU.mult),
                     ["ang"], ["ang2"])
                S.op("dve", lambda e: e.tensor_copy(posi[:], ang2[:]), ["ang2"], ["posi"])
                S.op("dve", lambda e: e.tensor_copy(ang2[:], posi[:]), ["posi"], ["ang2"])
                S.op("dve", lambda e, tgt=tgt: e.scalar_tensor_tensor(tgt[:], ang2[:], -TWO_PI, ang[:], ALU.mult, ALU.add),
                     ["ang2", "ang"], [tk])
                if shift != 0.0:
                    S.op("dve", lambda e, tgt=tgt, shift=shift: e.tensor_scalar(tgt[:], tgt[:], shift, None, ALU.add),
                         [tk], [tk])
                S.op("act", lambda e, tgt=tgt: e.activation(out=tgt[:], in_=tgt[:], func=AF.Sin, scale=0.999999),
                     [tk], [tk])
            S.dma("sp", cs_d[:, 0, ch * T:(ch + 1) * T], sinn[:], ["sinn"], [f"cs_d{ch}"], "st0")
            S.dma("sp", cs_d[:, 1, ch * T:(ch + 1) * T], cosn[:], ["cosn"], [f"cs_d{ch}"], "st0")
        S.fence()

    def load_cs(g):
        S.dma("sp", csg[:], cs_d[:, :, g * G:(g + 1) * G], [], ["csg"], "ld1")

    def load_cs_own(g):
        for i in range(2):
            S.dma("sp", csg[:, i, :].rearrange("p (j t) -> p j t", j=4),
                  cs_d[:, i, :].rearrange("p (j s t) -> p j s t", s=8, t=128)[:, 4 * g:4 * g + 4, 0, :],
                  [], ["csg"], "ld1")

    def rope_apply(src, skey, N, outf, okey, tmp1, tmp2):
        pr, prk = pbank(0, 4)
        S.mm([(pr[0:64, 0:N], rot[:], src)], [skey], [prk])
        S.op("dve", lambda e: e.tensor_tensor(tmp1, pr[0:64, 0:N], csg[:, 0, 0:N], ALU.mult),
             [prk, "csg"], [okey + "t1"])
        S.op("dve", lambda e: e.tensor_tensor(tmp2, src, csg[:, 1, 0:N], ALU.mult),
             [skey, "csg"], [okey + "t2"])
        S.op("dve", lambda e: e.tensor_tensor(outf, tmp2, tmp1, ALU.subtract), [okey + "t1", okey + "t2"], [okey])

    with ExitStack() as es:
        memtok = es.enter_context(nc.sbuf_tensor("memtok", [128, 2, D], F32))
        memT = es.enter_context(nc.sbuf_tensor("memT", [128, DC, 256], F32))
        memn = es.enter_context(nc.sbuf_tensor("memn", [128, DC, 256], BF16))
        mkf = es.enter_context(nc.sbuf_tensor("mkf", [128, 8, 256], F32))
        S.dma("sp", memtok[:], mem_d.rearrange("(j p) d -> p j d", p=128), [], ["memtok"], "ld0")
        for j in range(2):
            for c4 in range(4):
                pb, pbk = pbank(0, 4)
                S.tr([(pb[:, q * 128:(q + 1) * 128], memtok[:, j, (c4 * 4 + q) * 128:(c4 * 4 + q + 1) * 128])
                      for q in range(4)], ident[:], ["memtok"], [pbk])
                S.op("act", lambda e, pb=pb, j=j, c4=c4: e.activation(
                    out=memT[:, c4 * 4:c4 * 4 + 4, j * 128:(j + 1) * 128],
                    in_=pb[:].rearrange("p (q t) -> p q t", q=4), func=AF.Copy), [pbk], ["memT"])
        rmsnorm_fm(memT, "memT", DC, gmem, 2048, memn, "memn", N=256)
        for m2 in range(4):
            wv, wk = wload(w_mkv, 0, DC, m2 * 256, 256)
            for q in range(2):
                m = m2 * 2 + q
                pb, pbk = pbank(0, 4)
                S.mm([(pb[:, 0:256], wv[:, k, q * 128:(q + 1) * 128], memn[:, k, :]) for k in range(DC)],
                     [wk, "memn"], [pbk])
                S.op("act", lambda e, pb=pb, m=m: e.activation(out=mkf[:, m, :], in_=pb[:, 0:256], func=AF.Copy),
                     [pbk], ["mkf"])
        for hh in range(4):
            r, rk = stat_rstd([(mkf[:, 2 * hh + q, :], 128) for q in range(2)], 256, ["mkf"])
            for q in range(2):
                S.op("dve", lambda e, hh=hh, q=q, r=r: e.scalar_tensor_tensor(
                    mkT[:, 2 * hh + q, :], mkf[:, 2 * hh + q, :], gmk[:, q:q + 1], r[:, 0:256], ALU.mult, ALU.mult),
                    ["mkf", rk], ["mkT"])
        for m2 in range(4):
            wv, wk = wload(w_mkv, 0, DC, 1024 + m2 * 256, 256)
            for j in range(2):
                pb, pbk = pbank(0, 4)
                S.mm([(pb[:, 0:256], memn[:, k, j * 128:(j + 1) * 128], wv[:, k, :]) for k in range(DC)],
                     [wk, "memn"], [pbk])
                S.op("act", lambda e, pb=pb, j=j, m2=m2: e.activation(
                    out=mv[:, j, m2 * 256:(m2 + 1) * 256], in_=pb[:, 0:256], func=AF.Copy), [pbk], ["mv"])
        S.fence()

    esA = ExitStack()
    for i_ in range(3, 6):
        wring.append(esA.enter_context(nc.sbuf_tensor(f"wringA{i_}", [128, 5632], BF16)))
    wstate["n"] = 6
    for m2 in range(FC // 2):
        for gu in range(2):
            b_ = 2 * m2 + gu
            wv, wk = wload(w1gu, 0, DC, gu * DFF + m2 * 256, 256)
            S.dma("sp", wbf_gu[b_], wv.rearrange("p k n -> p (k n)"), [wk], [f"wbfg{b_}"], f"wst{b_ % 3}")
    for mo in range(DC):
        wv, wk = wload(w1d, 0, FC, mo * 128, 128)
        S.dma("sp", wbf_d[mo], wv.rearrange("p k n -> p (k n)"), [wk], [f"wbfd{mo}"], f"wst{mo % 3}")
    S.fence(("pe", "act", "dve", "sp", "pool"))
    if STAGE == "P":
        S.final()
        return nc
    for g in range(NGA if STAGE != "A1" else 2):
        jsup, own = g // 2, (g % 2 == 0)
        with ExitStack() as es:
            sbt = lambda name, shape, dt: es.enter_context(nc.sbuf_tensor(f"{name}_a{g}", shape, dt))
            xT = sbt("xT", [128, DC, G], F32)
            xn = sbt("xn", [128, DC, G], BF16)
            load_cs(g)
            with ExitStack() as es0:
                xtok = es0.enter_context(nc.sbuf_tensor(f"xtok_a{g}", [128, 2, D], F32))
                for j in range(4):
                    S.dma("sp", xtok[:, j % 2, :], x_d[4 * g + j], [], [f"xtok{j % 2}"], f"ld{j % 2}")
                    for c4 in range(4):
                        pb, pbk = pbank(0, 4)
                        S.tr([(pb[:, q * 128:(q + 1) * 128], xtok[:, j % 2, (c4 * 4 + q) * 128:(c4 * 4 + q + 1) * 128])
                              for q in range(4)], ident[:], [f"xtok{j % 2}"], [pbk])
                        if c4 % 2:
                            S.op("act", lambda e, pb=pb, j=j, c4=c4: e.activation(
                                out=xT[:, c4 * 4:c4 * 4 + 4, j * 128:(j + 1) * 128],
                                in_=pb[:].rearrange("p (q t) -> p q t", q=4), func=AF.Copy), [pbk], ["xT"])
                        else:
                            S.op("dve", lambda e, pb=pb, j=j, c4=c4: e.tensor_copy(
                                xT[:, c4 * 4:c4 * 4 + 4, j * 128:(j + 1) * 128],
                                pb[:].rearrange("p (q t) -> p q t", q=4)), [pbk], ["xT"])
                S.fence()
            rmsnorm_fm(xT, "xT", DC, g1, 2048, xn, "xn")
            with ExitStack() as es0:
                hff = es0.enter_context(nc.sbuf_tensor(f"hff_a{g}", [128, FC, G], BF16))
                ffn(xT, "xT", xn, "xn", hff, w1gu, w1d, pre=(wbf_gu, wbf_d))
                S.fence()
            if own:
                S.dma("sp", x1d[:, :, jsup * 128:(jsup + 1) * 128], xT[:, :, 0:128], ["xT"], [f"x1d{jsup}"], "st0")
            rmsnorm_fm(xT, "xT", DC, gm, 2048, xn, "xn")
            with ExitStack() as es2:
                sb2 = lambda name, shape, dt: es2.enter_context(nc.sbuf_tensor(f"{name}_a{g}", shape, dt))
                ckv = sb2("ckv", [128, 4, G], F32)
                ckvn = sb2("ckvn", [128, 4, G], BF16)
                kpe = sb2("kpe", [64, G], F32)
                kper = sb2("kper", [64, G], F32)
                t1 = sb2("t1", [64, G], F32)
                t2 = sb2("t2", [64, G], F32)
                knf = sb2("knf", [128, G], F32)
                kout = sb2("kout", [128, 2, G], BF16)
                krout = sb2("krout", [64, 2, G], BF16)
                vsb = sb2("vsb", [128, 4, H, 128], BF16)
                tail = sb2("tail", [128, CC, 128], F32)
                sgt = sb2("sgt", [128, 2, 128], F32)
                for m2 in range(2):
                    wv, wk = wload(w_in, 0, DC, O_CKV + m2 * 256, 256)
                    for q in range(2):
                        pb, pbk = pbank(0, 4)
                        S.mm([(pb[:], wv[:, k, q * 128:(q + 1) * 128], xn[:, k, :]) for k in range(DC)],
                             [wk, "xn"], [pbk])
                        S.op("act", lambda e, pb=pb, m=m2 * 2 + q: e.activation(out=ckv[:, m, :], in_=pb[:],
                                                                               func=AF.Copy), [pbk], ["ckv"])
                wv, wk = wload(w_in, 0, DC, O_KPE, 64)
                pb, pbk = pbank(0, 4)
                S.mm([(pb[0:64, :], wv[:, k, :], xn[:, k, :]) for k in range(DC)], [wk, "xn"], [pbk])
                kpesq = sb2("kpesq", [64, G], F32)
                S.op("act", lambda e: e.activation(out=kpesq[:], in_=pb[0:64, :], func=AF.Square), [pbk], ["kpesq2"])
                S.op("dve", lambda e: e.tensor_scalar(kpe[:], pb[0:64, :], gkr[:, 0:1], None, ALU.mult),
                     [pbk], ["kpe"])
                rope_apply(kpe[:], "kpe", G, kper[:], "kper", t1[:], t2[:])
                rmsnorm_fm(ckv, "ckv", 4, gkva, 512, ckvn, "ckvn")
                wv, wk = None, None
                for h in range(H):
                    if h % 3 == 0:
                        wv, wk = wload(w_ukv, 0, 4, h * 256, 768)
                    hq = h % 3
                    pb, pbk = pbank(0, 4)
                    S.mm([(pb[:], wv[:, k, hq * 256:hq * 256 + 128], ckvn[:, k, :]) for k in range(4)],
                         [wk, "ckvn"], [pbk])
                    S.op("act", lambda e, pb=pb: e.activation(out=knf[:], in_=pb[:], func=AF.Copy), [pbk], ["knf"])
                    S.op("act", lambda e: e.activation(out=sqbuf[1][:], in_=knf[:], func=AF.Square), ["knf"], ["sq1"])
                    i = rs["i"] % 2
                    rs["i"] += 1
                    sbank, skey = ps[4 + i], f"ps{4 + i}"
                    S.mm([(sbank[:], inv[192][0:64, :], kpesq[:]), (sbank[:], inv[192][:], sqbuf[1][:])],
                         ["kpesq2", "sq1"], [skey])
                    r, rk = rstd[i], f"rstd{i}"
                    S.op("act", lambda e, r=r, sbank=sbank: e.activation(out=r[:], in_=sbank[:], func=AF.Sqrt,
                                                                         bias=epst[:, 0:1]), [skey], [rk])
                    S.op("dve", lambda e, r=r: e.reciprocal(r[:], r[:]), [rk], [rk])
                    S.op("dve", lambda e, r=r, h=h: e.scalar_tensor_tensor(kout[:, h % 2, :], knf[:], gkn[:, 0:1], r[:],
                                                                         ALU.mult, ALU.mult),
                         ["knf", rk], [f"kout{h % 2}"])
                    S.op("dve", lambda e, r=r, h=h: e.tensor_tensor(krout[:, h % 2, :], kper[:], r[0:64, :], ALU.mult),
                         ["kper", rk], [f"krout{h % 2}"])
                    S.dma("sp", kloc[h * 192:h * 192 + 128, g * G:(g + 1) * G], kout[:, h % 2, :],
                          [f"kout{h % 2}"], [f"kloc.{g}.{h}"], f"stk{h % 2}")
                    S.dma("sp", kloc[h * 192 + 128:h * 192 + 192, g * G:(g + 1) * G], krout[:, h % 2, :],
                          [f"krout{h % 2}"], [f"klocr.{g}.{h}"], f"stk{h % 2}")
                for h3 in range(4):
                    wv, wk = wload(w_ukv, 0, 4, h3 * 768, 768)
                    for j in range(4):
                        pb, pbk = pbank(0, 4)
                        S.mm([(pb[:, 0:384].rearrange("p (a d) -> p a d", a=3),
                               ckvn[:, k, j * 128:(j + 1) * 128],
                               wv[:, k, :].rearrange("p (a d) -> p a d", a=3)[:, :, 128:256]) for k in range(4)],
                             [wk, "ckvn"], [pbk])
                        S.op("act", lambda e, pb=pb, j=j, h3=h3: e.activation(
                            out=vsb[:, j, h3 * 3:h3 * 3 + 3, :],
                            in_=pb[:, 0:384].rearrange("p (a d) -> p a d", a=3), func=AF.Copy),
                            [pbk], [f"vsb{j}"])
                        if h3 == 3:
                            tok0 = (4 * g + j) * 128
                            S.dma("sp", vloc.rearrange("(h t) d -> t h d", h=H)[tok0:tok0 + 128, :, :],
                                  vsb[:, j, :, :], [f"vsb{j}"], [f"vloc.{g}.{j}"], f"stv{j % 2}")
                if not own:
                    for c2 in range(CC // 2):
                        wv, wk = wload(w_in, 0, DC, O_CV + c2 * 256, 256)
                        wg_, wgk = wload(w_in, 0, DC, O_CG + c2 * 256, 256)
                        for q in range(2):
                            cc = c2 * 2 + q
                            pv, pvk = pbank(0, 4)
                            pg, pgk = pbank(0, 4)
                            S.mm([(pv[:, 0:32], wv[:, k, q * 128:(q + 1) * 128], xn[:, k, 480:512]) for k in range(DC)],
                                 [wk, "xn"], [pvk])
                            S.mm([(pg[:, 0:32], wg_[:, k, q * 128:(q + 1) * 128], xn[:, k, 480:512]) for k in range(DC)],
                                 [wgk, "xn"], [pgk])
                            S.op("act", lambda e, pg=pg, cc=cc: e.activation(out=sgt[:, cc % 2, 0:32], in_=pg[:, 0:32],
                                                                             func=AF.Sigmoid), [pgk], [f"sgt{cc % 2}"])
                            S.op("dve", lambda e, pv=pv, cc=cc: e.tensor_tensor(tail[:, cc, 0:32], sgt[:, cc % 2, 0:32],
                                                                                pv[:, 0:32], ALU.mult),
                                 [pvk, f"sgt{cc % 2}"], ["tail"])
                    S.dma("sp", tloc.rearrange("(c p) n -> p c n", p=128)[:, :, jsup * 32:(jsup + 1) * 32], tail[:, :, 0:32],
                          ["tail"], [f"tloc.{g}"], "st1")
                S.fence()
            S.fence()

    if STAGE == "A":
        S.final()
        return nc
    S.fence(("pe", "act", "dve", "sp", "pool"))
    wstate["n"] = 3
    wstate["i"] = 0
    del wring[3:]
    esA.close()
    S.fence(("pe", "act", "dve", "sp", "pool"))
    with ExitStack() as es:
        hacc = es.enter_context(nc.sbuf_tensor("hacc", [128, CC, 512], F32))
        tb = es.enter_context(nc.sbuf_tensor("trk", [128, CC, 512], F32))
        S.dma("sp", tb[:], tloc.rearrange("(c p) n -> p c n", p=128), [], ["trk"], "ld0")
        S.op("dve", lambda e: e.tensor_scalar(hacc[:], tb[:], sel[:, 0:1], None, ALU.mult), ["trk"], ["hacc"])
        S.op("dve", lambda e: e.scalar_tensor_tensor(hacc[:, :, 32:512], tb[:, :, 0:480], sel[:, 1:2],
                                                     hacc[:, :, 32:512], ALU.mult, ALU.add), ["trk", "hacc"], ["hacc"])
        S.dma("sp", hd.rearrange("(c p) n -> p c n", p=128), hacc[:], ["hacc"], ["hd"], "st1")
        S.fence()
    if STAGE == "C":
        S.final()
        return nc
    hT = sb("hT", [128, DC, G], BF16)
    mT = sb("mT", [128, DC, G], BF16)
    for g in range(NG if STAGE == "full" else 1):
        J = g
        with ExitStack() as es:
            sbt = lambda name, shape, dt: es.enter_context(nc.sbuf_tensor(f"{name}_b{g}", shape, dt))
            load_cs_own(g)
            with ExitStack() as es1:
                x1T = es1.enter_context(nc.sbuf_tensor(f"x1Ta_b{g}", [128, DC, G], F32))
                S.dma("sp", x1T[:], x1d[:, :, g * G:(g + 1) * G], [], ["x1T"], "ld0")
                rmsnorm_fm(x1T, "x1T", DC, gm, 2048, hT, "hT")
                S.fence()
            merged = sbt("merged", [128, DC, G], F32)
            gsb = [sbt(f"gsb{i}", [128, G], F32) for i in range(2)]

            def branch_out(br, wy, KCy, act_in, akey):
                for m2 in range(DC // 2):
                    wv, wk = wload(wy, 0, KCy, m2 * 256, 256)
                    wg_, wgk = wload(w_in, 0, DC, O_GATE + br * D + m2 * 256, 256)
                    for q in range(2):
                        m = m2 * 2 + q
                        py, pyk = pbank(0, 4)
                        pg, pgk = pbank(0, 4)
                        S.mm([(py[:], wv[:, k, q * 128:(q + 1) * 128], act_in[:, k, :]) for k in range(KCy)],
                             [wk, akey], [pyk])
                        S.mm([(pg[:], wg_[:, k, q * 128:(q + 1) * 128], hT[:, k, :]) for k in range(DC)],
                             [wgk, "hT"], [pgk])
                        gb, gbk = gsb[m % 2], f"gsb{m % 2}"
                        S.op("act", lambda e, pg=pg, gb=gb, m=m: e.activation(
                            out=gb[:], in_=pg[:], func=AF.Sigmoid, bias=bg[:, br * DC + m:br * DC + m + 1]),
                            [pgk], [gbk])
                        if br == 0:
                            S.op("dve", lambda e, py=py, gb=gb, m=m: e.tensor_tensor(merged[:, m, :], gb[:], py[:],
                                                                                    ALU.mult), [pyk, gbk], ["merged"])
                        else:
                            S.op("dve", lambda e, py=py, gb=gb: e.tensor_tensor(gb[:], gb[:], py[:], ALU.mult),
                                 [pyk, gbk], [gbk])
                            S.op("dve", lambda e, gb=gb, m=m: e.tensor_tensor(merged[:, m, :], merged[:, m, :], gb[:],
                                                                             ALU.add), [gbk, "merged"], ["merged"])

            with ExitStack() as es2:
                sb2 = lambda name, shape, dt: es2.enter_context(nc.sbuf_tensor(f"{name}_b{g}", shape, dt))
                HC = CC // 2
                apad = sb2("apad", [128, HC, 4, 160], F32)
                cacc = sb2("cacc", [128, CC, G], F32)
                an = sb2("an", [128, CC, G], BF16)
                sgc = [sb2(f"sgc{i}", [128, G], F32) for i in range(2)]
                meansb = sb2("meansb", [128, G], F32)
                for half in range(2):
                    for cl in range(HC):
                        S.dma("sp", apad[:, cl, :, 0:32],
                              hd.rearrange("(c p) (j t) -> p c j t", p=128, t=32)[:, half * HC + cl, 4 * g:4 * g + 4, :],
                              ["hd"], [f"apadh{cl}"], "ld0")
                    for c2 in range(HC // 2):
                        wv, wk = wload(w_in, 0, DC, O_CV + half * HC * 128 + c2 * 256, 256)
                        wg_, wgk = wload(w_in, 0, DC, O_CG + half * HC * 128 + c2 * 256, 256)
                        for q in range(2):
                            cl = c2 * 2 + q
                            pv, pvk = pbank(0, 4)
                            pg, pgk = pbank(0, 4)
                            S.mm([(pv[:], wv[:, k, q * 128:(q + 1) * 128], hT[:, k, :]) for k in range(DC)], [wk, "hT"], [pvk])
                            S.mm([(pg[:], wg_[:, k, q * 128:(q + 1) * 128], hT[:, k, :]) for k in range(DC)], [wgk, "hT"], [pgk])
                            sg_, sgk = sgc[cl % 2], f"sgc{cl % 2}"
                            S.op("act", lambda e, pg=pg, sg_=sg_: e.activation(out=sg_[:], in_=pg[:], func=AF.Sigmoid),
                                 [pgk], [sgk])
                            S.op("dve", lambda e, pv=pv, sg_=sg_, cl=cl: e.tensor_tensor(
                                apad[:, cl, :, 32:160], sg_[:].rearrange("p (j t) -> p j t", j=4),
                                pv[:].rearrange("p (j t) -> p j t", j=4), ALU.mult), [pvk, sgk], [f"apad{cl}"])
                    for k in range(31):
                        for cl in range(HC):
                            cc = half * HC + cl
                            src = apad[:, cl, :, 2 + k:130 + k]
                            dst = cacc[:, cc, :].rearrange("p (j t) -> p j t", j=4)
                            if k == 0:
                                S.op("dve", lambda e, src=src, dst=dst, cc=cc: e.tensor_scalar(
                                    dst, src, cw[:, cc, 0:1], cb[:, cc:cc + 1], ALU.mult, ALU.add),
                                    [f"apad{cl}", f"apadh{cl}"], [f"cacc{cc}"])
                            else:
                                S.op("dve", lambda e, src=src, dst=dst, cc=cc, k=k: e.scalar_tensor_tensor(
                                    dst, src, cw[:, cc, k:k + 1], dst, ALU.mult, ALU.add),
                                    [f"apad{cl}", f"apadh{cl}", f"cacc{cc}"], [f"cacc{cc}"])
                pmean, pmk = ps[4], "ps4"
                pms, pmsk = ps[5], "ps5"
                S.mm([(pmean[:], inv[1536][:], cacc[:, cc, :]) for cc in range(CC)],
                     [f"cacc{cc}" for cc in range(CC)], [pmk])
                for cc in range(CC):
                    sq, sqk = sqbuf[cc % 2], f"sq{cc % 2}"
                    S.op("act", lambda e, sq=sq, cc=cc: e.activation(out=sq[:], in_=cacc[:, cc, :], func=AF.Square),
                         [f"cacc{cc}"], [sqk])
                    S.mm([(pms[:], inv[1536][:], sq[:])], [sqk], [pmsk], start=(cc == 0), stop=(cc == CC - 1))
                S.op("act", lambda e: e.activation(out=meansb[:], in_=pmean[:], func=AF.Copy), [pmk], ["meansb"])
                S.op("dve", lambda e: e.tensor_tensor(rstd[0][:], meansb[:], meansb[:], ALU.mult), ["meansb"], ["rstd0"])
                S.op("dve", lambda e: e.tensor_tensor(rstd[0][:], pms[:], rstd[0][:], ALU.subtract), [pmsk, "rstd0"], ["rstd0"])
                S.op("act", lambda e: e.activation(out=rstd[0][:], in_=rstd[0][:], func=AF.Sqrt, bias=epst[:, 0:1]),
                     ["rstd0"], ["rstd0"])
                S.op("dve", lambda e: e.reciprocal(rstd[0][:], rstd[0][:]), ["rstd0"], ["rstd0"])
                for cc in range(CC):
                    S.op("dve", lambda e, cc=cc: e.tensor_tensor(cacc[:, cc, :], cacc[:, cc, :], meansb[:], ALU.subtract),
                         [f"cacc{cc}", "meansb"], [f"cacc{cc}"])
                    S.op("dve", lambda e, cc=cc: e.tensor_tensor(cacc[:, cc, :], cacc[:, cc, :], rstd[0][:], ALU.mult),
                         [f"cacc{cc}", "rstd0"], [f"cacc{cc}"])
                    S.op("act", lambda e, cc=cc: e.activation(out=an[:, cc, :], in_=cacc[:, cc, :], func=AF.Silu,
                                                              bias=lnb[:, cc:cc + 1], scale=lng[:, cc:cc + 1]),
                         [f"cacc{cc}"], ["an"])
                branch_out(0, w_co, CC, an, "an")
                S.fence()

            with ExitStack() as es2:
                sb2 = lambda name, shape, dt: es2.enter_context(nc.sbuf_tensor(f"{name}_b{g}", shape, dt))
                QN = sb2("QN", [128, H, G], BF16)
                QR = sb2("QR", [64, H, G], BF16)
                oT = sb2("oT", [128, H, G], BF16)
                es3 = ExitStack()
                sb3 = lambda name, shape, dt: es3.enter_context(nc.sbuf_tensor(f"{name}_b{g}", shape, dt))
                cq = sb3("cq", [128, 4, G], F32)
                cqn = sb3("cqn", [128, 4, G], BF16)
                qnf = sb3("qnf", [128, G], F32)
                qrf = sb3("qrf", [64, G], F32)
                qrg = sb3("qrg", [64, G], F32)
                qrr = sb3("qrr", [64, G], F32)
                t1 = sb3("t1", [64, G], F32)
                t2 = sb3("t2", [64, G], F32)
                for m2 in range(2):
                    wv, wk = wload(w_in, 0, DC, O_CQ + m2 * 256, 256)
                    for q in range(2):
                        pb, pbk = pbank(0, 4)
                        S.mm([(pb[:], wv[:, k, q * 128:(q + 1) * 128], hT[:, k, :]) for k in range(DC)], [wk, "hT"], [pbk])
                        S.op("act", lambda e, pb=pb, m=m2 * 2 + q: e.activation(out=cq[:, m, :], in_=pb[:], func=AF.Copy),
                             [pbk], ["cq"])
                rmsnorm_fm(cq, "cq", 4, gqa, 512, cqn, "cqn")
                wv, wk = None, None
                for h in range(H):
                    if h % 4 == 0:
                        wv, wk = wload(w_uq, 0, 4, h * 192, 768)
                    hq = h % 4
                    pn, pnk = pbank(0, 4)
                    pr_, prk_ = pbank(0, 4)
                    S.mm([(pn[:], wv[:, k, hq * 192:hq * 192 + 128], cqn[:, k, :]) for k in range(4)], [wk, "cqn"], [pnk])
                    S.mm([(pr_[0:64, :], wv[:, k, hq * 192 + 128:hq * 192 + 192], cqn[:, k, :]) for k in range(4)],
                         [wk, "cqn"], [prk_])
                    S.op("act", lambda e, pn=pn: e.activation(out=qnf[:], in_=pn[:], func=AF.Copy), [pnk], ["qnf"])
                    S.op("act", lambda e, pr_=pr_: e.activation(out=qrf[:], in_=pr_[0:64, :], func=AF.Copy), [prk_], ["qrf"])
                    r, rk = stat_rstd([(qnf[:], 128), (qrf[:], 64)], 192, ["qnf", "qrf"])
                    S.op("dve", lambda e, r=r, h=h: e.scalar_tensor_tensor(QN[:, h, :], qnf[:], gqn[:, 0:1], r[:],
                                                                         ALU.mult, ALU.mult), ["qnf", rk], ["QN"])
                    S.op("dve", lambda e: e.tensor_scalar(qrg[:], qrf[:], gqr[:, 0:1], None, ALU.mult),
                         ["qrf"], ["qrg"])
                    rope_apply(qrg[:], "qrg", G, qrr[:], "qrr", t1[:], t2[:])
                    S.op("dve", lambda e, r=r, h=h: e.tensor_tensor(QR[:, h, :], qrr[:], r[0:64, :], ALU.mult),
                         ["qrr", rk], ["QR"])
                S.fence()
                es3.close()
                es3 = ExitStack()
                rden = sb3("rden", [128, G], F32)
                kn = [sb3(f"kn{i}", [128, 8, 128], BF16) for i in range(3)]
                kr = [sb3(f"kr{i}", [64, 8, 128], BF16) for i in range(3)]
                vv = [sb3(f"vv{i}", [128, 8, 128], BF16) for i in range(3)]
                PT = [sb3(f"PT{i}", [128, G], BF16) for i in range(3)]
                nsc = 4 * J + 4
                kvi = 0
                for h in range(H):
                    po, pok = ps[6], "ps6"
                    pd, pdk = ps[7], "ps7"
                    first = True
                    for sc in range(nsc):
                        i = kvi % 3
                        kvi += 1
                        kkey = f"kv{i}"
                        S.dma("sp", kn[i][:].rearrange("p r t -> p (r t)"),
                              kloc[h * 192:h * 192 + 128, sc * 1024:(sc + 1) * 1024], [], [kkey], f"kv{i}")
                        S.dma("sp", kr[i][:].rearrange("p r t -> p (r t)"),
                              kloc[h * 192 + 128:h * 192 + 192, sc * 1024:(sc + 1) * 1024], [], [kkey], f"kv{i}")
                        S.dma("sp", vv[i][:], vloc.rearrange("(h n s t) d -> t h n s d", h=H, s=8, t=128)[:, h, sc, :, :],
                              [], [kkey], f"kv{i}")
                        q0 = max(0, sc - 4 * J) * 128
                        N = G - q0
                        for kt in range(8):
                            pS, pSk = pbank(0, 4)
                            S.mm([(pS[:, 0:N], kn[i][:, kt, :], QN[:, h, q0:G]), (pS[:, 0:N], kr[i][:, kt, :], QR[:, h, q0:G])],
                                 [kkey, "QN", "QR"], [pSk])
                            pi_ = (kvi * 8 + kt) % 3
                            P_, Pk = PT[pi_], f"PT{pi_}"
                            S.op("act", lambda e, pS=pS, P_=P_, N=N: e.activation(out=P_[:, 0:N], in_=pS[:, 0:N], func=AF.Exp,
                                                                                 scale=MLA_SCALE), [pSk], [Pk])
                            if sc >= 4 * J:
                                S.op("dve", lambda e, P_=P_, kt=kt: e.tensor_tensor(P_[:, 0:128], P_[:, 0:128], mask[:, kt, :],
                                                                                    ALU.mult), [Pk], [Pk])
                            last = (sc == nsc - 1 and kt == 7)
                            S.mm([(po[:, q0:G], vv[i][:, kt, :], P_[:, 0:N])], [kkey, Pk], [pok], start=first, stop=last)
                            S.mm([(pd[:, q0:G], ones_bf[:], P_[:, 0:N])], [Pk], [pdk], start=first, stop=last)
                            first = False
                    S.op("dve", lambda e: e.reciprocal(rden[:], pd[:]), [pdk], ["rden"])
                    S.op("dve", lambda e, h=h: e.tensor_tensor(oT[:, h, :], po[:], rden[:], ALU.mult), [pok, "rden"], ["oT"])
                S.fence()
                es3.close()
                branch_out(1, w_mo, H, oT, "oT")
                S.fence()

            with ExitStack() as es2:
                sb2 = lambda name, shape, dt: es2.enter_context(nc.sbuf_tensor(f"{name}_b{g}", shape, dt))
                mq = sb2("mq", [128, 8, G], F32)
                mqn = sb2("mqn", [128, 8, G], BF16)
                om = sb2("om", [128, 8, G], BF16)
                PM = [sb2(f"PM{i}", [128, G], BF16) for i in range(2)]
                rden = sb2("rdenm", [128, G], F32)
                for m2 in range(4):
                    wv, wk = wload(w_in, 0, DC, O_MQ + m2 * 256, 256)
                    for q in range(2):
                        pb, pbk = pbank(0, 4)
                        S.mm([(pb[:], wv[:, k, q * 128:(q + 1) * 128], hT[:, k, :]) for k in range(DC)], [wk, "hT"], [pbk])
                        S.op("act", lambda e, pb=pb, m=m2 * 2 + q: e.activation(out=mq[:, m, :], in_=pb[:], func=AF.Copy),
                             [pbk], ["mq"])
                for hh in range(4):
                    r, rk = stat_rstd([(mq[:, 2 * hh + q, :], 128) for q in range(2)], 256, ["mq"])
                    for q in range(2):
                        S.op("dve", lambda e, hh=hh, q=q, r=r: e.scalar_tensor_tensor(
                            mqn[:, 2 * hh + q, :], mq[:, 2 * hh + q, :], gmq[:, q:q + 1], r[:], ALU.mult, ALU.mult),
                            ["mq", rk], ["mqn"])
                    for kt in range(2):
                        pS, pSk = pbank(0, 4)
                        S.mm([(pS[:], mkT[:, 2 * hh + q, kt * 128:(kt + 1) * 128], mqn[:, 2 * hh + q, :]) for q in range(2)],
                             ["mkT", "mqn"], [pSk])
                        S.op("act", lambda e, pS=pS, kt=kt: e.activation(out=PM[kt][:], in_=pS[:], func=AF.Exp, scale=MEM_SCALE),
                             [pSk], [f"PM{kt}"])
                    pd, pdk = ps[7], "ps7"
                    S.mm([(pd[:], ones_bf[:], PM[kt][:]) for kt in range(2)], ["PM0", "PM1"], [pdk])
                    S.op("dve", lambda e: e.reciprocal(rden[:], pd[:]), [pdk], ["rdenm"])
                    for e_ in range(2):
                        po, pok = ps[6], "ps6"
                        S.mm([(po[:], mv[:, kt, hh * 256 + e_ * 128:hh * 256 + (e_ + 1) * 128], PM[kt][:]) for kt in range(2)],
                             ["PM0", "PM1", "mv"], [pok])
                        S.op("dve", lambda e, hh=hh, e_=e_: e.tensor_tensor(om[:, 2 * hh + e_, :], po[:], rden[:], ALU.mult),
                             [pok, "rdenm"], ["om"])
                branch_out(2, w_meo, 8, om, "om")
                S.fence()

            for m in range(DC):
                S.op("act", lambda e, m=m: e.activation(out=mT[:, m, :], in_=merged[:, m, :], func=AF.Copy), ["merged"], ["mT"])
            S.fence()
        with ExitStack() as es:
            sbt = lambda name, shape, dt: es.enter_context(nc.sbuf_tensor(f"{name}_c{g}", shape, dt))
            x1T = sbt("x1T", [128, DC, G], F32)
            hff = sbt("hff", [128, FC, G], BF16)
            otok = sbt("otok", [128, 2, D], F32)
            S.dma("sp", x1T[:], x1d[:, :, g * G:(g + 1) * G], [], ["x1T"], "ld0")
            for m2 in range(DC // 2):
                wv, wk = wload(w_out, 0, DC, m2 * 256, 256)
                for q in range(2):
                    m = m2 * 2 + q
                    pb, pbk = pbank(0, 4)
                    S.mm([(pb[:], wv[:, k, q * 128:(q + 1) * 128], mT[:, k, :]) for k in range(DC)], [wk, "mT"], [pbk])
                    S.op("dve", lambda e, pb=pb, m=m: e.tensor_tensor(x1T[:, m, :], x1T[:, m, :], pb[:], ALU.add),
                         [pbk, "x1T"], ["x1T"])
            rmsnorm_fm(x1T, "x1T", DC, g2, 2048, hT, "hT")
            ffn(x1T, "x1T", hT, "hT", hff, w2gu, w2d)
            for j in range(4):
                for c4 in range(4):
                    pb, pbk = pbank(0, 4)
                    S.tr([(pb[:, q * 128:(q + 1) * 128], x1T[:, c4 * 4 + q, j * 128:(j + 1) * 128]) for q in range(4)],
                         ident[:], ["x1T"], [pbk])
                    S.op("act" if c4 % 2 else "dve",
                         (lambda e, pb=pb, j=j, c4=c4: e.activation(out=otok[:, j % 2, c4 * 512:(c4 + 1) * 512], in_=pb[:],
                                                                    func=AF.Copy)) if c4 % 2 else
                         (lambda e, pb=pb, j=j, c4=c4: e.tensor_copy(otok[:, j % 2, c4 * 512:(c4 + 1) * 512], pb[:])),
                         [pbk], [f"otok{j % 2}"])
                S.dma("sp", out_d[4 * g + j], otok[:, j % 2, :], [f"otok{j % 2}"], [f"outd.{g}.{j}"], f"sto{j % 2}")
            S.fence()
    S.final()
    print('KERNEL STATS', S.cnt, 'waits', S.nwait, 'dma', {k: v[1] // 16 for k, v in S.dsem.items()}, flush=True)
    return nc


_CACHE = {}


def _fm(v, C):
    return np.ascontiguousarray(np.asarray(v, np.float32).reshape(C, 128).T)


def kernel(**inp):
    f32 = np.float32
    x = np.asarray(inp["x"], f32)[0]
    mem = np.ascontiguousarray(np.asarray(inp["mem"], f32)[0])
    pos = np.asarray(inp["positions"])[0].astype(np.int32)
    sq = lambda k: np.ascontiguousarray(np.asarray(inp[k], f32)[0])
    common = {
        "mem": mem,
        "g_ffn1": _fm(sq("g_ffn1"), DC), "g_mix": _fm(sq("g_mix"), DC), "g_ffn2": _fm(sq("g_ffn2"), DC),
        "g_mem": _fm(sq("g_mem"), DC), "b_gate": _fm(sq("b_gate"), 48),
        "conv_w": np.ascontiguousarray(sq("conv_w").T.reshape(CC, 128, 31).transpose(1, 0, 2)),
        "conv_b": _fm(sq("conv_b"), CC), "conv_ln_g": _fm(sq("conv_ln_g"), CC), "conv_ln_b": _fm(sq("conv_ln_b"), CC),
        "g_q_a": _fm(sq("g_q_a"), 4), "g_kv_a": _fm(sq("g_kv_a"), 4),
        "gqn": sq("g_qnorm")[0:128].reshape(128, 1).copy(), "gqr": sq("g_qnorm")[128:192].reshape(64, 1).copy(),
        "gkn": sq("g_knorm")[0:128].reshape(128, 1).copy(), "gkr": sq("g_knorm")[128:192].reshape(64, 1).copy(),
        "g_mqnorm": _fm(sq("g_mqnorm"), 2), "g_mknorm": _fm(sq("g_mknorm"), 2),
        "ident": np.eye(128, dtype=f32),
        "w_ffn1_gu": sq("w_ffn1_gu"), "w_ffn1_down": sq("w_ffn1_down"), "w_in": sq("w_in"),
        "w_conv_out": sq("w_conv_out"), "w_uq": sq("w_uq"), "w_ukv": sq("w_ukv"), "w_mla_out": sq("w_mla_out"),
        "w_mem_kv": sq("w_mem_kv"), "w_mem_out": sq("w_mem_out"), "w_out": sq("w_out"),
        "w_ffn2_gu": sq("w_ffn2_gu"), "w_ffn2_down": sq("w_ffn2_down"),
    }
    rot = np.zeros((64, 64), f32)
    for i in range(32):
        rot[i + 32, i] = 1.0
        rot[i, i + 32] = -1.0
    common["rot"] = rot
    invf = (10000.0 ** (-np.arange(0, 64, 2, dtype=f32) / f32(64))).astype(f32)
    common["invf"] = np.concatenate([invf, invf]).reshape(64, 1).astype(f32)
    tri = (np.arange(128)[:, None] <= np.arange(128)[None, :]).astype(f32)
    in_maps = []
    xg = x.reshape(NT, NCORE, 128, D)
    pg = pos.reshape(NT, NCORE, 128)
    for c in range(NCORE):
        m = dict(common)
        order = [(c + s_) % NCORE for s_ in range(NCORE)]
        m["x"] = np.ascontiguousarray(xg[:, order]).reshape(NTA, 128, D)
        pl = np.ascontiguousarray(pg[:, order]).reshape(1, TA)
        m["pos"] = np.ascontiguousarray(np.broadcast_to(pl, (64, TA))).astype(np.int32)
        mk = np.zeros((128, 8, 128), f32)
        mk[:, 0, :] = tri
        if c >= 1:
            mk[:, 8 - c:, :] = 1.0
        m["mask"] = mk
        sel = np.zeros((128, 8), f32)
        sel[:, 0] = 1.0 if c >= 1 else 0.0
        sel[:, 1] = 1.0 if c == 0 else 0.0
        m["sel"] = sel
        in_maps.append(m)
    if "nc" not in _CACHE:
        _CACHE["nc"] = build()
    res = run_bass_kernel_spmd(_CACHE["nc"], in_maps, core_ids=list(range(NCORE)))
    out = np.empty((NT, NCORE, 128, D), f32)
    for c in range(NCORE):
        out[:, c] = np.asarray(res.results[c]["out"], f32).reshape(NT, 128, D)
    return out.reshape(1, NT * NCORE * 128, D)
```

```python
import numpy as np
import ml_dtypes
from contextlib import ExitStack
import concourse.bass as bass
import concourse.mybir as mybir
from concourse.bass_utils import run_bass_kernel_spmd

F32 = mybir.dt.float32
BF16 = mybir.dt.bfloat16
I32 = mybir.dt.int32
AF = mybir.ActivationFunctionType
ALU = mybir.AluOpType

NCORE = 8
D = 2048
DC = 16
DFF = 5632
FC = 44
T = 2048
G = 512
NG = 4
NT = 16
NTA = 128
TA = NTA * 128
NGA = TA // 512
CCH = 1536
CC = 12
H = 12
EPS = 1e-6
O_CV, O_CG, O_CQ, O_CKV, O_KPE, O_MQ, O_GATE = 0, 1536, 3072, 3584, 4096, 4160, 5184
MLA_SCALE = 192 ** -0.5
MEM_SCALE = 256 ** -0.5
TWO_PI = 2.0 * np.pi
STAGE = "full"


class Sched:
    def __init__(s, nc):
        s.nc = nc
        s.E = dict(pe=nc.tensor, act=nc.scalar, dve=nc.vector, pool=nc.gpsimd, sp=nc.sync)
        s.csem = {e: nc.alloc_semaphore("c_" + e) for e in ("pe", "act", "dve")}
        s.cnt = {e: 0 for e in s.csem}
        s.dsem = {}
        s.waited = {e: {} for e in s.E}
        s.res = {}
        s.nwait = 0

    def _wait(s, eng, evs):
        best = {}
        for ev in evs:
            if ev is None:
                continue
            sem, val = ev
            k = id(sem)
            if k not in best or best[k][1] < val:
                best[k] = ev
        for k, (sem, val) in best.items():
            if eng == "pe" and sem is s.csem["pe"]:
                continue
            if s.waited[eng].get(k, 0) >= val:
                continue
            s.E[eng].wait_ge(sem, val)
            s.nwait += 1
            s.waited[eng][k] = val

    def _deps(s, reads, writes):
        evs = []
        for k in reads:
            r = s.res.get(k)
            if r:
                evs.append(r[0])
                if k.startswith("ps"):
                    evs.extend(r[1].values())
        for k in writes:
            r = s.res.get(k)
            if r:
                evs.append(r[0])
                evs.extend(r[1].values())
        return evs

    def _commit(s, ev, reads, writes):
        kk = id(ev[0])
        for k in reads:
            r = s.res.setdefault(k, [None, {}])
            if kk not in r[1] or r[1][kk][1] < ev[1]:
                r[1][kk] = ev
        for k in writes:
            s.res[k] = [ev, {}]

    def op(s, eng, fn, reads=(), writes=()):
        s._wait(eng, s._deps(reads, writes))
        ins = fn(s.E[eng])
        s.cnt[eng] += 1
        ins.then_inc(s.csem[eng], 1)
        s._commit((s.csem[eng], s.cnt[eng]), reads, writes)

    def mm(s, mms, reads, writes, start=True, stop=True):
        s._wait("pe", s._deps(reads, writes))
        pe = s.E["pe"]
        n = len(mms)
        ins = None
        for i, (o, l, r) in enumerate(mms):
            ins = pe.matmul(o, l, r, start=(start and i == 0), stop=(stop and i == n - 1))
        s.cnt["pe"] += 1
        ins.then_inc(s.csem["pe"], 1)
        s._commit((s.csem["pe"], s.cnt["pe"]), reads, writes)

    def tr(s, outs_ins, ident, reads, writes):
        s._wait("pe", s._deps(reads, writes))
        pe = s.E["pe"]
        ins = None
        for o, i in outs_ins:
            ins = pe.transpose(o, i, ident)
        s.cnt["pe"] += 1
        ins.then_inc(s.csem["pe"], 1)
        s._commit((s.csem["pe"], s.cnt["pe"]), reads, writes)

    def dma(s, q, out, in_, reads, writes, slot, **kw):
        s._wait(q, s._deps(reads, writes))
        if slot not in s.dsem:
            s.dsem[slot] = [s.nc.alloc_semaphore("d_" + slot), 0]
        d = s.dsem[slot]
        s.E[q].dma_start(out=out, in_=in_, **kw).then_inc(d[0], 16)
        d[1] += 16
        s._commit((d[0], d[1]), reads, writes)

    def all_events(s):
        evs = [(s.csem[e], s.cnt[e]) for e in s.csem if s.cnt[e] > 0]
        evs += [(d[0], d[1]) for d in s.dsem.values() if d[1] > 0]
        return evs

    def fence(s, engs=("pe", "act", "dve", "sp")):
        evs = s.all_events()
        for e in engs:
            s._wait(e, evs)

    def final(s):
        evs = s.all_events()
        for e in ("sp", "pool", "act", "dve", "pe"):
            s._wait(e, evs)


def build():
    nc = bass.Bass("TRN2", target_bir_lowering=False)
    S = Sched(nc)

    def din(name, shape, dt=F32):
        return nc.dram_tensor(name, list(shape), dt, kind="ExternalInput").ap()

    x_d = din("x", [NTA, 128, D])
    mem_d = din("mem", [256, D])
    pos_d = din("pos", [64, TA], I32)
    g1_d = din("g_ffn1", [128, DC])
    gm_d = din("g_mix", [128, DC])
    g2_d = din("g_ffn2", [128, DC])
    gmem_d = din("g_mem", [128, DC])
    bg_d = din("b_gate", [128, 48])
    cw_d = din("conv_w", [128, CC, 31])
    cb_d = din("conv_b", [128, CC])
    lng_d = din("conv_ln_g", [128, CC])
    lnb_d = din("conv_ln_b", [128, CC])
    gqa_d = din("g_q_a", [128, 4])
    gkva_d = din("g_kv_a", [128, 4])
    gqn_d = din("gqn", [128, 1])
    gqr_d = din("gqr", [64, 1])
    gkn_d = din("gkn", [128, 1])
    gkr_d = din("gkr", [64, 1])
    gmq_d = din("g_mqnorm", [128, 2])
    gmk_d = din("g_mknorm", [128, 2])
    ident_d = din("ident", [128, 128])
    rot_d = din("rot", [64, 64])
    mask_d = din("mask", [128, 8, 128])
    sel_d = din("sel", [128, 8])
    invf_d = din("invf", [64, 1])
    w1gu = din("w_ffn1_gu", [D, 2 * DFF])
    w1d = din("w_ffn1_down", [DFF, D])
    w_in = din("w_in", [D, 11328])
    w_co = din("w_conv_out", [CCH, D])
    w_uq = din("w_uq", [512, H * 192])
    w_ukv = din("w_ukv", [512, H * 256])
    w_mo = din("w_mla_out", [H * 128, D])
    w_mkv = din("w_mem_kv", [D, 2048])
    w_meo = din("w_mem_out", [1024, D])
    w_out = din("w_out", [D, D])
    w2gu = din("w_ffn2_gu", [D, 2 * DFF])
    w2d = din("w_ffn2_down", [DFF, D])
    out_d = nc.dram_tensor("out", [NT, 128, D], F32, kind="ExternalOutput").ap()

    x1d = nc.dram_tensor("x1d", [128, DC, T], F32).ap()
    kloc = nc.dram_tensor("kloc", [H * 192, TA], BF16).ap()
    vloc = nc.dram_tensor("vloc", [H * TA, 128], BF16).ap()
    tloc = nc.dram_tensor("tloc", [CCH, 512], F32).ap()
    hd = nc.dram_tensor("hd", [CCH, 512], F32).ap()
    wbf_gu = nc.dram_tensor("wbf_gu", [2 * (FC // 2), 128, DC * 256], BF16).ap()
    wbf_d = nc.dram_tensor("wbf_d", [DC, 128, FC * 128], BF16).ap()

    sb = nc.alloc_sbuf_tensor
    ident = sb("ident_s", [128, 128], F32)
    rot = sb("rot_s", [64, 64], F32)
    mask = sb("mask_s", [128, 8, 128], F32)
    sel = sb("sel_s", [128, 8], F32)
    invf = sb("invf_s", [64, 1], F32)
    g1 = sb("g1_s", [128, DC], F32)
    gm = sb("gm_s", [128, DC], F32)
    g2 = sb("g2_s", [128, DC], F32)
    gmem = sb("gmem_s", [128, DC], F32)
    bg = sb("bg_s", [128, 48], F32)
    cw = sb("cw_s", [128, CC, 31], F32)
    cb = sb("cb_s", [128, CC], F32)
    lng = sb("lng_s", [128, CC], F32)
    lnb = sb("lnb_s", [128, CC], F32)
    gqa = sb("gqa_s", [128, 4], F32)
    gkva = sb("gkva_s", [128, 4], F32)
    gqn = sb("gqn_s", [128, 1], F32)
    gqr = sb("gqr_s", [64, 1], F32)
    gkn = sb("gkn_s", [128, 1], F32)
    gkr = sb("gkr_s", [64, 1], F32)
    gmq = sb("gmq_s", [128, 2], F32)
    gmk = sb("gmk_s", [128, 2], F32)
    ones_bf = sb("ones_bf", [128, 128], BF16)
    epst = sb("epst", [128, 1], F32)
    inv = {n: sb(f"inv{n}", [128, 128], F32) for n in (2048, 512, 192, 256, 1536)}
    mkT = sb("mkT", [128, 8, 256], BF16)
    mv = sb("mv", [128, 2, 1024], BF16)
    csg = sb("csg", [64, 2, G], F32)
    cs_d = nc.dram_tensor("cs_d", [64, 2, TA], F32).ap()
    rstd = [sb(f"rstd{i}", [128, G], F32) for i in range(2)]
    wring = [sb(f"wring{i}", [128, 5632], BF16) for i in range(3)]
    ps = [nc.alloc_psum_tensor(f"ps{i}", [128, G], F32) for i in range(8)]

    consts = [(ident, ident_d), (rot, rot_d), (mask, mask_d), (sel, sel_d), (invf, invf_d), (g1, g1_d),
              (gm, gm_d), (g2, g2_d), (gmem, gmem_d), (bg, bg_d), (cw, cw_d), (cb, cb_d), (lng, lng_d),
              (lnb, lnb_d), (gqa, gqa_d), (gkva, gkva_d), (gqn, gqn_d), (gqr, gqr_d), (gkn, gkn_d),
              (gkr, gkr_d), (gmq, gmq_d), (gmk, gmk_d)]
    for ci, (t, d_) in enumerate(consts):
        S.dma("sp", t[:], d_, [], [f"const{ci}"], "const")
    S.op("dve", lambda e: e.memset(ones_bf[:], 1.0), [], ["constA"])
    S.op("dve", lambda e: e.memset(epst[:], EPS), [], ["constB"])
    for n, t in inv.items():
        S.op("dve", lambda e, t=t, n=n: e.memset(t[:], 1.0 / n), [], [f"constI{n}"])
    S.fence(("pe", "act", "dve", "sp", "pool"))
    CONST = []

    wstate = {"i": 0, "n": 3}

    def wload(w, r0, KC, c0, ncols):
        i = wstate["i"] % wstate["n"]
        wstate["i"] += 1
        view = wring[i][:, 0:KC * ncols].rearrange("p (k n) -> p k n", k=KC)
        src = w[r0:r0 + KC * 128, c0:c0 + ncols].rearrange("(k p) n -> p k n", p=128)
        S.dma("pool", view, src, [], [f"w{i}"], f"w{i}")
        return view, f"w{i}"

    def wload_bf(src, KC, ncols, rkey):
        i = wstate["i"] % wstate["n"]
        wstate["i"] += 1
        S.dma("pool", wring[i][:, 0:KC * ncols], src, [rkey], [f"w{i}"], f"w{i}")
        return wring[i][:, 0:KC * ncols].rearrange("p (k n) -> p k n", k=KC), f"w{i}"

    pstate = {}

    def pbank(lo=0, n=4):
        c = pstate.get((lo, n), 0)
        pstate[(lo, n)] = c + 1
        i = lo + c % n
        return ps[i], f"ps{i}"

    rs = {"i": 0}

    def stat_rstd(parts, n, extra_reads, eps=EPS):
        i = rs["i"] % 2
        rs["i"] += 1
        N = parts[0][0].shape[-1]
        sbank, skey = ps[4 + i], f"ps{4 + i}"
        for j, (ap, P) in enumerate(parts):
            sq, sqk = sqbuf[j % 2], f"sq{j % 2}"
            S.op("act", lambda e, ap=ap, P=P, sq=sq: e.activation(out=sq[0:P, 0:N], in_=ap, func=AF.Square),
                 extra_reads, [sqk])
            S.mm([(sbank[:, 0:N], ones_bf[0:P, :], sq[0:P, 0:N])], [sqk], [skey],
                 start=(j == 0), stop=(j == len(parts) - 1))
        r, rk = rstd[i], f"rstd{i}"
        S.op("act", lambda e: e.activation(out=r[:, 0:N], in_=sbank[:, 0:N], func=AF.Sqrt, bias=epst[:, 0:1],
                                           scale=1.0 / n), [skey], [rk])
        S.op("dve", lambda e: e.reciprocal(r[:, 0:N], r[:, 0:N]), [rk], [rk])
        return r, rk

    sqbuf = [sb(f"sqb{i}", [128, G], BF16) for i in range(2)]

    def rmsnorm_fm(src, skey, C, gain, n, out, okey, N=G):
        r, rk = stat_rstd([(src[:, c, :], 128) for c in range(C)], n, [skey])
        for c in range(C):
            S.op("dve", lambda e, c=c: e.scalar_tensor_tensor(out[:, c, :], src[:, c, :], gain[:, c:c + 1],
                                                             r[:, 0:N], ALU.mult, ALU.mult),
                 [skey, rk], [okey])

    def ffn(xT, xkey, xn, xnkey, hff, wgu, wdn, pre=None):
        for m2 in range(FC // 2):
            if pre is None:
                wg, wgk = wload(wgu, 0, DC, m2 * 256, 256)
                wu, wuk = wload(wgu, 0, DC, DFF + m2 * 256, 256)
            else:
                wg, wgk = wload_bf(pre[0][2 * m2], DC, 256, f"wbfg{2 * m2}")
                wu, wuk = wload_bf(pre[0][2 * m2 + 1], DC, 256, f"wbfg{2 * m2 + 1}")
            for mm_ in range(2):
                m = m2 * 2 + mm_
                pg, pgk = pbank(0, 4)
                pu, puk = pbank(0, 4)
                S.mm([(pg[:], wg[:, k, mm_ * 128:(mm_ + 1) * 128], xn[:, k, :]) for k in range(DC)],
                     [wgk, xnkey], [pgk])
                S.mm([(pu[:], wu[:, k, mm_ * 128:(mm_ + 1) * 128], xn[:, k, :]) for k in range(DC)],
                     [wuk, xnkey], [puk])
                sg, sgk = sgbuf[m % 2], f"sg{m % 2}"
                S.op("act", lambda e, pg=pg, sg=sg: e.activation(out=sg[:], in_=pg[:], func=AF.Silu), [pgk], [sgk])
                S.op("dve", lambda e, pu=pu, sg=sg, m=m: e.tensor_tensor(hff[:, m, :], sg[:], pu[:], ALU.mult),
                     [sgk, puk], ["hff"])
        for mo in range(DC):
            if pre is None:
                wd, wdk = wload(wdn, 0, FC, mo * 128, 128)
            else:
                wd, wdk = wload_bf(pre[1][mo], FC, 128, f"wbfd{mo}")
            pa, pak = pbank(6, 2)
            S.mm([(pa[:], wd[:, k, :], hff[:, k, :]) for k in range(FC)], [wdk, "hff"], [pak])
            S.op("dve", lambda e, pa=pa, mo=mo: e.scalar_tensor_tensor(xT[:, mo, :], pa[:], 0.5, xT[:, mo, :],
                                                                      ALU.mult, ALU.add),
                 [pak, xkey], [xkey])

    sgbuf = [sb(f"sgb{i}", [128, G], F32) for i in range(2)]

    with ExitStack() as es:
        posi = es.enter_context(nc.sbuf_tensor("posi", [64, T], I32))
        ang = es.enter_context(nc.sbuf_tensor("ang", [64, T], F32))
        ang2 = es.enter_context(nc.sbuf_tensor("ang2", [64, T], F32))
        sinn = es.enter_context(nc.sbuf_tensor("sinn", [64, T], F32))
        cosn = es.enter_context(nc.sbuf_tensor("cosn", [64, T], F32))
        for ch in range(TA // T):
            S.dma("sp", posi[:], pos_d[:, ch * T:(ch + 1) * T], [], ["posi"], "const")
            S.op("dve", lambda e: e.tensor_copy(ang[:], posi[:]), ["posi"], ["ang"])
            S.op("dve", lambda e: e.tensor_scalar(ang[:], ang[:], invf[:, 0:1], None, ALU.mult), ["ang"], ["ang"])
            for tgt, tk, shift in ((sinn, "sinn", 0.0), (cosn, "cosn", np.pi / 2)):
                S.op("dve", lambda e, shift=shift: e.tensor_scalar(ang2[:], ang[:], shift, 1.0 / TWO_PI, ALU.add, ALU.mult),
                     ["ang"], ["ang2"])
                S.op("dve", lambda e: e.tensor_copy(posi[:], ang2[:]), ["ang2"], ["posi"])
                S.op("dve", lambda e: e.tensor_copy(ang2[:], posi[:]), ["posi"], ["ang2"])
                S.op("dve", lambda e, tgt=tgt: e.scalar_tensor_tensor(tgt[:], ang2[:], -TWO_PI, ang[:], ALU.mult, ALU.add),
                     ["ang2", "ang"], [tk])
                if shift != 0.0:
                    S.op("dve", lambda e, tgt=tgt, shift=shift: e.tensor_scalar(tgt[:], tgt[:], shift, None, ALU.add),
                         [tk], [tk])
                S.op("act", lambda e, tgt=tgt: e.activation(out=tgt[:], in_=tgt[:], func=AF.Sin, scale=0.999999),
                     [tk], [tk])
            S.dma("sp", cs_d[:, 0, ch * T:(ch + 1) * T], sinn[:], ["sinn"], [f"cs_d{ch}"], "st0")
            S.dma("sp", cs_d[:, 1, ch * T:(ch + 1) * T], cosn[:], ["cosn"], [f"cs_d{ch}"], "st0")
        S.fence()

    def load_cs(g):
        S.dma("sp", csg[:], cs_d[:, :, g * G:(g + 1) * G], [], ["csg"], "ld1")

    def load_cs_own(g):
        for i in range(2):
            S.dma("sp", csg[:, i, :].rearrange("p (j t) -> p j t", j=4),
                  cs_d[:, i, :].rearrange("p (j s t) -> p j s t", s=8, t=128)[:, 4 * g:4 * g + 4, 0, :],
                  [], ["csg"], "ld1")

    def rope_apply(src, skey, N, outf, okey, tmp1, tmp2):
        pr, prk = pbank(0, 4)
        S.mm([(pr[0:64, 0:N], rot[:], src)], [skey], [prk])
        S.op("dve", lambda e: e.tensor_tensor(tmp1, pr[0:64, 0:N], csg[:, 0, 0:N], ALU.mult),
             [prk, "csg"], [okey + "t1"])
        S.op("dve", lambda e: e.tensor_tensor(tmp2, src, csg[:, 1, 0:N], ALU.mult),
             [skey, "csg"], [okey + "t2"])
        S.op("dve", lambda e: e.tensor_tensor(outf, tmp2, tmp1, ALU.subtract), [okey + "t1", okey + "t2"], [okey])

    with ExitStack() as es:
        memtok = es.enter_context(nc.sbuf_tensor("memtok", [128, 2, D], F32))
        memT = es.enter_context(nc.sbuf_tensor("memT", [128, DC, 256], F32))
        memn = es.enter_context(nc.sbuf_tensor("memn", [128, DC, 256], BF16))
        mkf = es.enter_context(nc.sbuf_tensor("mkf", [128, 8, 256], F32))
        S.dma("sp", memtok[:], mem_d.rearrange("(j p) d -> p j d", p=128), [], ["memtok"], "ld0")
        for j in range(2):
            for c4 in range(4):
                pb, pbk = pbank(0, 4)
                S.tr([(pb[:, q * 128:(q + 1) * 128], memtok[:, j, (c4 * 4 + q) * 128:(c4 * 4 + q + 1) * 128])
                      for q in range(4)], ident[:], ["memtok"], [pbk])
                S.op("act", lambda e, pb=pb, j=j, c4=c4: e.activation(
                    out=memT[:, c4 * 4:c4 * 4 + 4, j * 128:(j + 1) * 128],
                    in_=pb[:].rearrange("p (q t) -> p q t", q=4), func=AF.Copy), [pbk], ["memT"])
        rmsnorm_fm(memT, "memT", DC, gmem, 2048, memn, "memn", N=256)
        for m2 in range(4):
            wv, wk = wload(w_mkv, 0, DC, m2 * 256, 256)
            for q in range(2):
                m = m2 * 2 + q
                pb, pbk = pbank(0, 4)
                S.mm([(pb[:, 0:256], wv[:, k, q * 128:(q + 1) * 128], memn[:, k, :]) for k in range(DC)],
                     [wk, "memn"], [pbk])
                S.op("act", lambda e, pb=pb, m=m: e.activation(out=mkf[:, m, :], in_=pb[:, 0:256], func=AF.Copy),
                     [pbk], ["mkf"])
        for hh in range(4):
            r, rk = stat_rstd([(mkf[:, 2 * hh + q, :], 128) for q in range(2)], 256, ["mkf"])
            for q in range(2):
                S.op("dve", lambda e, hh=hh, q=q, r=r: e.scalar_tensor_tensor(
                    mkT[:, 2 * hh + q, :], mkf[:, 2 * hh + q, :], gmk[:, q:q + 1], r[:, 0:256], ALU.mult, ALU.mult),
                    ["mkf", rk], ["mkT"])
        for m2 in range(4):
            wv, wk = wload(w_mkv, 0, DC, 1024 + m2 * 256, 256)
            for j in range(2):
                pb, pbk = pbank(0, 4)
                S.mm([(pb[:, 0:256], memn[:, k, j * 128:(j + 1) * 128], wv[:, k, :]) for k in range(DC)],
                     [wk, "memn"], [pbk])
                S.op("act", lambda e, pb=pb, j=j, m2=m2: e.activation(
                    out=mv[:, j, m2 * 256:(m2 + 1) * 256], in_=pb[:, 0:256], func=AF.Copy), [pbk], ["mv"])
        S.fence()

    esA = ExitStack()
    for i_ in range(3, 6):
        wring.append(esA.enter_context(nc.sbuf_tensor(f"wringA{i_}", [128, 5632], BF16)))
    wstate["n"] = 6
    for m2 in range(FC // 2):
        for gu in range(2):
            b_ = 2 * m2 + gu
            wv, wk = wload(w1gu, 0, DC, gu * DFF + m2 * 256, 256)
            S.dma("sp", wbf_gu[b_], wv.rearrange("p k n -> p (k n)"), [wk], [f"wbfg{b_}"], f"wst{b_ % 3}")
    for mo in range(DC):
        wv, wk = wload(w1d, 0, FC, mo * 128, 128)
        S.dma("sp", wbf_d[mo], wv.rearrange("p k n -> p (k n)"), [wk], [f"wbfd{mo}"], f"wst{mo % 3}")
    S.fence(("pe", "act", "dve", "sp", "pool"))
    if STAGE == "P":
        S.final()
        return nc
    for g in range(NGA if STAGE != "A1" else 2):
        jsup, own = g // 2, (g % 2 == 0)
        with ExitStack() as es:
            sbt = lambda name, shape, dt: es.enter_context(nc.sbuf_tensor(f"{name}_a{g}", shape, dt))
            xT = sbt("xT", [128, DC, G], F32)
            xn = sbt("xn", [128, DC, G], BF16)
            load_cs(g)
            with ExitStack() as es0:
                xtok = es0.enter_context(nc.sbuf_tensor(f"xtok_a{g}", [128, 2, D], F32))
                for j in range(4):
                    S.dma("sp", xtok[:, j % 2, :], x_d[4 * g + j], [], [f"xtok{j % 2}"], f"ld{j % 2}")
                    for c4 in range(4):
                        pb, pbk = pbank(0, 4)
                        S.tr([(pb[:, q * 128:(q + 1) * 128], xtok[:, j % 2, (c4 * 4 + q) * 128:(c4 * 4 + q + 1) * 128])
                              for q in range(4)], ident[:], [f"xtok{j % 2}"], [pbk])
                        if c4 % 2:
                            S.op("act", lambda e, pb=pb, j=j, c4=c4: e.activation(
                                out=xT[:, c4 * 4:c4 * 4 + 4, j * 128:(j + 1) * 128],
                                in_=pb[:].rearrange("p (q t) -> p q t", q=4), func=AF.Copy), [pbk], ["xT"])
                        else:
                            S.op("dve", lambda e, pb=pb, j=j, c4=c4: e.tensor_copy(
                                xT[:, c4 * 4:c4 * 4 + 4, j * 128:(j + 1) * 128],
                                pb[:].rearrange("p (q t) -> p q t", q=4)), [pbk], ["xT"])
                S.fence()
            rmsnorm_fm(xT, "xT", DC, g1, 2048, xn, "xn")
            with ExitStack() as es0:
                hff = es0.enter_context(nc.sbuf_tensor(f"hff_a{g}", [128, FC, G], BF16))
                ffn(xT, "xT", xn, "xn", hff, w1gu, w1d, pre=(wbf_gu, wbf_d))
                S.fence()
            if own:
                S.dma("sp", x1d[:, :, jsup * 128:(jsup + 1) * 128], xT[:, :, 0:128], ["xT"], [f"x1d{jsup}"], "st0")
            rmsnorm_fm(xT, "xT", DC, gm, 2048, xn, "xn")
            with ExitStack() as es2:
                sb2 = lambda name, shape, dt: es2.enter_context(nc.sbuf_tensor(f"{name}_a{g}", shape, dt))
                ckv = sb2("ckv", [128, 4, G], F32)
                ckvn = sb2("ckvn", [128, 4, G], BF16)
                kpe = sb2("kpe", [64, G], F32)
                kper = sb2("kper", [64, G], F32)
                t1 = sb2("t1", [64, G], F32)
                t2 = sb2("t2", [64, G], F32)
                knf = sb2("knf", [128, G], F32)
                kout = sb2("kout", [128, 2, G], BF16)
                krout = sb2("krout", [64, 2, G], BF16)
                vsb = sb2("vsb", [128, 4, H, 128], BF16)
                tail = sb2("tail", [128, CC, 128], F32)
                sgt = sb2("sgt", [128, 2, 128], F32)
                for m2 in range(2):
                    wv, wk = wload(w_in, 0, DC, O_CKV + m2 * 256, 256)
                    for q in range(2):
                        pb, pbk = pbank(0, 4)
                        S.mm([(pb[:], wv[:, k, q * 128:(q + 1) * 128], xn[:, k, :]) for k in range(DC)],
                             [wk, "xn"], [pbk])
                        S.op("act", lambda e, pb=pb, m=m2 * 2 + q: e.activation(out=ckv[:, m, :], in_=pb[:],
                                                                               func=AF.Copy), [pbk], ["ckv"])
                wv, wk = wload(w_in, 0, DC, O_KPE, 64)
                pb, pbk = pbank(0, 4)
                S.mm([(pb[0:64, :], wv[:, k, :], xn[:, k, :]) for k in range(DC)], [wk, "xn"], [pbk])
                kpesq = sb2("kpesq", [64, G], BF16)
                S.op("act", lambda e: e.activation(out=kpesq[:], in_=pb[0:64, :], func=AF.Square), [pbk], ["kpesq2"])
                S.op("dve", lambda e: e.tensor_scalar(kpe[:], pb[0:64, :], gkr[:, 0:1], None, ALU.mult),
                     [pbk], ["kpe"])
                rope_apply(kpe[:], "kpe", G, kper[:], "kper", t1[:], t2[:])
                rmsnorm_fm(ckv, "ckv", 4, gkva, 512, ckvn, "ckvn")
                wv, wk = None, None
                for h in range(H):
                    if h % 3 == 0:
                        wv, wk = wload(w_ukv, 0, 4, h * 256, 768)
                    hq = h % 3
                    pb, pbk = pbank(0, 4)
                    S.mm([(pb[:], wv[:, k, hq * 256:hq * 256 + 128], ckvn[:, k, :]) for k in range(4)],
                         [wk, "ckvn"], [pbk])
                    S.op("act", lambda e, pb=pb: e.activation(out=knf[:], in_=pb[:], func=AF.Copy), [pbk], ["knf"])
                    S.op("act", lambda e: e.activation(out=sqbuf[1][:], in_=knf[:], func=AF.Square), ["knf"], ["sq1"])
                    i = rs["i"] % 2
                    rs["i"] += 1
                    sbank, skey = ps[4 + i], f"ps{4 + i}"
                    S.mm([(sbank[:], ones_bf[0:64, :], kpesq[:]), (sbank[:], ones_bf[:], sqbuf[1][:])],
                         ["kpesq2", "sq1"], [skey])
                    r, rk = rstd[i], f"rstd{i}"
                    S.op("act", lambda e, r=r, sbank=sbank: e.activation(out=r[:], in_=sbank[:], func=AF.Sqrt,
                                                                         bias=epst[:, 0:1], scale=1.0 / 192), [skey], [rk])
                    S.op("dve", lambda e, r=r: e.reciprocal(r[:], r[:]), [rk], [rk])
                    S.op("dve", lambda e, r=r, h=h: e.scalar_tensor_tensor(kout[:, h % 2, :], knf[:], gkn[:, 0:1], r[:],
                                                                         ALU.mult, ALU.mult),
                         ["knf", rk], [f"kout{h % 2}"])
                    S.op("dve", lambda e, r=r, h=h: e.tensor_tensor(krout[:, h % 2, :], kper[:], r[0:64, :], ALU.mult),
                         ["kper", rk], [f"krout{h % 2}"])
                    S.dma("sp", kloc[h * 192:h * 192 + 128, g * G:(g + 1) * G], kout[:, h % 2, :],
                          [f"kout{h % 2}"], [f"kloc.{g}.{h}"], f"stk{h % 2}")
                    S.dma("sp", kloc[h * 192 + 128:h * 192 + 192, g * G:(g + 1) * G], krout[:, h % 2, :],
                          [f"krout{h % 2}"], [f"klocr.{g}.{h}"], f"stk{h % 2}")
                for h3 in range(4):
                    wv, wk = wload(w_ukv, 0, 4, h3 * 768, 768)
                    for j in range(4):
                        pb, pbk = pbank(0, 4)
                        S.mm([(pb[:, 0:384].rearrange("p (a d) -> p a d", a=3),
                               ckvn[:, k, j * 128:(j + 1) * 128],
                               wv[:, k, :].rearrange("p (a d) -> p a d", a=3)[:, :, 128:256]) for k in range(4)],
                             [wk, "ckvn"], [pbk])
                        S.op("act", lambda e, pb=pb, j=j, h3=h3: e.activation(
                            out=vsb[:, j, h3 * 3:h3 * 3 + 3, :],
                            in_=pb[:, 0:384].rearrange("p (a d) -> p a d", a=3), func=AF.Copy),
                            [pbk], [f"vsb{j}"])
                        if h3 == 3:
                            tok0 = (4 * g + j) * 128
                            S.dma("sp", vloc.rearrange("(h t) d -> t h d", h=H)[tok0:tok0 + 128, :, :],
                                  vsb[:, j, :, :], [f"vsb{j}"], [f"vloc.{g}.{j}"], f"stv{j % 2}")
                if not own:
                    for c2 in range(CC // 2):
                        wv, wk = wload(w_in, 0, DC, O_CV + c2 * 256, 256)
                        wg_, wgk = wload(w_in, 0, DC, O_CG + c2 * 256, 256)
                        for q in range(2):
                            cc = c2 * 2 + q
                            pv, pvk = pbank(0, 4)
                            pg, pgk = pbank(0, 4)
                            S.mm([(pv[:, 0:32], wv[:, k, q * 128:(q + 1) * 128], xn[:, k, 480:512]) for k in range(DC)],
                                 [wk, "xn"], [pvk])
                            S.mm([(pg[:, 0:32], wg_[:, k, q * 128:(q + 1) * 128], xn[:, k, 480:512]) for k in range(DC)],
                                 [wgk, "xn"], [pgk])
                            S.op("act", lambda e, pg=pg, cc=cc: e.activation(out=sgt[:, cc % 2, 0:32], in_=pg[:, 0:32],
                                                                             func=AF.Sigmoid), [pgk], [f"sgt{cc % 2}"])
                            S.op("dve", lambda e, pv=pv, cc=cc: e.tensor_tensor(tail[:, cc, 0:32], sgt[:, cc % 2, 0:32],
                                                                                pv[:, 0:32], ALU.mult),
                                 [pvk, f"sgt{cc % 2}"], ["tail"])
                    S.dma("sp", tloc.rearrange("(c p) n -> p c n", p=128)[:, :, jsup * 32:(jsup + 1) * 32], tail[:, :, 0:32],
                          ["tail"], [f"tloc.{g}"], "st1")
                S.fence()
            S.fence()

    if STAGE == "A":
        S.final()
        return nc
    S.fence(("pe", "act", "dve", "sp", "pool"))
    wstate["n"] = 3
    wstate["i"] = 0
    del wring[3:]
    esA.close()
    S.fence(("pe", "act", "dve", "sp", "pool"))
    with ExitStack() as es:
        hacc = es.enter_context(nc.sbuf_tensor("hacc", [128, CC, 512], F32))
        tb = es.enter_context(nc.sbuf_tensor("trk", [128, CC, 512], F32))
        S.dma("sp", tb[:], tloc.rearrange("(c p) n -> p c n", p=128), [], ["trk"], "ld0")
        S.op("dve", lambda e: e.tensor_scalar(hacc[:], tb[:], sel[:, 0:1], None, ALU.mult), ["trk"], ["hacc"])
        S.op("dve", lambda e: e.scalar_tensor_tensor(hacc[:, :, 32:512], tb[:, :, 0:480], sel[:, 1:2],
                                                     hacc[:, :, 32:512], ALU.mult, ALU.add), ["trk", "hacc"], ["hacc"])
        S.dma("sp", hd.rearrange("(c p) n -> p c n", p=128), hacc[:], ["hacc"], ["hd"], "st1")
        S.fence()
    if STAGE == "C":
        S.final()
        return nc
    hT = sb("hT", [128, DC, G], BF16)
    mT = sb("mT", [128, DC, G], BF16)
    for g in range(NG if STAGE == "full" else 1):
        J = g
        with ExitStack() as es:
            sbt = lambda name, shape, dt: es.enter_context(nc.sbuf_tensor(f"{name}_b{g}", shape, dt))
            load_cs_own(g)
            with ExitStack() as es1:
                x1T = es1.enter_context(nc.sbuf_tensor(f"x1Ta_b{g}", [128, DC, G], F32))
                S.dma("sp", x1T[:], x1d[:, :, g * G:(g + 1) * G], [], ["x1T"], "ld0")
                rmsnorm_fm(x1T, "x1T", DC, gm, 2048, hT, "hT")
                S.fence()
            merged = sbt("merged", [128, DC, G], F32)
            gsb = [sbt(f"gsb{i}", [128, G], F32) for i in range(2)]

            def branch_out(br, wy, KCy, act_in, akey):
                for m2 in range(DC // 2):
                    wv, wk = wload(wy, 0, KCy, m2 * 256, 256)
                    wg_, wgk = wload(w_in, 0, DC, O_GATE + br * D + m2 * 256, 256)
                    for q in range(2):
                        m = m2 * 2 + q
                        py, pyk = pbank(0, 4)
                        pg, pgk = pbank(0, 4)
                        S.mm([(py[:], wv[:, k, q * 128:(q + 1) * 128], act_in[:, k, :]) for k in range(KCy)],
                             [wk, akey], [pyk])
                        S.mm([(pg[:], wg_[:, k, q * 128:(q + 1) * 128], hT[:, k, :]) for k in range(DC)],
                             [wgk, "hT"], [pgk])
                        gb, gbk = gsb[m % 2], f"gsb{m % 2}"
                        S.op("act", lambda e, pg=pg, gb=gb, m=m: e.activation(
                            out=gb[:], in_=pg[:], func=AF.Sigmoid, bias=bg[:, br * DC + m:br * DC + m + 1]),
                            [pgk], [gbk])
                        if br == 0:
                            S.op("dve", lambda e, py=py, gb=gb, m=m: e.tensor_tensor(merged[:, m, :], gb[:], py[:],
                                                                                    ALU.mult), [pyk, gbk], ["merged"])
                        else:
                            S.op("dve", lambda e, py=py, gb=gb: e.tensor_tensor(gb[:], gb[:], py[:], ALU.mult),
                                 [pyk, gbk], [gbk])
                            S.op("dve", lambda e, gb=gb, m=m: e.tensor_tensor(merged[:, m, :], merged[:, m, :], gb[:],
                                                                             ALU.add), [gbk, "merged"], ["merged"])

            with ExitStack() as es2:
                sb2 = lambda name, shape, dt: es2.enter_context(nc.sbuf_tensor(f"{name}_b{g}", shape, dt))
                HC = CC // 2
                apad = sb2("apad", [128, HC, 4, 160], F32)
                cacc = sb2("cacc", [128, CC, G], F32)
                an = sb2("an", [128, CC, G], BF16)
                sgc = [sb2(f"sgc{i}", [128, G], F32) for i in range(2)]
                meansb = sb2("meansb", [128, G], F32)
                for half in range(2):
                    for cl in range(HC):
                        S.dma("sp", apad[:, cl, :, 0:32],
                              hd.rearrange("(c p) (j t) -> p c j t", p=128, t=32)[:, half * HC + cl, 4 * g:4 * g + 4, :],
                              ["hd"], [f"apadh{cl}"], "ld0")
                    for c2 in range(HC // 2):
                        wv, wk = wload(w_in, 0, DC, O_CV + half * HC * 128 + c2 * 256, 256)
                        wg_, wgk = wload(w_in, 0, DC, O_CG + half * HC * 128 + c2 * 256, 256)
                        for q in range(2):
                            cl = c2 * 2 + q
                            pv, pvk = pbank(0, 4)
                            pg, pgk = pbank(0, 4)
                            S.mm([(pv[:], wv[:, k, q * 128:(q + 1) * 128], hT[:, k, :]) for k in range(DC)], [wk, "hT"], [pvk])
                            S.mm([(pg[:], wg_[:, k, q * 128:(q + 1) * 128], hT[:, k, :]) for k in range(DC)], [wgk, "hT"], [pgk])
                            sg_, sgk = sgc[cl % 2], f"sgc{cl % 2}"
                            S.op("act", lambda e, pg=pg, sg_=sg_: e.activation(out=sg_[:], in_=pg[:], func=AF.Sigmoid),
                                 [pgk], [sgk])
                            S.op("dve", lambda e, pv=pv, sg_=sg_, cl=cl: e.tensor_tensor(
                                apad[:, cl, :, 32:160], sg_[:].rearrange("p (j t) -> p j t", j=4),
                                pv[:].rearrange("p (j t) -> p j t", j=4), ALU.mult), [pvk, sgk], [f"apad{cl}"])
                    for k in range(31):
                        for cl in range(HC):
                            cc = half * HC + cl
                            src = apad[:, cl, :, 2 + k:130 + k]
                            dst = cacc[:, cc, :].rearrange("p (j t) -> p j t", j=4)
                            if k == 0:
                                S.op("dve", lambda e, src=src, dst=dst, cc=cc: e.tensor_scalar(
                                    dst, src, cw[:, cc, 0:1], cb[:, cc:cc + 1], ALU.mult, ALU.add),
                                    [f"apad{cl}", f"apadh{cl}"], [f"cacc{cc}"])
                            else:
                                S.op("dve", lambda e, src=src, dst=dst, cc=cc, k=k: e.scalar_tensor_tensor(
                                    dst, src, cw[:, cc, k:k + 1], dst, ALU.mult, ALU.add),
                                    [f"apad{cl}", f"apadh{cl}", f"cacc{cc}"], [f"cacc{cc}"])
                pmean, pmk = ps[4], "ps4"
                pms, pmsk = ps[5], "ps5"
                S.mm([(pmean[:], inv[1536][:], cacc[:, cc, :]) for cc in range(CC)],
                     [f"cacc{cc}" for cc in range(CC)], [pmk])
                for cc in range(CC):
                    sq, sqk = sqbuf[cc % 2], f"sq{cc % 2}"
                    S.op("act", lambda e, sq=sq, cc=cc: e.activation(out=sq[:], in_=cacc[:, cc, :], func=AF.Square),
                         [f"cacc{cc}"], [sqk])
                    S.mm([(pms[:], ones_bf[:], sq[:])], [sqk], [pmsk], start=(cc == 0), stop=(cc == CC - 1))
                S.op("act", lambda e: e.activation(out=meansb[:], in_=pmean[:], func=AF.Copy), [pmk], ["meansb"])
                S.op("dve", lambda e: e.tensor_tensor(rstd[0][:], meansb[:], meansb[:], ALU.mult), ["meansb"], ["rstd0"])
                S.op("dve", lambda e: e.scalar_tensor_tensor(rstd[0][:], pms[:], 1.0 / 1536, rstd[0][:], ALU.mult, ALU.subtract),
                     [pmsk, "rstd0"], ["rstd0"])
                S.op("act", lambda e: e.activation(out=rstd[0][:], in_=rstd[0][:], func=AF.Sqrt, bias=epst[:, 0:1]),
                     ["rstd0"], ["rstd0"])
                S.op("dve", lambda e: e.reciprocal(rstd[0][:], rstd[0][:]), ["rstd0"], ["rstd0"])
                for cc in range(CC):
                    S.op("dve", lambda e, cc=cc: e.tensor_tensor(cacc[:, cc, :], cacc[:, cc, :], meansb[:], ALU.subtract),
                         [f"cacc{cc}", "meansb"], [f"cacc{cc}"])
                    S.op("dve", lambda e, cc=cc: e.tensor_tensor(cacc[:, cc, :], cacc[:, cc, :], rstd[0][:], ALU.mult),
                         [f"cacc{cc}", "rstd0"], [f"cacc{cc}"])
                    S.op("act", lambda e, cc=cc: e.activation(out=an[:, cc, :], in_=cacc[:, cc, :], func=AF.Silu,
                                                              bias=lnb[:, cc:cc + 1], scale=lng[:, cc:cc + 1]),
                         [f"cacc{cc}"], ["an"])
                branch_out(0, w_co, CC, an, "an")
                S.fence()

            with ExitStack() as es2:
                sb2 = lambda name, shape, dt: es2.enter_context(nc.sbuf_tensor(f"{name}_b{g}", shape, dt))
                QN = sb2("QN", [128, H, G], BF16)
                QR = sb2("QR", [64, H, G], BF16)
                oT = sb2("oT", [128, H, G], BF16)
                es3 = ExitStack()
                sb3 = lambda name, shape, dt: es3.enter_context(nc.sbuf_tensor(f"{name}_b{g}", shape, dt))
                cq = sb3("cq", [128, 4, G], F32)
                cqn = sb3("cqn", [128, 4, G], BF16)
                qnf = sb3("qnf", [128, G], F32)
                qrf = sb3("qrf", [64, G], F32)
                qrg = sb3("qrg", [64, G], F32)
                qrr = sb3("qrr", [64, G], F32)
                t1 = sb3("t1", [64, G], F32)
                t2 = sb3("t2", [64, G], F32)
                for m2 in range(2):
                    wv, wk = wload(w_in, 0, DC, O_CQ + m2 * 256, 256)
                    for q in range(2):
                        pb, pbk = pbank(0, 4)
                        S.mm([(pb[:], wv[:, k, q * 128:(q + 1) * 128], hT[:, k, :]) for k in range(DC)], [wk, "hT"], [pbk])
                        S.op("act", lambda e, pb=pb, m=m2 * 2 + q: e.activation(out=cq[:, m, :], in_=pb[:], func=AF.Copy),
                             [pbk], ["cq"])
                rmsnorm_fm(cq, "cq", 4, gqa, 512, cqn, "cqn")
                wv, wk = None, None
                for h in range(H):
                    if h % 4 == 0:
                        wv, wk = wload(w_uq, 0, 4, h * 192, 768)
                    hq = h % 4
                    pn, pnk = pbank(0, 4)
                    pr_, prk_ = pbank(0, 4)
                    S.mm([(pn[:], wv[:, k, hq * 192:hq * 192 + 128], cqn[:, k, :]) for k in range(4)], [wk, "cqn"], [pnk])
                    S.mm([(pr_[0:64, :], wv[:, k, hq * 192 + 128:hq * 192 + 192], cqn[:, k, :]) for k in range(4)],
                         [wk, "cqn"], [prk_])
                    S.op("act", lambda e, pn=pn: e.activation(out=qnf[:], in_=pn[:], func=AF.Copy), [pnk], ["qnf"])
                    S.op("act", lambda e, pr_=pr_: e.activation(out=qrf[:], in_=pr_[0:64, :], func=AF.Copy), [prk_], ["qrf"])
                    r, rk = stat_rstd([(qnf[:], 128), (qrf[:], 64)], 192, ["qnf", "qrf"])
                    S.op("dve", lambda e, r=r, h=h: e.scalar_tensor_tensor(QN[:, h, :], qnf[:], gqn[:, 0:1], r[:],
                                                                         ALU.mult, ALU.mult), ["qnf", rk], ["QN"])
                    S.op("dve", lambda e: e.tensor_scalar(qrg[:], qrf[:], gqr[:, 0:1], None, ALU.mult),
                         ["qrf"], ["qrg"])
                    rope_apply(qrg[:], "qrg", G, qrr[:], "qrr", t1[:], t2[:])
                    S.op("dve", lambda e, r=r, h=h: e.tensor_tensor(QR[:, h, :], qrr[:], r[0:64, :], ALU.mult),
                         ["qrr", rk], ["QR"])
                S.fence()
                es3.close()
                es3 = ExitStack()
                rden = sb3("rden", [128, G], F32)
                kn = [sb3(f"kn{i}", [128, 8, 128], BF16) for i in range(3)]
                kr = [sb3(f"kr{i}", [64, 8, 128], BF16) for i in range(3)]
                vv = [sb3(f"vv{i}", [128, 8, 128], BF16) for i in range(3)]
                PT = [sb3(f"PT{i}", [128, G], BF16) for i in range(3)]
                nsc = 4 * J + 4
                kvi = 0
                for h in range(H):
                    po, pok = ps[6], "ps6"
                    pd, pdk = ps[7], "ps7"
                    first = True
                    for sc in range(nsc):
                        i = kvi % 3
                        kvi += 1
                        kkey = f"kv{i}"
                        S.dma("sp", kn[i][:].rearrange("p r t -> p (r t)"),
                              kloc[h * 192:h * 192 + 128, sc * 1024:(sc + 1) * 1024], [], [kkey], f"kv{i}")
                        S.dma("sp", kr[i][:].rearrange("p r t -> p (r t)"),
                              kloc[h * 192 + 128:h * 192 + 192, sc * 1024:(sc + 1) * 1024], [], [kkey], f"kv{i}")
                        S.dma("sp", vv[i][:], vloc.rearrange("(h n s t) d -> t h n s d", h=H, s=8, t=128)[:, h, sc, :, :],
                              [], [kkey], f"kv{i}")
                        q0 = max(0, sc - 4 * J) * 128
                        N = G - q0
                        for kt in range(8):
                            pS, pSk = pbank(0, 4)
                            S.mm([(pS[:, 0:N], kn[i][:, kt, :], QN[:, h, q0:G]), (pS[:, 0:N], kr[i][:, kt, :], QR[:, h, q0:G])],
                                 [kkey, "QN", "QR"], [pSk])
                            pi_ = (kvi * 8 + kt) % 3
                            P_, Pk = PT[pi_], f"PT{pi_}"
                            S.op("act", lambda e, pS=pS, P_=P_, N=N: e.activation(out=P_[:, 0:N], in_=pS[:, 0:N], func=AF.Exp,
                                                                                 scale=MLA_SCALE), [pSk], [Pk])
                            if sc >= 4 * J:
                                S.op("dve", lambda e, P_=P_, kt=kt: e.tensor_tensor(P_[:, 0:128], P_[:, 0:128], mask[:, kt, :],
                                                                                    ALU.mult), [Pk], [Pk])
                            last = (sc == nsc - 1 and kt == 7)
                            S.mm([(po[:, q0:G], vv[i][:, kt, :], P_[:, 0:N])], [kkey, Pk], [pok], start=first, stop=last)
                            S.mm([(pd[:, q0:G], ones_bf[:], P_[:, 0:N])], [Pk], [pdk], start=first, stop=last)
                            first = False
                    S.op("dve", lambda e: e.reciprocal(rden[:], pd[:]), [pdk], ["rden"])
                    S.op("dve", lambda e, h=h: e.tensor_tensor(oT[:, h, :], po[:], rden[:], ALU.mult), [pok, "rden"], ["oT"])
                S.fence()
                es3.close()
                branch_out(1, w_mo, H, oT, "oT")
                S.fence()

            with ExitStack() as es2:
                sb2 = lambda name, shape, dt: es2.enter_context(nc.sbuf_tensor(f"{name}_b{g}", shape, dt))
                mq = sb2("mq", [128, 8, G], F32)
                mqn = sb2("mqn", [128, 8, G], BF16)
                om = sb2("om", [128, 8, G], BF16)
                PM = [sb2(f"PM{i}", [128, G], BF16) for i in range(2)]
                rden = sb2("rdenm", [128, G], F32)
                for m2 in range(4):
                    wv, wk = wload(w_in, 0, DC, O_MQ + m2 * 256, 256)
                    for q in range(2):
                        pb, pbk = pbank(0, 4)
                        S.mm([(pb[:], wv[:, k, q * 128:(q + 1) * 128], hT[:, k, :]) for k in range(DC)], [wk, "hT"], [pbk])
                        S.op("act", lambda e, pb=pb, m=m2 * 2 + q: e.activation(out=mq[:, m, :], in_=pb[:], func=AF.Copy),
                             [pbk], ["mq"])
                for hh in range(4):
                    r, rk = stat_rstd([(mq[:, 2 * hh + q, :], 128) for q in range(2)], 256, ["mq"])
                    for q in range(2):
                        S.op("dve", lambda e, hh=hh, q=q, r=r: e.scalar_tensor_tensor(
                            mqn[:, 2 * hh + q, :], mq[:, 2 * hh + q, :], gmq[:, q:q + 1], r[:], ALU.mult, ALU.mult),
                            ["mq", rk], ["mqn"])
                    for kt in range(2):
                        pS, pSk = pbank(0, 4)
                        S.mm([(pS[:], mkT[:, 2 * hh + q, kt * 128:(kt + 1) * 128], mqn[:, 2 * hh + q, :]) for q in range(2)],
                             ["mkT", "mqn"], [pSk])
                        S.op("act", lambda e, pS=pS, kt=kt: e.activation(out=PM[kt][:], in_=pS[:], func=AF.Exp, scale=MEM_SCALE),
                             [pSk], [f"PM{kt}"])
                    pd, pdk = ps[7], "ps7"
                    S.mm([(pd[:], ones_bf[:], PM[kt][:]) for kt in range(2)], ["PM0", "PM1"], [pdk])
                    S.op("dve", lambda e: e.reciprocal(rden[:], pd[:]), [pdk], ["rdenm"])
                    for e_ in range(2):
                        po, pok = ps[6], "ps6"
                        S.mm([(po[:], mv[:, kt, hh * 256 + e_ * 128:hh * 256 + (e_ + 1) * 128], PM[kt][:]) for kt in range(2)],
                             ["PM0", "PM1", "mv"], [pok])
                        S.op("dve", lambda e, hh=hh, e_=e_: e.tensor_tensor(om[:, 2 * hh + e_, :], po[:], rden[:], ALU.mult),
                             [pok, "rdenm"], ["om"])
                branch_out(2, w_meo, 8, om, "om")
                S.fence()

            for m in range(DC):
                S.op("act", lambda e, m=m: e.activation(out=mT[:, m, :], in_=merged[:, m, :], func=AF.Copy), ["merged"], ["mT"])
            S.fence()
        with ExitStack() as es:
            sbt = lambda name, shape, dt: es.enter_context(nc.sbuf_tensor(f"{name}_c{g}", shape, dt))
            x1T = sbt("x1T", [128, DC, G], F32)
            hff = sbt("hff", [128, FC, G], BF16)
            otok = sbt("otok", [128, 2, D], F32)
            S.dma("sp", x1T[:], x1d[:, :, g * G:(g + 1) * G], [], ["x1T"], "ld0")
            for m2 in range(DC // 2):
                wv, wk = wload(w_out, 0, DC, m2 * 256, 256)
                for q in range(2):
                    m = m2 * 2 + q
                    pb, pbk = pbank(0, 4)
                    S.mm([(pb[:], wv[:, k, q * 128:(q + 1) * 128], mT[:, k, :]) for k in range(DC)], [wk, "mT"], [pbk])
                    S.op("dve", lambda e, pb=pb, m=m: e.tensor_tensor(x1T[:, m, :], x1T[:, m, :], pb[:], ALU.add),
                         [pbk, "x1T"], ["x1T"])
            rmsnorm_fm(x1T, "x1T", DC, g2, 2048, hT, "hT")
            ffn(x1T, "x1T", hT, "hT", hff, w2gu, w2d)
            for j in range(4):
                for c4 in range(4):
                    pb, pbk = pbank(0, 4)
                    S.tr([(pb[:, q * 128:(q + 1) * 128], x1T[:, c4 * 4 + q, j * 128:(j + 1) * 128]) for q in range(4)],
                         ident[:], ["x1T"], [pbk])
                    S.op("act" if c4 % 2 else "dve",
                         (lambda e, pb=pb, j=j, c4=c4: e.activation(out=otok[:, j % 2, c4 * 512:(c4 + 1) * 512], in_=pb[:],
                                                                    func=AF.Copy)) if c4 % 2 else
                         (lambda e, pb=pb, j=j, c4=c4: e.tensor_copy(otok[:, j % 2, c4 * 512:(c4 + 1) * 512], pb[:])),
                         [pbk], [f"otok{j % 2}"])
                S.dma("sp", out_d[4 * g + j], otok[:, j % 2, :], [f"otok{j % 2}"], [f"outd.{g}.{j}"], f"sto{j % 2}")
            S.fence()
    S.final()
    print('KERNEL STATS', S.cnt, 'waits', S.nwait, 'dma', {k: v[1] // 16 for k, v in S.dsem.items()}, flush=True)
    return nc


_CACHE = {}


def _fm(v, C):
    return np.ascontiguousarray(np.asarray(v, np.float32).reshape(C, 128).T)


def kernel(**inp):
    f32 = np.float32
    x = np.asarray(inp["x"], f32)[0]
    mem = np.ascontiguousarray(np.asarray(inp["mem"], f32)[0])
    pos = np.asarray(inp["positions"])[0].astype(np.int32)
    sq = lambda k: np.ascontiguousarray(np.asarray(inp[k], f32)[0])
    common = {
        "mem": mem,
        "g_ffn1": _fm(sq("g_ffn1"), DC), "g_mix": _fm(sq("g_mix"), DC), "g_ffn2": _fm(sq("g_ffn2"), DC),
        "g_mem": _fm(sq("g_mem"), DC), "b_gate": _fm(sq("b_gate"), 48),
        "conv_w": np.ascontiguousarray(sq("conv_w").T.reshape(CC, 128, 31).transpose(1, 0, 2)),
        "conv_b": _fm(sq("conv_b"), CC), "conv_ln_g": _fm(sq("conv_ln_g"), CC), "conv_ln_b": _fm(sq("conv_ln_b"), CC),
        "g_q_a": _fm(sq("g_q_a"), 4), "g_kv_a": _fm(sq("g_kv_a"), 4),
        "gqn": sq("g_qnorm")[0:128].reshape(128, 1).copy(), "gqr": sq("g_qnorm")[128:192].reshape(64, 1).copy(),
        "gkn": sq("g_knorm")[0:128].reshape(128, 1).copy(), "gkr": sq("g_knorm")[128:192].reshape(64, 1).copy(),
        "g_mqnorm": _fm(sq("g_mqnorm"), 2), "g_mknorm": _fm(sq("g_mknorm"), 2),
        "ident": np.eye(128, dtype=f32),
        "w_ffn1_gu": sq("w_ffn1_gu"), "w_ffn1_down": sq("w_ffn1_down"), "w_in": sq("w_in"),
        "w_conv_out": sq("w_conv_out"), "w_uq": sq("w_uq"), "w_ukv": sq("w_ukv"), "w_mla_out": sq("w_mla_out"),
        "w_mem_kv": sq("w_mem_kv"), "w_mem_out": sq("w_mem_out"), "w_out": sq("w_out"),
        "w_ffn2_gu": sq("w_ffn2_gu"), "w_ffn2_down": sq("w_ffn2_down"),
    }
    rot = np.zeros((64, 64), f32)
    for i in range(32):
        rot[i + 32, i] = 1.0
        rot[i, i + 32] = -1.0
    common["rot"] = rot
    invf = (10000.0 ** (-np.arange(0, 64, 2, dtype=f32) / f32(64))).astype(f32)
    common["invf"] = np.concatenate([invf, invf]).reshape(64, 1).astype(f32)
    tri = (np.arange(128)[:, None] <= np.arange(128)[None, :]).astype(f32)
    in_maps = []
    xg = x.reshape(NT, NCORE, 128, D)
    pg = pos.reshape(NT, NCORE, 128)
    for c in range(NCORE):
        m = dict(common)
        order = [(c + s_) % NCORE for s_ in range(NCORE)]
        m["x"] = np.ascontiguousarray(xg[:, order]).reshape(NTA, 128, D)
        pl = np.ascontiguousarray(pg[:, order]).reshape(1, TA)
        m["pos"] = np.ascontiguousarray(np.broadcast_to(pl, (64, TA))).astype(np.int32)
        mk = np.zeros((128, 8, 128), f32)
        mk[:, 0, :] = tri
        if c >= 1:
            mk[:, 8 - c:, :] = 1.0
        m["mask"] = mk
        sel = np.zeros((128, 8), f32)
        sel[:, 0] = 1.0 if c >= 1 else 0.0
        sel[:, 1] = 1.0 if c == 0 else 0.0
        m["sel"] = sel
        in_maps.append(m)
    if "nc" not in _CACHE:
        _CACHE["nc"] = build()
    res = run_bass_kernel_spmd(_CACHE["nc"], in_maps, core_ids=list(range(NCORE)))
    out = np.empty((NT, NCORE, 128, D), f32)
    for c in range(NCORE):
        out[:, c] = np.asarray(res.results[c]["out"], f32).reshape(NT, 128, D)
    return out.reshape(1, NT * NCORE * 128, D)
```

```python
import numpy as np
import ml_dtypes
from contextlib import ExitStack
import concourse.bass as bass
import concourse.mybir as mybir
from concourse.bass_utils import run_bass_kernel_spmd

F32 = mybir.dt.float32
BF16 = mybir.dt.bfloat16
I32 = mybir.dt.int32
AF = mybir.ActivationFunctionType
ALU = mybir.AluOpType

NCORE = 8
D = 2048
DC = 16
DFF = 5632
FC = 44
T = 2048
G = 512
NG = 4
NT = 16
NTA = 128
TA = NTA * 128
NGA = TA // 512
CCH = 1536
CC = 12
H = 12
EPS = 1e-6
O_CV, O_CG, O_CQ, O_CKV, O_KPE, O_MQ, O_GATE = 0, 1536, 3072, 3584, 4096, 4160, 5184
MLA_SCALE = 192 ** -0.5
MEM_SCALE = 256 ** -0.5
TWO_PI = 2.0 * np.pi
STAGE = "full"


class Sched:
    def __init__(s, nc):
        s.nc = nc
        s.E = dict(pe=nc.tensor, act=nc.scalar, dve=nc.vector, pool=nc.gpsimd, sp=nc.sync)
        s.csem = {e: nc.alloc_semaphore("c_" + e) for e in ("pe", "act", "dve")}
        s.cnt = {e: 0 for e in s.csem}
        s.dsem = {}
        s.waited = {e: {} for e in s.E}
        s.res = {}
        s.nwait = 0

    def _wait(s, eng, evs):
        best = {}
        for ev in evs:
            if ev is None:
                continue
            sem, val = ev
            k = id(sem)
            if k not in best or best[k][1] < val:
                best[k] = ev
        for k, (sem, val) in best.items():
            if eng == "pe" and sem is s.csem["pe"]:
                continue
            if s.waited[eng].get(k, 0) >= val:
                continue
            s.E[eng].wait_ge(sem, val)
            s.nwait += 1
            s.waited[eng][k] = val

    def _deps(s, reads, writes):
        evs = []
        for k in reads:
            r = s.res.get(k)
            if r:
                evs.append(r[0])
                if k.startswith("ps"):
                    evs.extend(r[1].values())
        for k in writes:
            r = s.res.get(k)
            if r:
                evs.append(r[0])
                evs.extend(r[1].values())
        return evs

    def _commit(s, ev, reads, writes):
        kk = id(ev[0])
        for k in reads:
            r = s.res.setdefault(k, [None, {}])
            if kk not in r[1] or r[1][kk][1] < ev[1]:
                r[1][kk] = ev
        for k in writes:
            s.res[k] = [ev, {}]

    def op(s, eng, fn, reads=(), writes=()):
        s._wait(eng, s._deps(reads, writes))
        ins = fn(s.E[eng])
        s.cnt[eng] += 1
        ins.then_inc(s.csem[eng], 1)
        s._commit((s.csem[eng], s.cnt[eng]), reads, writes)

    def mm(s, mms, reads, writes, start=True, stop=True):
        s._wait("pe", s._deps(reads, writes))
        pe = s.E["pe"]
        n = len(mms)
        ins = None
        for i, (o, l, r) in enumerate(mms):
            ins = pe.matmul(o, l, r, start=(start and i == 0), stop=(stop and i == n - 1))
        s.cnt["pe"] += 1
        ins.then_inc(s.csem["pe"], 1)
        s._commit((s.csem["pe"], s.cnt["pe"]), reads, writes)

    def tr(s, outs_ins, ident, reads, writes):
        s._wait("pe", s._deps(reads, writes))
        pe = s.E["pe"]
        ins = None
        for o, i in outs_ins:
            ins = pe.transpose(o, i, ident)
        s.cnt["pe"] += 1
        ins.then_inc(s.csem["pe"], 1)
        s._commit((s.csem["pe"], s.cnt["pe"]), reads, writes)

    def dma(s, q, out, in_, reads, writes, slot, **kw):
        s._wait(q, s._deps(reads, writes))
        if slot not in s.dsem:
            s.dsem[slot] = [s.nc.alloc_semaphore("d_" + slot), 0]
        d = s.dsem[slot]
        s.E[q].dma_start(out=out, in_=in_, **kw).then_inc(d[0], 16)
        d[1] += 16
        s._commit((d[0], d[1]), reads, writes)

    def all_events(s):
        evs = [(s.csem[e], s.cnt[e]) for e in s.csem if s.cnt[e] > 0]
        evs += [(d[0], d[1]) for d in s.dsem.values() if d[1] > 0]
        return evs

    def fence(s, engs=("pe", "act", "dve", "sp")):
        evs = s.all_events()
        for e in engs:
            s._wait(e, evs)

    def final(s):
        evs = s.all_events()
        for e in ("sp", "pool", "act", "dve", "pe"):
            s._wait(e, evs)


def build():
    nc = bass.Bass("TRN2", target_bir_lowering=False)
    S = Sched(nc)

    def din(name, shape, dt=F32):
        return nc.dram_tensor(name, list(shape), dt, kind="ExternalInput").ap()

    x_d = din("x", [NTA, 128, D])
    mem_d = din("mem", [256, D])
    pos_d = din("pos", [64, TA], I32)
    g1_d = din("g_ffn1", [128, DC])
    gm_d = din("g_mix", [128, DC])
    g2_d = din("g_ffn2", [128, DC])
    gmem_d = din("g_mem", [128, DC])
    bg_d = din("b_gate", [128, 48])
    cw_d = din("conv_w", [128, CC, 31])
    cb_d = din("conv_b", [128, CC])
    lng_d = din("conv_ln_g", [128, CC])
    lnb_d = din("conv_ln_b", [128, CC])
    gqa_d = din("g_q_a", [128, 4])
    gkva_d = din("g_kv_a", [128, 4])
    gqn_d = din("gqn", [128, 1])
    gqr_d = din("gqr", [64, 1])
    gkn_d = din("gkn", [128, 1])
    gkr_d = din("gkr", [64, 1])
    gmq_d = din("g_mqnorm", [128, 2])
    gmk_d = din("g_mknorm", [128, 2])
    ident_d = din("ident", [128, 128])
    rot_d = din("rot", [64, 64])
    mask_d = din("mask", [128, 8, 128])
    sel_d = din("sel", [128, 8])
    invf_d = din("invf", [64, 1])
    w1gu = din("w_ffn1_gu", [D, 2 * DFF])
    w1d = din("w_ffn1_down", [DFF, D])
    w_in = din("w_in", [D, 11328])
    w_co = din("w_conv_out", [CCH, D])
    w_uq = din("w_uq", [512, H * 192])
    w_ukv = din("w_ukv", [512, H * 256])
    w_mo = din("w_mla_out", [H * 128, D])
    w_mkv = din("w_mem_kv", [D, 2048])
    w_meo = din("w_mem_out", [1024, D])
    w_out = din("w_out", [D, D])
    w2gu = din("w_ffn2_gu", [D, 2 * DFF])
    w2d = din("w_ffn2_down", [DFF, D])
    out_d = nc.dram_tensor("out", [NT, 128, D], F32, kind="ExternalOutput").ap()

    x1d = nc.dram_tensor("x1d", [128, DC, T], F32).ap()
    kloc = nc.dram_tensor("kloc", [H * 192, TA], BF16).ap()
    vloc = nc.dram_tensor("vloc", [H * TA, 128], BF16).ap()
    tloc = nc.dram_tensor("tloc", [CCH, 512], F32).ap()
    hd = nc.dram_tensor("hd", [CCH, 512], F32).ap()
    wbf_gu = nc.dram_tensor("wbf_gu", [2 * (FC // 2), 128, DC * 256], BF16).ap()
    wbf_d = nc.dram_tensor("wbf_d", [DC, 128, FC * 128], BF16).ap()

    sb = nc.alloc_sbuf_tensor
    ident = sb("ident_s", [128, 128], F32)
    rot = sb("rot_s", [64, 64], F32)
    mask = sb("mask_s", [128, 8, 128], F32)
    sel = sb("sel_s", [128, 8], F32)
    invf = sb("invf_s", [64, 1], F32)
    g1 = sb("g1_s", [128, DC], F32)
    gm = sb("gm_s", [128, DC], F32)
    g2 = sb("g2_s", [128, DC], F32)
    gmem = sb("gmem_s", [128, DC], F32)
    bg = sb("bg_s", [128, 48], F32)
    cw = sb("cw_s", [128, CC, 31], F32)
    cb = sb("cb_s", [128, CC], F32)
    lng = sb("lng_s", [128, CC], F32)
    lnb = sb("lnb_s", [128, CC], F32)
    gqa = sb("gqa_s", [128, 4], F32)
    gkva = sb("gkva_s", [128, 4], F32)
    gqn = sb("gqn_s", [128, 1], F32)
    gqr = sb("gqr_s", [64, 1], F32)
    gkn = sb("gkn_s", [128, 1], F32)
    gkr = sb("gkr_s", [64, 1], F32)
    gmq = sb("gmq_s", [128, 2], F32)
    gmk = sb("gmk_s", [128, 2], F32)
    ones_bf = sb("ones_bf", [128, 128], BF16)
    epst = sb("epst", [128, 1], F32)
    inv = {n: sb(f"inv{n}", [128, 128], F32) for n in (2048, 512, 192, 256, 1536)}
    mkT = sb("mkT", [128, 8, 256], BF16)
    mv = sb("mv", [128, 2, 1024], BF16)
    csg = sb("csg", [64, 2, G], F32)
    cs_d = nc.dram_tensor("cs_d", [64, 2, TA], F32).ap()
    rstd = [sb(f"rstd{i}", [128, G], F32) for i in range(2)]
    wring = [sb(f"wring{i}", [128, 5632], BF16) for i in range(3)]
    ps = [nc.alloc_psum_tensor(f"ps{i}", [128, G], F32) for i in range(8)]

    consts = [(ident, ident_d), (rot, rot_d), (mask, mask_d), (sel, sel_d), (invf, invf_d), (g1, g1_d),
              (gm, gm_d), (g2, g2_d), (gmem, gmem_d), (bg, bg_d), (cw, cw_d), (cb, cb_d), (lng, lng_d),
              (lnb, lnb_d), (gqa, gqa_d), (gkva, gkva_d), (gqn, gqn_d), (gqr, gqr_d), (gkn, gkn_d),
              (gkr, gkr_d), (gmq, gmq_d), (gmk, gmk_d)]
    for ci, (t, d_) in enumerate(consts):
        S.dma("sp", t[:], d_, [], [f"const{ci}"], "const")
    S.op("dve", lambda e: e.memset(ones_bf[:], 1.0), [], ["constA"])
    S.op("dve", lambda e: e.memset(epst[:], EPS), [], ["constB"])
    for n, t in inv.items():
        S.op("dve", lambda e, t=t, n=n: e.memset(t[:], 1.0 / n), [], [f"constI{n}"])
    S.fence(("pe", "act", "dve", "sp", "pool"))
    CONST = []

    wstate = {"i": 0, "n": 3}

    def wload(w, r0, KC, c0, ncols):
        i = wstate["i"] % wstate["n"]
        wstate["i"] += 1
        view = wring[i][:, 0:KC * ncols].rearrange("p (k n) -> p k n", k=KC)
        src = w[r0:r0 + KC * 128, c0:c0 + ncols].rearrange("(k p) n -> p k n", p=128)
        S.dma("pool", view, src, [], [f"w{i}"], f"w{i}")
        return view, f"w{i}"

    def wload_bf(src, KC, ncols, rkey):
        i = wstate["i"] % wstate["n"]
        wstate["i"] += 1
        S.dma("pool", wring[i][:, 0:KC * ncols], src, [rkey], [f"w{i}"], f"w{i}")
        return wring[i][:, 0:KC * ncols].rearrange("p (k n) -> p k n", k=KC), f"w{i}"

    pstate = {}

    def pbank(lo=0, n=4):
        c = pstate.get((lo, n), 0)
        pstate[(lo, n)] = c + 1
        i = lo + c % n
        return ps[i], f"ps{i}"

    rs = {"i": 0}

    def stat_rstd(parts, n, extra_reads, eps=EPS):
        i = rs["i"] % 2
        rs["i"] += 1
        N = parts[0][0].shape[-1]
        sbank, skey = ps[4 + i], f"ps{4 + i}"
        for j, (ap, P) in enumerate(parts):
            sq, sqk = sqbuf[j % 2], f"sq{j % 2}"
            S.op("act", lambda e, ap=ap, P=P, sq=sq: e.activation(out=sq[0:P, 0:N], in_=ap, func=AF.Square),
                 extra_reads, [sqk])
            S.mm([(sbank[:, 0:N], ones_bf[0:P, :], sq[0:P, 0:N])], [sqk], [skey],
                 start=(j == 0), stop=(j == len(parts) - 1))
        r, rk = rstd[i], f"rstd{i}"
        S.op("act", lambda e: e.activation(out=r[:, 0:N], in_=sbank[:, 0:N], func=AF.Sqrt, bias=epst[:, 0:1],
                                           scale=1.0 / n), [skey], [rk])
        S.op("dve", lambda e: e.reciprocal(r[:, 0:N], r[:, 0:N]), [rk], [rk])
        return r, rk

    sqbuf = [sb(f"sqb{i}", [128, G], BF16) for i in range(2)]

    def rmsnorm_fm(src, skey, C, gain, n, out, okey, N=G):
        r, rk = stat_rstd([(src[:, c, :], 128) for c in range(C)], n, [skey])
        for c in range(C):
            S.op("dve", lambda e, c=c: e.scalar_tensor_tensor(out[:, c, :], src[:, c, :], gain[:, c:c + 1],
                                                             r[:, 0:N], ALU.mult, ALU.mult),
                 [skey, rk], [okey])

    def ffn(xT, xkey, xn, xnkey, hff, wgu, wdn, pre=None):
        for m2 in range(FC // 2):
            if pre is None:
                wg, wgk = wload(wgu, 0, DC, m2 * 256, 256)
                wu, wuk = wload(wgu, 0, DC, DFF + m2 * 256, 256)
            else:
                wg, wgk = wload_bf(pre[0][2 * m2], DC, 256, f"wbfg{2 * m2}")
                wu, wuk = wload_bf(pre[0][2 * m2 + 1], DC, 256, f"wbfg{2 * m2 + 1}")
            for mm_ in range(2):
                m = m2 * 2 + mm_
                pg, pgk = pbank(0, 4)
                pu, puk = pbank(0, 4)
                S.mm([(pg[:], wg[:, k, mm_ * 128:(mm_ + 1) * 128], xn[:, k, :]) for k in range(DC)],
                     [wgk, xnkey], [pgk])
                S.mm([(pu[:], wu[:, k, mm_ * 128:(mm_ + 1) * 128], xn[:, k, :]) for k in range(DC)],
                     [wuk, xnkey], [puk])
                sg, sgk = sgbuf[m % 2], f"sg{m % 2}"
                S.op("act", lambda e, pg=pg, sg=sg: e.activation(out=sg[:], in_=pg[:], func=AF.Silu), [pgk], [sgk])
                S.op("dve", lambda e, pu=pu, sg=sg, m=m: e.tensor_tensor(hff[:, m, :], sg[:], pu[:], ALU.mult),
                     [sgk, puk], ["hff"])
        for mo in range(DC):
            if pre is None:
                wd, wdk = wload(wdn, 0, FC, mo * 128, 128)
            else:
                wd, wdk = wload_bf(pre[1][mo], FC, 128, f"wbfd{mo}")
            pa, pak = pbank(6, 2)
            S.mm([(pa[:], wd[:, k, :], hff[:, k, :]) for k in range(FC)], [wdk, "hff"], [pak])
            S.op("dve", lambda e, pa=pa, mo=mo: e.scalar_tensor_tensor(xT[:, mo, :], pa[:], 0.5, xT[:, mo, :],
                                                                      ALU.mult, ALU.add),
                 [pak, xkey], [xkey])

    sgbuf = [sb(f"sgb{i}", [128, G], F32) for i in range(2)]

    with ExitStack() as es:
        posi = es.enter_context(nc.sbuf_tensor("posi", [64, T], I32))
        ang = es.enter_context(nc.sbuf_tensor("ang", [64, T], F32))
        ang2 = es.enter_context(nc.sbuf_tensor("ang2", [64, T], F32))
        sinn = es.enter_context(nc.sbuf_tensor("sinn", [64, T], F32))
        cosn = es.enter_context(nc.sbuf_tensor("cosn", [64, T], F32))
        for ch in range(TA // T):
            S.dma("sp", posi[:], pos_d[:, ch * T:(ch + 1) * T], [], ["posi"], "const")
            S.op("dve", lambda e: e.tensor_copy(ang[:], posi[:]), ["posi"], ["ang"])
            S.op("dve", lambda e: e.tensor_scalar(ang[:], ang[:], invf[:, 0:1], None, ALU.mult), ["ang"], ["ang"])
            for tgt, tk, shift in ((sinn, "sinn", 0.0), (cosn, "cosn", np.pi / 2)):
                S.op("dve", lambda e, shift=shift: e.tensor_scalar(ang2[:], ang[:], shift, 1.0 / TWO_PI, ALU.add, ALU.mult),
                     ["ang"], ["ang2"])
                S.op("dve", lambda e: e.tensor_copy(posi[:], ang2[:]), ["ang2"], ["posi"])
                S.op("dve", lambda e: e.tensor_copy(ang2[:], posi[:]), ["posi"], ["ang2"])
                S.op("dve", lambda e, tgt=tgt: e.scalar_tensor_tensor(tgt[:], ang2[:], -TWO_PI, ang[:], ALU.mult, ALU.add),
                     ["ang2", "ang"], [tk])
                if shift != 0.0:
                    S.op("dve", lambda e, tgt=tgt, shift=shift: e.tensor_scalar(tgt[:], tgt[:], shift, None, ALU.add),
                         [tk], [tk])
                S.op("act", lambda e, tgt=tgt: e.activation(out=tgt[:], in_=tgt[:], func=AF.Sin, scale=0.999999),
                     [tk], [tk])
            S.dma("sp", cs_d[:, 0, ch * T:(ch + 1) * T], sinn[:], ["sinn"], [f"cs_d{ch}"], "st0")
            S.dma("sp", cs_d[:, 1, ch * T:(ch + 1) * T], cosn[:], ["cosn"], [f"cs_d{ch}"], "st0")
        S.fence()

    def load_cs(g):
        S.dma("sp", csg[:], cs_d[:, :, g * G:(g + 1) * G], [], ["csg"], "ld1")

    def load_cs_own(g):
        for i in range(2):
            S.dma("sp", csg[:, i, :].rearrange("p (j t) -> p j t", j=4),
                  cs_d[:, i, :].rearrange("p (j s t) -> p j s t", s=8, t=128)[:, 4 * g:4 * g + 4, 0, :],
                  [], ["csg"], "ld1")

    def rope_apply(src, skey, N, outf, okey, tmp1, tmp2):
        pr, prk = pbank(0, 4)
        S.mm([(pr[0:64, 0:N], rot[:], src)], [skey], [prk])
        S.op("dve", lambda e: e.tensor_tensor(tmp1, pr[0:64, 0:N], csg[:, 0, 0:N], ALU.mult),
             [prk, "csg"], [okey + "t1"])
        S.op("dve", lambda e: e.tensor_tensor(tmp2, src, csg[:, 1, 0:N], ALU.mult),
             [skey, "csg"], [okey + "t2"])
        S.op("dve", lambda e: e.tensor_tensor(outf, tmp2, tmp1, ALU.subtract), [okey + "t1", okey + "t2"], [okey])

    with ExitStack() as es:
        memtok = es.enter_context(nc.sbuf_tensor("memtok", [128, 2, D], F32))
        memT = es.enter_context(nc.sbuf_tensor("memT", [128, DC, 256], F32))
        memn = es.enter_context(nc.sbuf_tensor("memn", [128, DC, 256], BF16))
        mkf = es.enter_context(nc.sbuf_tensor("mkf", [128, 8, 256], F32))
        S.dma("sp", memtok[:], mem_d.rearrange("(j p) d -> p j d", p=128), [], ["memtok"], "ld0")
        for j in range(2):
            for c4 in range(4):
                pb, pbk = pbank(0, 4)
                S.tr([(pb[:, q * 128:(q + 1) * 128], memtok[:, j, (c4 * 4 + q) * 128:(c4 * 4 + q + 1) * 128])
                      for q in range(4)], ident[:], ["memtok"], [pbk])
                S.op("act", lambda e, pb=pb, j=j, c4=c4: e.activation(
                    out=memT[:, c4 * 4:c4 * 4 + 4, j * 128:(j + 1) * 128],
                    in_=pb[:].rearrange("p (q t) -> p q t", q=4), func=AF.Copy), [pbk], ["memT"])
        rmsnorm_fm(memT, "memT", DC, gmem, 2048, memn, "memn", N=256)
        for m2 in range(4):
            wv, wk = wload(w_mkv, 0, DC, m2 * 256, 256)
            for q in range(2):
                m = m2 * 2 + q
                pb, pbk = pbank(0, 4)
                S.mm([(pb[:, 0:256], wv[:, k, q * 128:(q + 1) * 128], memn[:, k, :]) for k in range(DC)],
                     [wk, "memn"], [pbk])
                S.op("act", lambda e, pb=pb, m=m: e.activation(out=mkf[:, m, :], in_=pb[:, 0:256], func=AF.Copy),
                     [pbk], ["mkf"])
        for hh in range(4):
            r, rk = stat_rstd([(mkf[:, 2 * hh + q, :], 128) for q in range(2)], 256, ["mkf"])
            for q in range(2):
                S.op("dve", lambda e, hh=hh, q=q, r=r: e.scalar_tensor_tensor(
                    mkT[:, 2 * hh + q, :], mkf[:, 2 * hh + q, :], gmk[:, q:q + 1], r[:, 0:256], ALU.mult, ALU.mult),
                    ["mkf", rk], ["mkT"])
        for m2 in range(4):
            wv, wk = wload(w_mkv, 0, DC, 1024 + m2 * 256, 256)
            for j in range(2):
                pb, pbk = pbank(0, 4)
                S.mm([(pb[:, 0:256], memn[:, k, j * 128:(j + 1) * 128], wv[:, k, :]) for k in range(DC)],
                     [wk, "memn"], [pbk])
                S.op("act", lambda e, pb=pb, j=j, m2=m2: e.activation(
                    out=mv[:, j, m2 * 256:(m2 + 1) * 256], in_=pb[:, 0:256], func=AF.Copy), [pbk], ["mv"])
        S.fence()

    esA = ExitStack()
    xtl = esA.enter_context(nc.sbuf_tensor("xtl", [128, DC, 512], BF16))
    for m2 in range(FC // 2):
        for gu in range(2):
            b_ = 2 * m2 + gu
            wv, wk = wload(w1gu, 0, DC, gu * DFF + m2 * 256, 256)
            S.dma("sp", wbf_gu[b_], wv.rearrange("p k n -> p (k n)"), [wk], [f"wbfg{b_}"], f"wst{b_ % 3}")
    for mo in range(DC):
        wv, wk = wload(w1d, 0, FC, mo * 128, 128)
        S.dma("sp", wbf_d[mo], wv.rearrange("p k n -> p (k n)"), [wk], [f"wbfd{mo}"], f"wst{mo % 3}")
    S.fence(("pe", "act", "dve", "sp", "pool"))
    if STAGE == "P":
        S.final()
        return nc
    for g in range(NGA if STAGE != "A1" else 2):
        jsup, own = g // 2, (g % 2 == 0)
        with ExitStack() as es:
            sbt = lambda name, shape, dt: es.enter_context(nc.sbuf_tensor(f"{name}_a{g}", shape, dt))
            xT = sbt("xT", [128, DC, G], F32)
            xn = sbt("xn", [128, DC, G], BF16)
            load_cs(g)
            with ExitStack() as es0:
                xtok = es0.enter_context(nc.sbuf_tensor(f"xtok_a{g}", [128, 2, D], F32))
                for j in range(4):
                    S.dma("sp", xtok[:, j % 2, :], x_d[4 * g + j], [], [f"xtok{j % 2}"], f"ld{j % 2}")
                    for c4 in range(4):
                        pb, pbk = pbank(0, 4)
                        S.tr([(pb[:, q * 128:(q + 1) * 128], xtok[:, j % 2, (c4 * 4 + q) * 128:(c4 * 4 + q + 1) * 128])
                              for q in range(4)], ident[:], [f"xtok{j % 2}"], [pbk])
                        if c4 % 2:
                            S.op("act", lambda e, pb=pb, j=j, c4=c4: e.activation(
                                out=xT[:, c4 * 4:c4 * 4 + 4, j * 128:(j + 1) * 128],
                                in_=pb[:].rearrange("p (q t) -> p q t", q=4), func=AF.Copy), [pbk], ["xT"])
                        else:
                            S.op("dve", lambda e, pb=pb, j=j, c4=c4: e.tensor_copy(
                                xT[:, c4 * 4:c4 * 4 + 4, j * 128:(j + 1) * 128],
                                pb[:].rearrange("p (q t) -> p q t", q=4)), [pbk], ["xT"])
                S.fence()
            rmsnorm_fm(xT, "xT", DC, g1, 2048, xn, "xn")
            with ExitStack() as es0:
                hff = es0.enter_context(nc.sbuf_tensor(f"hff_a{g}", [128, FC, G], BF16))
                ffn(xT, "xT", xn, "xn", hff, w1gu, w1d, pre=(wbf_gu, wbf_d))
                S.fence()
            if own:
                S.dma("sp", x1d[:, :, jsup * 128:(jsup + 1) * 128], xT[:, :, 0:128], ["xT"], [f"x1d{jsup}"], "st0")
            rmsnorm_fm(xT, "xT", DC, gm, 2048, xn, "xn")
            with ExitStack() as es2:
                sb2 = lambda name, shape, dt: es2.enter_context(nc.sbuf_tensor(f"{name}_a{g}", shape, dt))
                ckv = sb2("ckv", [128, 4, G], F32)
                ckvn = sb2("ckvn", [128, 4, G], BF16)
                kpe = sb2("kpe", [64, G], F32)
                kper = sb2("kper", [64, G], F32)
                t1 = sb2("t1", [64, G], F32)
                t2 = sb2("t2", [64, G], F32)
                knf = sb2("knf", [128, 2, G], F32)
                kout = sb2("kout", [128, 2, G], BF16)
                krout = sb2("krout", [64, 2, G], BF16)
                vsb = sb2("vsb", [128, 4, H, 128], BF16)
                tail = sb2("tail", [128, CC, 128], F32)
                sgt = sb2("sgt", [128, 2, 128], F32)
                for m2 in range(2):
                    wv, wk = wload(w_in, 0, DC, O_CKV + m2 * 256, 256)
                    for q in range(2):
                        pb, pbk = pbank(0, 4)
                        S.mm([(pb[:], wv[:, k, q * 128:(q + 1) * 128], xn[:, k, :]) for k in range(DC)],
                             [wk, "xn"], [pbk])
                        S.op("act", lambda e, pb=pb, m=m2 * 2 + q: e.activation(out=ckv[:, m, :], in_=pb[:],
                                                                               func=AF.Copy), [pbk], ["ckv"])
                wv, wk = wload(w_in, 0, DC, O_KPE, 64)
                pb, pbk = pbank(0, 4)
                S.mm([(pb[0:64, :], wv[:, k, :], xn[:, k, :]) for k in range(DC)], [wk, "xn"], [pbk])
                kpesq = sb2("kpesq", [64, G], BF16)
                S.op("act", lambda e: e.activation(out=kpesq[:], in_=pb[0:64, :], func=AF.Square), [pbk], ["kpesq2"])
                S.op("dve", lambda e: e.tensor_scalar(kpe[:], pb[0:64, :], gkr[:, 0:1], None, ALU.mult),
                     [pbk], ["kpe"])
                rope_apply(kpe[:], "kpe", G, kper[:], "kper", t1[:], t2[:])
                rmsnorm_fm(ckv, "ckv", 4, gkva, 512, ckvn, "ckvn")
                wv, wk = None, None
                for h in range(H):
                    if h % 3 == 0:
                        wv, wk = wload(w_ukv, 0, 4, h * 256, 768)
                    hq = h % 3
                    pb, pbk = pbank(0, 4)
                    S.mm([(pb[:], wv[:, k, hq * 256:hq * 256 + 128], ckvn[:, k, :]) for k in range(4)],
                         [wk, "ckvn"], [pbk])
                    knh, knk = knf[:, h % 2, :], f"knf{h % 2}"
                    S.op("act", lambda e, pb=pb, knh=knh: e.activation(out=knh, in_=pb[:], func=AF.Copy), [pbk], [knk])
                    S.op("act", lambda e, knh=knh: e.activation(out=sqbuf[1][:], in_=knh, func=AF.Square), [knk], ["sq1"])
                    i = rs["i"] % 2
                    rs["i"] += 1
                    sbank, skey = ps[4 + i], f"ps{4 + i}"
                    S.mm([(sbank[:], ones_bf[0:64, :], kpesq[:]), (sbank[:], ones_bf[:], sqbuf[1][:])],
                         ["kpesq2", "sq1"], [skey])
                    r, rk = rstd[i], f"rstd{i}"
                    S.op("act", lambda e, r=r, sbank=sbank: e.activation(out=r[:], in_=sbank[:], func=AF.Sqrt,
                                                                         bias=epst[:, 0:1], scale=1.0 / 192), [skey], [rk])
                    S.op("dve", lambda e, r=r: e.reciprocal(r[:], r[:]), [rk], [rk])
                    S.op("dve", lambda e, r=r, h=h, knh=knh: e.scalar_tensor_tensor(kout[:, h % 2, :], knh, gkn[:, 0:1], r[:],
                                                                                  ALU.mult, ALU.mult),
                         [knk, rk], [f"kout{h % 2}"])
                    S.op("dve", lambda e, r=r, h=h: e.tensor_tensor(krout[:, h % 2, :], kper[:], r[0:64, :], ALU.mult),
                         ["kper", rk], [f"krout{h % 2}"])
                    S.dma("sp", kloc[h * 192:h * 192 + 128, g * G:(g + 1) * G], kout[:, h % 2, :],
                          [f"kout{h % 2}"], [f"kloc.{g}.{h}"], f"stk{h % 2}")
                    S.dma("sp", kloc[h * 192 + 128:h * 192 + 192, g * G:(g + 1) * G], krout[:, h % 2, :],
                          [f"krout{h % 2}"], [f"klocr.{g}.{h}"], f"stk{h % 2}")
                for h3 in range(4):
                    wv, wk = wload(w_ukv, 0, 4, h3 * 768, 768)
                    for j in range(4):
                        pb, pbk = pbank(0, 4)
                        S.mm([(pb[:, 0:384].rearrange("p (a d) -> p a d", a=3),
                               ckvn[:, k, j * 128:(j + 1) * 128],
                               wv[:, k, :].rearrange("p (a d) -> p a d", a=3)[:, :, 128:256]) for k in range(4)],
                             [wk, "ckvn"], [pbk])
                        S.op("act", lambda e, pb=pb, j=j, h3=h3: e.activation(
                            out=vsb[:, j, h3 * 3:h3 * 3 + 3, :],
                            in_=pb[:, 0:384].rearrange("p (a d) -> p a d", a=3), func=AF.Copy),
                            [pbk], [f"vsb{j}"])
                        if h3 == 3:
                            tok0 = (4 * g + j) * 128
                            S.dma("sp", vloc.rearrange("(h t) d -> t h d", h=H)[tok0:tok0 + 128, :, :],
                                  vsb[:, j, :, :], [f"vsb{j}"], [f"vloc.{g}.{j}"], f"stv{j % 2}")
                if not own:
                    S.op("dve", lambda e: e.tensor_copy(xtl[:, :, jsup * 32:(jsup + 1) * 32], xn[:, :, 480:512]),
                         ["xn"], ["xtl"])
                S.fence()
            S.fence()

    if STAGE == "A":
        S.final()
        return nc
    S.fence(("pe", "act", "dve", "sp", "pool"))
    with ExitStack() as est:
        tailf = est.enter_context(nc.sbuf_tensor("tailf", [128, CC, 512], F32))
        sgt2 = [est.enter_context(nc.sbuf_tensor(f"sgt2_{i}", [128, 512], F32)) for i in range(2)]
        for c2 in range(CC // 2):
            wv, wk = wload(w_in, 0, DC, O_CV + c2 * 256, 256)
            wg_, wgk = wload(w_in, 0, DC, O_CG + c2 * 256, 256)
            for q in range(2):
                cc = c2 * 2 + q
                pv, pvk = pbank(0, 4)
                pg, pgk = pbank(0, 4)
                S.mm([(pv[:], wv[:, k, q * 128:(q + 1) * 128], xtl[:, k, :]) for k in range(DC)], [wk, "xtl"], [pvk])
                S.mm([(pg[:], wg_[:, k, q * 128:(q + 1) * 128], xtl[:, k, :]) for k in range(DC)], [wgk, "xtl"], [pgk])
                S.op("act", lambda e, pg=pg, cc=cc: e.activation(out=sgt2[cc % 2][:], in_=pg[:], func=AF.Sigmoid),
                     [pgk], [f"sgt2_{cc % 2}"])
                S.op("dve", lambda e, pv=pv, cc=cc: e.tensor_tensor(tailf[:, cc, :], sgt2[cc % 2][:], pv[:], ALU.mult),
                     [pvk, f"sgt2_{cc % 2}"], ["tailf"])
        S.dma("sp", tloc.rearrange("(c p) n -> p c n", p=128), tailf[:], ["tailf"], ["tloc"], "st1")
        S.fence(("pe", "act", "dve", "sp", "pool"))
    esA.close()
    S.fence(("pe", "act", "dve", "sp", "pool"))
    with ExitStack() as es:
        hacc = es.enter_context(nc.sbuf_tensor("hacc", [128, CC, 512], F32))
        tb = es.enter_context(nc.sbuf_tensor("trk", [128, CC, 512], F32))
        S.dma("sp", tb[:], tloc.rearrange("(c p) n -> p c n", p=128), [], ["trk"], "ld0")
        S.op("dve", lambda e: e.tensor_scalar(hacc[:], tb[:], sel[:, 0:1], None, ALU.mult), ["trk"], ["hacc"])
        S.op("dve", lambda e: e.scalar_tensor_tensor(hacc[:, :, 32:512], tb[:, :, 0:480], sel[:, 1:2],
                                                     hacc[:, :, 32:512], ALU.mult, ALU.add), ["trk", "hacc"], ["hacc"])
        S.dma("sp", hd.rearrange("(c p) n -> p c n", p=128), hacc[:], ["hacc"], ["hd"], "st1")
        S.fence()
    if STAGE == "C":
        S.final()
        return nc
    hT = sb("hT", [128, DC, G], BF16)
    mT = sb("mT", [128, DC, G], BF16)
    for g in range(NG if STAGE == "full" else 1):
        J = g
        with ExitStack() as es:
            sbt = lambda name, shape, dt: es.enter_context(nc.sbuf_tensor(f"{name}_b{g}", shape, dt))
            load_cs_own(g)
            with ExitStack() as es1:
                x1T = es1.enter_context(nc.sbuf_tensor(f"x1Ta_b{g}", [128, DC, G], F32))
                S.dma("sp", x1T[:], x1d[:, :, g * G:(g + 1) * G], [], ["x1T"], "ld0")
                rmsnorm_fm(x1T, "x1T", DC, gm, 2048, hT, "hT")
                S.fence()
            merged = sbt("merged", [128, DC, G], F32)
            gsb = [sbt(f"gsb{i}", [128, G], F32) for i in range(2)]

            def branch_out(br, wy, KCy, act_in, akey):
                for m2 in range(DC // 2):
                    wv, wk = wload(wy, 0, KCy, m2 * 256, 256)
                    wg_, wgk = wload(w_in, 0, DC, O_GATE + br * D + m2 * 256, 256)
                    for q in range(2):
                        m = m2 * 2 + q
                        py, pyk = pbank(0, 4)
                        pg, pgk = pbank(0, 4)
                        S.mm([(py[:], wv[:, k, q * 128:(q + 1) * 128], act_in[:, k, :]) for k in range(KCy)],
                             [wk, akey], [pyk])
                        S.mm([(pg[:], wg_[:, k, q * 128:(q + 1) * 128], hT[:, k, :]) for k in range(DC)],
                             [wgk, "hT"], [pgk])
                        gb, gbk = gsb[m % 2], f"gsb{m % 2}"
                        S.op("act", lambda e, pg=pg, gb=gb, m=m: e.activation(
                            out=gb[:], in_=pg[:], func=AF.Sigmoid, bias=bg[:, br * DC + m:br * DC + m + 1]),
                            [pgk], [gbk])
                        if br == 0:
                            S.op("dve", lambda e, py=py, gb=gb, m=m: e.tensor_tensor(merged[:, m, :], gb[:], py[:],
                                                                                    ALU.mult), [pyk, gbk], ["merged"])
                        else:
                            S.op("dve", lambda e, py=py, gb=gb: e.tensor_tensor(gb[:], gb[:], py[:], ALU.mult),
                                 [pyk, gbk], [gbk])
                            S.op("dve", lambda e, gb=gb, m=m: e.tensor_tensor(merged[:, m, :], merged[:, m, :], gb[:],
                                                                             ALU.add), [gbk, "merged"], ["merged"])

            with ExitStack() as es2:
                sb2 = lambda name, shape, dt: es2.enter_context(nc.sbuf_tensor(f"{name}_b{g}", shape, dt))
                HC = CC // 2
                apad = sb2("apad", [128, HC, 4, 160], F32)
                cacc = sb2("cacc", [128, CC, G], F32)
                an = sb2("an", [128, CC, G], BF16)
                sgc = [sb2(f"sgc{i}", [128, G], F32) for i in range(2)]
                meansb = sb2("meansb", [128, G], F32)
                for half in range(2):
                    for cl in range(HC):
                        S.dma("sp", apad[:, cl, :, 0:32],
                              hd.rearrange("(c p) (j t) -> p c j t", p=128, t=32)[:, half * HC + cl, 4 * g:4 * g + 4, :],
                              ["hd"], [f"apadh{cl}"], "ld0")
                    for c2 in range(HC // 2):
                        wv, wk = wload(w_in, 0, DC, O_CV + half * HC * 128 + c2 * 256, 256)
                        wg_, wgk = wload(w_in, 0, DC, O_CG + half * HC * 128 + c2 * 256, 256)
                        for q in range(2):
                            cl = c2 * 2 + q
                            pv, pvk = pbank(0, 4)
                            pg, pgk = pbank(0, 4)
                            S.mm([(pv[:], wv[:, k, q * 128:(q + 1) * 128], hT[:, k, :]) for k in range(DC)], [wk, "hT"], [pvk])
                            S.mm([(pg[:], wg_[:, k, q * 128:(q + 1) * 128], hT[:, k, :]) for k in range(DC)], [wgk, "hT"], [pgk])
                            sg_, sgk = sgc[cl % 2], f"sgc{cl % 2}"
                            S.op("act", lambda e, pg=pg, sg_=sg_: e.activation(out=sg_[:], in_=pg[:], func=AF.Sigmoid),
                                 [pgk], [sgk])
                            S.op("dve", lambda e, pv=pv, sg_=sg_, cl=cl: e.tensor_tensor(
                                apad[:, cl, :, 32:160], sg_[:].rearrange("p (j t) -> p j t", j=4),
                                pv[:].rearrange("p (j t) -> p j t", j=4), ALU.mult), [pvk, sgk], [f"apad{cl}"])
                    for k in range(31):
                        for cl in range(HC):
                            cc = half * HC + cl
                            src = apad[:, cl, :, 2 + k:130 + k]
                            dst = cacc[:, cc, :].rearrange("p (j t) -> p j t", j=4)
                            if k == 0:
                                S.op("dve", lambda e, src=src, dst=dst, cc=cc: e.tensor_scalar(
                                    dst, src, cw[:, cc, 0:1], cb[:, cc:cc + 1], ALU.mult, ALU.add),
                                    [f"apad{cl}", f"apadh{cl}"], [f"cacc{cc}"])
                            else:
                                S.op("dve", lambda e, src=src, dst=dst, cc=cc, k=k: e.scalar_tensor_tensor(
                                    dst, src, cw[:, cc, k:k + 1], dst, ALU.mult, ALU.add),
                                    [f"apad{cl}", f"apadh{cl}", f"cacc{cc}"], [f"cacc{cc}"])
                pmean, pmk = ps[4], "ps4"
                pms, pmsk = ps[5], "ps5"
                S.mm([(pmean[:], inv[1536][:], cacc[:, cc, :]) for cc in range(CC)],
                     [f"cacc{cc}" for cc in range(CC)], [pmk])
                for cc in range(CC):
                    sq, sqk = sqbuf[cc % 2], f"sq{cc % 2}"
                    S.op("act", lambda e, sq=sq, cc=cc: e.activation(out=sq[:], in_=cacc[:, cc, :], func=AF.Square),
                         [f"cacc{cc}"], [sqk])
                    S.mm([(pms[:], ones_bf[:], sq[:])], [sqk], [pmsk], start=(cc == 0), stop=(cc == CC - 1))
                S.op("act", lambda e: e.activation(out=meansb[:], in_=pmean[:], func=AF.Copy), [pmk], ["meansb"])
                S.op("dve", lambda e: e.tensor_tensor(rstd[0][:], meansb[:], meansb[:], ALU.mult), ["meansb"], ["rstd0"])
                S.op("dve", lambda e: e.scalar_tensor_tensor(rstd[0][:], pms[:], 1.0 / 1536, rstd[0][:], ALU.mult, ALU.subtract),
                     [pmsk, "rstd0"], ["rstd0"])
                S.op("act", lambda e: e.activation(out=rstd[0][:], in_=rstd[0][:], func=AF.Sqrt, bias=epst[:, 0:1]),
                     ["rstd0"], ["rstd0"])
                S.op("dve", lambda e: e.reciprocal(rstd[0][:], rstd[0][:]), ["rstd0"], ["rstd0"])
                for cc in range(CC):
                    S.op("dve", lambda e, cc=cc: e.tensor_tensor(cacc[:, cc, :], cacc[:, cc, :], meansb[:], ALU.subtract),
                         [f"cacc{cc}", "meansb"], [f"cacc{cc}"])
                    S.op("dve", lambda e, cc=cc: e.tensor_tensor(cacc[:, cc, :], cacc[:, cc, :], rstd[0][:], ALU.mult),
                         [f"cacc{cc}", "rstd0"], [f"cacc{cc}"])
                    S.op("act", lambda e, cc=cc: e.activation(out=an[:, cc, :], in_=cacc[:, cc, :], func=AF.Silu,
                                                              bias=lnb[:, cc:cc + 1], scale=lng[:, cc:cc + 1]),
                         [f"cacc{cc}"], ["an"])
                branch_out(0, w_co, CC, an, "an")
                S.fence()

            with ExitStack() as es2:
                sb2 = lambda name, shape, dt: es2.enter_context(nc.sbuf_tensor(f"{name}_b{g}", shape, dt))
                QN = sb2("QN", [128, H, G], BF16)
                QR = sb2("QR", [64, H, G], BF16)
                oT = sb2("oT", [128, H, G], BF16)
                es3 = ExitStack()
                sb3 = lambda name, shape, dt: es3.enter_context(nc.sbuf_tensor(f"{name}_b{g}", shape, dt))
                cq = sb3("cq", [128, 4, G], F32)
                cqn = sb3("cqn", [128, 4, G], BF16)
                qnf = sb3("qnf", [128, G], F32)
                qrf = sb3("qrf", [64, G], F32)
                qrg = sb3("qrg", [64, G], F32)
                qrr = sb3("qrr", [64, G], F32)
                t1 = sb3("t1", [64, G], F32)
                t2 = sb3("t2", [64, G], F32)
                for m2 in range(2):
                    wv, wk = wload(w_in, 0, DC, O_CQ + m2 * 256, 256)
                    for q in range(2):
                        pb, pbk = pbank(0, 4)
                        S.mm([(pb[:], wv[:, k, q * 128:(q + 1) * 128], hT[:, k, :]) for k in range(DC)], [wk, "hT"], [pbk])
                        S.op("act", lambda e, pb=pb, m=m2 * 2 + q: e.activation(out=cq[:, m, :], in_=pb[:], func=AF.Copy),
                             [pbk], ["cq"])
                rmsnorm_fm(cq, "cq", 4, gqa, 512, cqn, "cqn")
                wv, wk = None, None
                for h in range(H):
                    if h % 4 == 0:
                        wv, wk = wload(w_uq, 0, 4, h * 192, 768)
                    hq = h % 4
                    pn, pnk = pbank(0, 4)
                    pr_, prk_ = pbank(0, 4)
                    S.mm([(pn[:], wv[:, k, hq * 192:hq * 192 + 128], cqn[:, k, :]) for k in range(4)], [wk, "cqn"], [pnk])
                    S.mm([(pr_[0:64, :], wv[:, k, hq * 192 + 128:hq * 192 + 192], cqn[:, k, :]) for k in range(4)],
                         [wk, "cqn"], [prk_])
                    S.op("act", lambda e, pn=pn: e.activation(out=qnf[:], in_=pn[:], func=AF.Copy), [pnk], ["qnf"])
                    S.op("act", lambda e, pr_=pr_: e.activation(out=qrf[:], in_=pr_[0:64, :], func=AF.Copy), [prk_], ["qrf"])
                    r, rk = stat_rstd([(qnf[:], 128), (qrf[:], 64)], 192, ["qnf", "qrf"])
                    S.op("dve", lambda e, r=r, h=h: e.scalar_tensor_tensor(QN[:, h, :], qnf[:], gqn[:, 0:1], r[:],
                                                                         ALU.mult, ALU.mult), ["qnf", rk], ["QN"])
                    S.op("dve", lambda e: e.tensor_scalar(qrg[:], qrf[:], gqr[:, 0:1], None, ALU.mult),
                         ["qrf"], ["qrg"])
                    rope_apply(qrg[:], "qrg", G, qrr[:], "qrr", t1[:], t2[:])
                    S.op("dve", lambda e, r=r, h=h: e.tensor_tensor(QR[:, h, :], qrr[:], r[0:64, :], ALU.mult),
                         ["qrr", rk], ["QR"])
                S.fence()
                es3.close()
                es3 = ExitStack()
                rden = sb3("rden", [128, G], F32)
                kn = [sb3(f"kn{i}", [128, 8, 128], BF16) for i in range(3)]
                kr = [sb3(f"kr{i}", [64, 8, 128], BF16) for i in range(3)]
                vv = [sb3(f"vv{i}", [128, 8, 128], BF16) for i in range(3)]
                PT = [sb3(f"PT{i}", [128, G], BF16) for i in range(3)]
                nsc = 4 * J + 4
                kvi = 0
                for h in range(H):
                    po, pok = ps[6], "ps6"
                    pd, pdk = ps[7], "ps7"
                    first = True
                    for sc in range(nsc):
                        i = kvi % 3
                        kvi += 1
                        kkey = f"kv{i}"
                        S.dma("sp", kn[i][:].rearrange("p r t -> p (r t)"),
                              kloc[h * 192:h * 192 + 128, sc * 1024:(sc + 1) * 1024], [], [kkey], f"kv{i}")
                        S.dma("sp", kr[i][:].rearrange("p r t -> p (r t)"),
                              kloc[h * 192 + 128:h * 192 + 192, sc * 1024:(sc + 1) * 1024], [], [kkey], f"kv{i}")
                        S.dma("sp", vv[i][:], vloc.rearrange("(h n s t) d -> t h n s d", h=H, s=8, t=128)[:, h, sc, :, :],
                              [], [kkey], f"kv{i}")
                        q0 = max(0, sc - 4 * J) * 128
                        N = G - q0
                        for kt in range(8):
                            pS, pSk = pbank(0, 4)
                            S.mm([(pS[:, 0:N], kn[i][:, kt, :], QN[:, h, q0:G]), (pS[:, 0:N], kr[i][:, kt, :], QR[:, h, q0:G])],
                                 [kkey, "QN", "QR"], [pSk])
                            pi_ = (kvi * 8 + kt) % 3
                            P_, Pk = PT[pi_], f"PT{pi_}"
                            S.op("act", lambda e, pS=pS, P_=P_, N=N: e.activation(out=P_[:, 0:N], in_=pS[:, 0:N], func=AF.Exp,
                                                                                 scale=MLA_SCALE), [pSk], [Pk])
                            if sc >= 4 * J:
                                S.op("dve", lambda e, P_=P_, kt=kt: e.tensor_tensor(P_[:, 0:128], P_[:, 0:128], mask[:, kt, :],
                                                                                    ALU.mult), [Pk], [Pk])
                            last = (sc == nsc - 1 and kt == 7)
                            S.mm([(po[:, q0:G], vv[i][:, kt, :], P_[:, 0:N])], [kkey, Pk], [pok], start=first, stop=last)
                            S.mm([(pd[:, q0:G], ones_bf[:], P_[:, 0:N])], [Pk], [pdk], start=first, stop=last)
                            first = False
                    S.op("dve", lambda e: e.reciprocal(rden[:], pd[:]), [pdk], ["rden"])
                    S.op("dve", lambda e, h=h: e.tensor_tensor(oT[:, h, :], po[:], rden[:], ALU.mult), [pok, "rden"], ["oT"])
                S.fence()
                es3.close()
                branch_out(1, w_mo, H, oT, "oT")
                S.fence()

            with ExitStack() as es2:
                sb2 = lambda name, shape, dt: es2.enter_context(nc.sbuf_tensor(f"{name}_b{g}", shape, dt))
                mq = sb2("mq", [128, 8, G], F32)
                mqn = sb2("mqn", [128, 8, G], BF16)
                om = sb2("om", [128, 8, G], BF16)
                PM = [sb2(f"PM{i}", [128, G], BF16) for i in range(2)]
                rden = sb2("rdenm", [128, G], F32)
                for m2 in range(4):
                    wv, wk = wload(w_in, 0, DC, O_MQ + m2 * 256, 256)
                    for q in range(2):
                        pb, pbk = pbank(0, 4)
                        S.mm([(pb[:], wv[:, k, q * 128:(q + 1) * 128], hT[:, k, :]) for k in range(DC)], [wk, "hT"], [pbk])
                        S.op("act", lambda e, pb=pb, m=m2 * 2 + q: e.activation(out=mq[:, m, :], in_=pb[:], func=AF.Copy),
                             [pbk], ["mq"])
                for hh in range(4):
                    r, rk = stat_rstd([(mq[:, 2 * hh + q, :], 128) for q in range(2)], 256, ["mq"])
                    for q in range(2):
                        S.op("dve", lambda e, hh=hh, q=q, r=r: e.scalar_tensor_tensor(
                            mqn[:, 2 * hh + q, :], mq[:, 2 * hh + q, :], gmq[:, q:q + 1], r[:], ALU.mult, ALU.mult),
                            ["mq", rk], ["mqn"])
                    for kt in range(2):
                        pS, pSk = pbank(0, 4)
                        S.mm([(pS[:], mkT[:, 2 * hh + q, kt * 128:(kt + 1) * 128], mqn[:, 2 * hh + q, :]) for q in range(2)],
                             ["mkT", "mqn"], [pSk])
                        S.op("act", lambda e, pS=pS, kt=kt: e.activation(out=PM[kt][:], in_=pS[:], func=AF.Exp, scale=MEM_SCALE),
                             [pSk], [f"PM{kt}"])
                    pd, pdk = ps[7], "ps7"
                    S.mm([(pd[:], ones_bf[:], PM[kt][:]) for kt in range(2)], ["PM0", "PM1"], [pdk])
                    S.op("dve", lambda e: e.reciprocal(rden[:], pd[:]), [pdk], ["rdenm"])
                    for e_ in range(2):
                        po, pok = ps[6], "ps6"
                        S.mm([(po[:], mv[:, kt, hh * 256 + e_ * 128:hh * 256 + (e_ + 1) * 128], PM[kt][:]) for kt in range(2)],
                             ["PM0", "PM1", "mv"], [pok])
                        S.op("dve", lambda e, hh=hh, e_=e_: e.tensor_tensor(om[:, 2 * hh + e_, :], po[:], rden[:], ALU.mult),
                             [pok, "rdenm"], ["om"])
                branch_out(2, w_meo, 8, om, "om")
                S.fence()

            for m in range(DC):
                S.op("act", lambda e, m=m: e.activation(out=mT[:, m, :], in_=merged[:, m, :], func=AF.Copy), ["merged"], ["mT"])
            S.fence()
        with ExitStack() as es:
            sbt = lambda name, shape, dt: es.enter_context(nc.sbuf_tensor(f"{name}_c{g}", shape, dt))
            x1T = sbt("x1T", [128, DC, G], F32)
            hff = sbt("hff", [128, FC, G], BF16)
            otok = sbt("otok", [128, 2, D], F32)
            S.dma("sp", x1T[:], x1d[:, :, g * G:(g + 1) * G], [], ["x1T"], "ld0")
            for m2 in range(DC // 2):
                wv, wk = wload(w_out, 0, DC, m2 * 256, 256)
                for q in range(2):
                    m = m2 * 2 + q
                    pb, pbk = pbank(0, 4)
                    S.mm([(pb[:], wv[:, k, q * 128:(q + 1) * 128], mT[:, k, :]) for k in range(DC)], [wk, "mT"], [pbk])
                    S.op("dve", lambda e, pb=pb, m=m: e.tensor_tensor(x1T[:, m, :], x1T[:, m, :], pb[:], ALU.add),
                         [pbk, "x1T"], ["x1T"])
            rmsnorm_fm(x1T, "x1T", DC, g2, 2048, hT, "hT")
            ffn(x1T, "x1T", hT, "hT", hff, w2gu, w2d)
            for j in range(4):
                for c4 in range(4):
                    pb, pbk = pbank(0, 4)
                    S.tr([(pb[:, q * 128:(q + 1) * 128], x1T[:, c4 * 4 + q, j * 128:(j + 1) * 128]) for q in range(4)],
                         ident[:], ["x1T"], [pbk])
                    S.op("act" if c4 % 2 else "dve",
                         (lambda e, pb=pb, j=j, c4=c4: e.activation(out=otok[:, j % 2, c4 * 512:(c4 + 1) * 512], in_=pb[:],
                                                                    func=AF.Copy)) if c4 % 2 else
                         (lambda e, pb=pb, j=j, c4=c4: e.tensor_copy(otok[:, j % 2, c4 * 512:(c4 + 1) * 512], pb[:])),
                         [pbk], [f"otok{j % 2}"])
                S.dma("sp", out_d[4 * g + j], otok[:, j % 2, :], [f"otok{j % 2}"], [f"outd.{g}.{j}"], f"sto{j % 2}")
            S.fence()
    S.final()
    print('KERNEL STATS', S.cnt, 'waits', S.nwait, 'dma', {k: v[1] // 16 for k, v in S.dsem.items()}, flush=True)
    return nc


_CACHE = {}


def _fm(v, C):
    return np.ascontiguousarray(np.asarray(v, np.float32).reshape(C, 128).T)


def kernel(**inp):
    f32 = np.float32
    x = np.asarray(inp["x"], f32)[0]
    mem = np.ascontiguousarray(np.asarray(inp["mem"], f32)[0])
    pos = np.asarray(inp["positions"])[0].astype(np.int32)
    sq = lambda k: np.ascontiguousarray(np.asarray(inp[k], f32)[0])
    common = {
        "mem": mem,
        "g_ffn1": _fm(sq("g_ffn1"), DC), "g_mix": _fm(sq("g_mix"), DC), "g_ffn2": _fm(sq("g_ffn2"), DC),
        "g_mem": _fm(sq("g_mem"), DC), "b_gate": _fm(sq("b_gate"), 48),
        "conv_w": np.ascontiguousarray(sq("conv_w").T.reshape(CC, 128, 31).transpose(1, 0, 2)),
        "conv_b": _fm(sq("conv_b"), CC), "conv_ln_g": _fm(sq("conv_ln_g"), CC), "conv_ln_b": _fm(sq("conv_ln_b"), CC),
        "g_q_a": _fm(sq("g_q_a"), 4), "g_kv_a": _fm(sq("g_kv_a"), 4),
        "gqn": sq("g_qnorm")[0:128].reshape(128, 1).copy(), "gqr": sq("g_qnorm")[128:192].reshape(64, 1).copy(),
        "gkn": sq("g_knorm")[0:128].reshape(128, 1).copy(), "gkr": sq("g_knorm")[128:192].reshape(64, 1).copy(),
        "g_mqnorm": _fm(sq("g_mqnorm"), 2), "g_mknorm": _fm(sq("g_mknorm"), 2),
        "ident": np.eye(128, dtype=f32),
        "w_ffn1_gu": sq("w_ffn1_gu"), "w_ffn1_down": sq("w_ffn1_down"), "w_in": sq("w_in"),
        "w_conv_out": sq("w_conv_out"), "w_uq": sq("w_uq"), "w_ukv": sq("w_ukv"), "w_mla_out": sq("w_mla_out"),
        "w_mem_kv": sq("w_mem_kv"), "w_mem_out": sq("w_mem_out"), "w_out": sq("w_out"),
        "w_ffn2_gu": sq("w_ffn2_gu"), "w_ffn2_down": sq("w_ffn2_down"),
    }
    rot = np.zeros((64, 64), f32)
    for i in range(32):
        rot[i + 32, i] = 1.0
        rot[i, i + 32] = -1.0
    common["rot"] = rot
    invf = (10000.0 ** (-np.arange(0, 64, 2, dtype=f32) / f32(64))).astype(f32)
    common["invf"] = np.concatenate([invf, invf]).reshape(64, 1).astype(f32)
    tri = (np.arange(128)[:, None] <= np.arange(128)[None, :]).astype(f32)
    in_maps = []
    xg = x.reshape(NT, NCORE, 128, D)
    pg = pos.reshape(NT, NCORE, 128)
    for c in range(NCORE):
        m = dict(common)
        order = [(c + s_) % NCORE for s_ in range(NCORE)]
        m["x"] = np.ascontiguousarray(xg[:, order]).reshape(NTA, 128, D)
        pl = np.ascontiguousarray(pg[:, order]).reshape(1, TA)
        m["pos"] = np.ascontiguousarray(np.broadcast_to(pl, (64, TA))).astype(np.int32)
        mk = np.zeros((128, 8, 128), f32)
        mk[:, 0, :] = tri
        if c >= 1:
            mk[:, 8 - c:, :] = 1.0
        m["mask"] = mk
        sel = np.zeros((128, 8), f32)
        sel[:, 0] = 1.0 if c >= 1 else 0.0
        sel[:, 1] = 1.0 if c == 0 else 0.0
        m["sel"] = sel
        in_maps.append(m)
    if "nc" not in _CACHE:
        _CACHE["nc"] = build()
    res = run_bass_kernel_spmd(_CACHE["nc"], in_maps, core_ids=list(range(NCORE)))
    out = np.empty((NT, NCORE, 128, D), f32)
    for c in range(NCORE):
        out[:, c] = np.asarray(res.results[c]["out"], f32).reshape(NT, 128, D)
    return out.reshape(1, NT * NCORE * 128, D)
```

```python
import numpy as np
import ml_dtypes
from contextlib import ExitStack
import concourse.bass as bass
import concourse.mybir as mybir
from concourse.bass_utils import run_bass_kernel_spmd

F32 = mybir.dt.float32
BF16 = mybir.dt.bfloat16
I32 = mybir.dt.int32
AF = mybir.ActivationFunctionType
ALU = mybir.AluOpType

NCORE = 8
D = 2048
DC = 16
DFF = 5632
FC = 44
T = 2048
G = 512
NG = 4
NT = 16
NTA = 128
TA = NTA * 128
NGA = TA // 512
CCH = 1536
CC = 12
H = 12
EPS = 1e-6
O_CV, O_CG, O_CQ, O_CKV, O_KPE, O_MQ, O_GATE = 0, 1536, 3072, 3584, 4096, 4160, 5184
MLA_SCALE = 192 ** -0.5
MEM_SCALE = 256 ** -0.5
TWO_PI = 2.0 * np.pi
STAGE = "full"


class Sched:
    def __init__(s, nc):
        s.nc = nc
        s.E = dict(pe=nc.tensor, act=nc.scalar, dve=nc.vector, pool=nc.gpsimd, sp=nc.sync)
        s.csem = {e: nc.alloc_semaphore("c_" + e) for e in ("pe", "act", "dve")}
        s.cnt = {e: 0 for e in s.csem}
        s.dsem = {}
        s.waited = {e: {} for e in s.E}
        s.res = {}
        s.nwait = 0

    def _wait(s, eng, evs):
        best = {}
        for ev in evs:
            if ev is None:
                continue
            sem, val = ev
            k = id(sem)
            if k not in best or best[k][1] < val:
                best[k] = ev
        for k, (sem, val) in best.items():
            if eng == "pe" and sem is s.csem["pe"]:
                continue
            if s.waited[eng].get(k, 0) >= val:
                continue
            s.E[eng].wait_ge(sem, val)
            s.nwait += 1
            s.waited[eng][k] = val

    def _deps(s, reads, writes):
        evs = []
        for k in reads:
            r = s.res.get(k)
            if r:
                evs.append(r[0])
                if k.startswith("ps"):
                    evs.extend(r[1].values())
        for k in writes:
            r = s.res.get(k)
            if r:
                evs.append(r[0])
                evs.extend(r[1].values())
        return evs

    def _commit(s, ev, reads, writes):
        kk = id(ev[0])
        for k in reads:
            r = s.res.setdefault(k, [None, {}])
            if kk not in r[1] or r[1][kk][1] < ev[1]:
                r[1][kk] = ev
        for k in writes:
            s.res[k] = [ev, {}]

    def op(s, eng, fn, reads=(), writes=()):
        s._wait(eng, s._deps(reads, writes))
        ins = fn(s.E[eng])
        s.cnt[eng] += 1
        ins.then_inc(s.csem[eng], 1)
        s._commit((s.csem[eng], s.cnt[eng]), reads, writes)

    def mm(s, mms, reads, writes, start=True, stop=True):
        s._wait("pe", s._deps(reads, writes))
        pe = s.E["pe"]
        n = len(mms)
        ins = None
        for i, (o, l, r) in enumerate(mms):
            ins = pe.matmul(o, l, r, start=(start and i == 0), stop=(stop and i == n - 1))
        s.cnt["pe"] += 1
        ins.then_inc(s.csem["pe"], 1)
        s._commit((s.csem["pe"], s.cnt["pe"]), reads, writes)

    def tr(s, outs_ins, ident, reads, writes):
        s._wait("pe", s._deps(reads, writes))
        pe = s.E["pe"]
        ins = None
        for o, i in outs_ins:
            ins = pe.transpose(o, i, ident)
        s.cnt["pe"] += 1
        ins.then_inc(s.csem["pe"], 1)
        s._commit((s.csem["pe"], s.cnt["pe"]), reads, writes)

    def dma(s, q, out, in_, reads, writes, slot, **kw):
        s._wait(q, s._deps(reads, writes))
        if slot not in s.dsem:
            s.dsem[slot] = [s.nc.alloc_semaphore("d_" + slot), 0]
        d = s.dsem[slot]
        s.E[q].dma_start(out=out, in_=in_, **kw).then_inc(d[0], 16)
        d[1] += 16
        s._commit((d[0], d[1]), reads, writes)

    def all_events(s):
        evs = [(s.csem[e], s.cnt[e]) for e in s.csem if s.cnt[e] > 0]
        evs += [(d[0], d[1]) for d in s.dsem.values() if d[1] > 0]
        return evs

    def fence(s, engs=("pe", "act", "dve", "sp")):
        evs = s.all_events()
        for e in engs:
            s._wait(e, evs)

    def final(s):
        evs = s.all_events()
        for e in ("sp", "pool", "act", "dve", "pe"):
            s._wait(e, evs)


def build():
    nc = bass.Bass("TRN2", target_bir_lowering=False)
    S = Sched(nc)

    def din(name, shape, dt=F32):
        return nc.dram_tensor(name, list(shape), dt, kind="ExternalInput").ap()

    x_d = din("x", [NTA, 128, D])
    mem_d = din("mem", [256, D])
    pos_d = din("pos", [64, TA], I32)
    g1_d = din("g_ffn1", [128, DC])
    gm_d = din("g_mix", [128, DC])
    g2_d = din("g_ffn2", [128, DC])
    gmem_d = din("g_mem", [128, DC])
    bg_d = din("b_gate", [128, 48])
    cw_d = din("conv_w", [128, CC, 31])
    cb_d = din("conv_b", [128, CC])
    lng_d = din("conv_ln_g", [128, CC])
    lnb_d = din("conv_ln_b", [128, CC])
    gqa_d = din("g_q_a", [128, 4])
    gkva_d = din("g_kv_a", [128, 4])
    gqn_d = din("gqn", [128, 1])
    gqr_d = din("gqr", [64, 1])
    gkn_d = din("gkn", [128, 1])
    gkr_d = din("gkr", [64, 1])
    gmq_d = din("g_mqnorm", [128, 2])
    gmk_d = din("g_mknorm", [128, 2])
    ident_d = din("ident", [128, 128])
    rot_d = din("rot", [64, 64])
    mask_d = din("mask", [128, 8, 128])
    sel_d = din("sel", [128, 8])
    invf_d = din("invf", [64, 1])
    w1gu = din("w_ffn1_gu", [D, 2 * DFF])
    w1d = din("w_ffn1_down", [DFF, D])
    w_in = din("w_in", [D, 11328])
    w_co = din("w_conv_out", [CCH, D])
    w_uq = din("w_uq", [512, H * 192])
    w_ukv = din("w_ukv", [512, H * 256])
    w_mo = din("w_mla_out", [H * 128, D])
    w_mkv = din("w_mem_kv", [D, 2048])
    w_meo = din("w_mem_out", [1024, D])
    w_out = din("w_out", [D, D])
    w2gu = din("w_ffn2_gu", [D, 2 * DFF])
    w2d = din("w_ffn2_down", [DFF, D])
    out_d = nc.dram_tensor("out", [NT, 128, D], F32, kind="ExternalOutput").ap()

    x1d = nc.dram_tensor("x1d", [128, DC, T], F32).ap()
    kloc = nc.dram_tensor("kloc", [H * 192, TA], BF16).ap()
    vloc = nc.dram_tensor("vloc", [H * TA, 128], BF16).ap()
    tloc = nc.dram_tensor("tloc", [CCH, 512], F32).ap()
    hd = nc.dram_tensor("hd", [CCH, 512], F32).ap()
    wbf_gu = nc.dram_tensor("wbf_gu", [2 * (FC // 2), 128, DC * 256], BF16).ap()
    wbf_d = nc.dram_tensor("wbf_d", [DC, 128, FC * 128], BF16).ap()

    sb = nc.alloc_sbuf_tensor
    ident = sb("ident_s", [128, 128], F32)
    rot = sb("rot_s", [64, 64], F32)
    mask = sb("mask_s", [128, 8, 128], F32)
    sel = sb("sel_s", [128, 8], F32)
    invf = sb("invf_s", [64, 1], F32)
    g1 = sb("g1_s", [128, DC], F32)
    gm = sb("gm_s", [128, DC], F32)
    g2 = sb("g2_s", [128, DC], F32)
    gmem = sb("gmem_s", [128, DC], F32)
    bg = sb("bg_s", [128, 48], F32)
    cw = sb("cw_s", [128, CC, 31], F32)
    cb = sb("cb_s", [128, CC], F32)
    lng = sb("lng_s", [128, CC], F32)
    lnb = sb("lnb_s", [128, CC], F32)
    gqa = sb("gqa_s", [128, 4], F32)
    gkva = sb("gkva_s", [128, 4], F32)
    gqn = sb("gqn_s", [128, 1], F32)
    gqr = sb("gqr_s", [64, 1], F32)
    gkn = sb("gkn_s", [128, 1], F32)
    gkr = sb("gkr_s", [64, 1], F32)
    gmq = sb("gmq_s", [128, 2], F32)
    gmk = sb("gmk_s", [128, 2], F32)
    ones_bf = sb("ones_bf", [128, 128], BF16)
    epst = sb("epst", [128, 1], F32)
    inv = {n: sb(f"inv{n}", [128, 128], F32) for n in (2048, 512, 192, 256, 1536)}
    mkT = sb("mkT", [128, 8, 256], BF16)
    mv = sb("mv", [128, 2, 1024], BF16)
    csg = sb("csg", [64, 2, G], F32)
    cs_d = nc.dram_tensor("cs_d", [64, 2, TA], F32).ap()
    rstd = [sb(f"rstd{i}", [128, G], F32) for i in range(2)]
    wring = [sb(f"wring{i}", [128, 5632], BF16) for i in range(3)]
    ps = [nc.alloc_psum_tensor(f"ps{i}", [128, G], F32) for i in range(8)]

    consts = [(ident, ident_d), (rot, rot_d), (mask, mask_d), (sel, sel_d), (invf, invf_d), (g1, g1_d),
              (gm, gm_d), (g2, g2_d), (gmem, gmem_d), (bg, bg_d), (cw, cw_d), (cb, cb_d), (lng, lng_d),
              (lnb, lnb_d), (gqa, gqa_d), (gkva, gkva_d), (gqn, gqn_d), (gqr, gqr_d), (gkn, gkn_d),
              (gkr, gkr_d), (gmq, gmq_d), (gmk, gmk_d)]
    for ci, (t, d_) in enumerate(consts):
        S.dma("sp", t[:], d_, [], [f"const{ci}"], "const")
    S.op("dve", lambda e: e.memset(ones_bf[:], 1.0), [], ["constA"])
    S.op("dve", lambda e: e.memset(epst[:], EPS), [], ["constB"])
    for n, t in inv.items():
        S.op("dve", lambda e, t=t, n=n: e.memset(t[:], 1.0 / n), [], [f"constI{n}"])
    S.fence(("pe", "act", "dve", "sp", "pool"))
    CONST = []

    wstate = {"i": 0, "n": 3}

    def wload(w, r0, KC, c0, ncols):
        i = wstate["i"] % wstate["n"]
        wstate["i"] += 1
        view = wring[i][:, 0:KC * ncols].rearrange("p (k n) -> p k n", k=KC)
        src = w[r0:r0 + KC * 128, c0:c0 + ncols].rearrange("(k p) n -> p k n", p=128)
        S.dma("pool", view, src, [], [f"w{i}"], f"w{i}")
        return view, f"w{i}"

    def wload_bf(src, KC, ncols, rkey):
        i = wstate["i"] % wstate["n"]
        wstate["i"] += 1
        S.dma("pool", wring[i][:, 0:KC * ncols], src, [rkey], [f"w{i}"], f"w{i}")
        return wring[i][:, 0:KC * ncols].rearrange("p (k n) -> p k n", k=KC), f"w{i}"

    pstate = {}

    def pbank(lo=0, n=4):
        c = pstate.get((lo, n), 0)
        pstate[(lo, n)] = c + 1
        i = lo + c % n
        return ps[i], f"ps{i}"

    rs = {"i": 0}

    def stat_rstd(parts, n, extra_reads, eps=EPS):
        i = rs["i"] % 2
        rs["i"] += 1
        N = parts[0][0].shape[-1]
        sbank, skey = ps[4 + i], f"ps{4 + i}"
        for j, (ap, P) in enumerate(parts):
            sq, sqk = sqbuf[j % 2], f"sq{j % 2}"
            S.op("act", lambda e, ap=ap, P=P, sq=sq: e.activation(out=sq[0:P, 0:N], in_=ap, func=AF.Square),
                 extra_reads, [sqk])
            S.mm([(sbank[:, 0:N], ones_bf[0:P, :], sq[0:P, 0:N])], [sqk], [skey],
                 start=(j == 0), stop=(j == len(parts) - 1))
        r, rk = rstd[i], f"rstd{i}"
        S.op("act", lambda e: e.activation(out=r[:, 0:N], in_=sbank[:, 0:N], func=AF.Sqrt, bias=epst[:, 0:1],
                                           scale=1.0 / n), [skey], [rk])
        S.op("dve", lambda e: e.reciprocal(r[:, 0:N], r[:, 0:N]), [rk], [rk])
        return r, rk

    sqbuf = [sb(f"sqb{i}", [128, G], BF16) for i in range(2)]

    def rmsnorm_fm(src, skey, C, gain, n, out, okey, N=G):
        r, rk = stat_rstd([(src[:, c, :], 128) for c in range(C)], n, [skey])
        for c in range(C):
            S.op("dve", lambda e, c=c: e.scalar_tensor_tensor(out[:, c, :], src[:, c, :], gain[:, c:c + 1],
                                                             r[:, 0:N], ALU.mult, ALU.mult),
                 [skey, rk], [okey])

    def ffn(xT, xkey, xn, xnkey, hff, wgu, wdn, pre=None):
        for m2 in range(FC // 2):
            if pre is None:
                wg, wgk = wload(wgu, 0, DC, m2 * 256, 256)
                wu, wuk = wload(wgu, 0, DC, DFF + m2 * 256, 256)
            else:
                wg, wgk = wload_bf(pre[0][2 * m2], DC, 256, f"wbfg{2 * m2}")
                wu, wuk = wload_bf(pre[0][2 * m2 + 1], DC, 256, f"wbfg{2 * m2 + 1}")
            for mm_ in range(2):
                m = m2 * 2 + mm_
                pg, pgk = pbank(0, 4)
                pu, puk = pbank(0, 4)
                S.mm([(pg[:], wg[:, k, mm_ * 128:(mm_ + 1) * 128], xn[:, k, :]) for k in range(DC)],
                     [wgk, xnkey], [pgk])
                S.mm([(pu[:], wu[:, k, mm_ * 128:(mm_ + 1) * 128], xn[:, k, :]) for k in range(DC)],
                     [wuk, xnkey], [puk])
                sg, sgk = sgbuf[m % 2], f"sg{m % 2}"
                S.op("act", lambda e, pg=pg, sg=sg: e.activation(out=sg[:], in_=pg[:], func=AF.Silu), [pgk], [sgk])
                S.op("dve", lambda e, pu=pu, sg=sg, m=m: e.tensor_tensor(hff[:, m, :], sg[:], pu[:], ALU.mult),
                     [sgk, puk], ["hff"])
        for mo in range(DC):
            if pre is None:
                wd, wdk = wload(wdn, 0, FC, mo * 128, 128)
            else:
                wd, wdk = wload_bf(pre[1][mo], FC, 128, f"wbfd{mo}")
            pa, pak = pbank(6, 2)
            S.mm([(pa[:], wd[:, k, :], hff[:, k, :]) for k in range(FC)], [wdk, "hff"], [pak])
            S.op("dve", lambda e, pa=pa, mo=mo: e.scalar_tensor_tensor(xT[:, mo, :], pa[:], 0.5, xT[:, mo, :],
                                                                      ALU.mult, ALU.add),
                 [pak, xkey], [xkey])

    sgbuf = [sb(f"sgb{i}", [128, G], F32) for i in range(2)]

    with ExitStack() as es:
        posi = es.enter_context(nc.sbuf_tensor("posi", [64, T], I32))
        ang = es.enter_context(nc.sbuf_tensor("ang", [64, T], F32))
        ang2 = es.enter_context(nc.sbuf_tensor("ang2", [64, T], F32))
        sinn = es.enter_context(nc.sbuf_tensor("sinn", [64, T], F32))
        cosn = es.enter_context(nc.sbuf_tensor("cosn", [64, T], F32))
        for ch in range(TA // T):
            S.dma("sp", posi[:], pos_d[:, ch * T:(ch + 1) * T], [], ["posi"], "const")
            S.op("dve", lambda e: e.tensor_copy(ang[:], posi[:]), ["posi"], ["ang"])
            S.op("dve", lambda e: e.tensor_scalar(ang[:], ang[:], invf[:, 0:1], None, ALU.mult), ["ang"], ["ang"])
            for tgt, tk, shift in ((sinn, "sinn", 0.0), (cosn, "cosn", np.pi / 2)):
                S.op("dve", lambda e, shift=shift: e.tensor_scalar(ang2[:], ang[:], shift, 1.0 / TWO_PI, ALU.add, ALU.mult),
                     ["ang"], ["ang2"])
                S.op("dve", lambda e: e.tensor_copy(posi[:], ang2[:]), ["ang2"], ["posi"])
                S.op("dve", lambda e: e.tensor_copy(ang2[:], posi[:]), ["posi"], ["ang2"])
                S.op("dve", lambda e, tgt=tgt: e.scalar_tensor_tensor(tgt[:], ang2[:], -TWO_PI, ang[:], ALU.mult, ALU.add),
                     ["ang2", "ang"], [tk])
                if shift != 0.0:
                    S.op("dve", lambda e, tgt=tgt, shift=shift: e.tensor_scalar(tgt[:], tgt[:], shift, None, ALU.add),
                         [tk], [tk])
                S.op("act", lambda e, tgt=tgt: e.activation(out=tgt[:], in_=tgt[:], func=AF.Sin, scale=0.999999),
                     [tk], [tk])
            S.dma("sp", cs_d[:, 0, ch * T:(ch + 1) * T], sinn[:], ["sinn"], [f"cs_d{ch}"], "st0")
            S.dma("sp", cs_d[:, 1, ch * T:(ch + 1) * T], cosn[:], ["cosn"], [f"cs_d{ch}"], "st0")
        S.fence()

    def load_cs(g):
        S.dma("sp", csg[:], cs_d[:, :, g * G:(g + 1) * G], [], ["csg"], "ld1")

    def load_cs_own(g):
        for i in range(2):
            S.dma("sp", csg[:, i, :].rearrange("p (j t) -> p j t", j=4),
                  cs_d[:, i, :].rearrange("p (j s t) -> p j s t", s=8, t=128)[:, 4 * g:4 * g + 4, 0, :],
                  [], ["csg"], "ld1")

    def rope_apply(src, skey, N, outf, okey, tmp1, tmp2):
        pr, prk = pbank(0, 4)
        S.mm([(pr[0:64, 0:N], rot[:], src)], [skey], [prk])
        S.op("dve", lambda e: e.tensor_tensor(tmp1, pr[0:64, 0:N], csg[:, 0, 0:N], ALU.mult),
             [prk, "csg"], [okey + "t1"])
        S.op("dve", lambda e: e.tensor_tensor(tmp2, src, csg[:, 1, 0:N], ALU.mult),
             [skey, "csg"], [okey + "t2"])
        S.op("dve", lambda e: e.tensor_tensor(outf, tmp2, tmp1, ALU.subtract), [okey + "t1", okey + "t2"], [okey])

    with ExitStack() as es:
        memtok = es.enter_context(nc.sbuf_tensor("memtok", [128, 2, D], F32))
        memT = es.enter_context(nc.sbuf_tensor("memT", [128, DC, 256], F32))
        memn = es.enter_context(nc.sbuf_tensor("memn", [128, DC, 256], BF16))
        mkf = es.enter_context(nc.sbuf_tensor("mkf", [128, 8, 256], F32))
        S.dma("sp", memtok[:], mem_d.rearrange("(j p) d -> p j d", p=128), [], ["memtok"], "ld0")
        for j in range(2):
            for c4 in range(4):
                pb, pbk = pbank(0, 4)
                S.tr([(pb[:, q * 128:(q + 1) * 128], memtok[:, j, (c4 * 4 + q) * 128:(c4 * 4 + q + 1) * 128])
                      for q in range(4)], ident[:], ["memtok"], [pbk])
                S.op("act", lambda e, pb=pb, j=j, c4=c4: e.activation(
                    out=memT[:, c4 * 4:c4 * 4 + 4, j * 128:(j + 1) * 128],
                    in_=pb[:].rearrange("p (q t) -> p q t", q=4), func=AF.Copy), [pbk], ["memT"])
        rmsnorm_fm(memT, "memT", DC, gmem, 2048, memn, "memn", N=256)
        for m2 in range(4):
            wv, wk = wload(w_mkv, 0, DC, m2 * 256, 256)
            for q in range(2):
                m = m2 * 2 + q
                pb, pbk = pbank(0, 4)
                S.mm([(pb[:, 0:256], wv[:, k, q * 128:(q + 1) * 128], memn[:, k, :]) for k in range(DC)],
                     [wk, "memn"], [pbk])
                S.op("act", lambda e, pb=pb, m=m: e.activation(out=mkf[:, m, :], in_=pb[:, 0:256], func=AF.Copy),
                     [pbk], ["mkf"])
        for hh in range(4):
            r, rk = stat_rstd([(mkf[:, 2 * hh + q, :], 128) for q in range(2)], 256, ["mkf"])
            for q in range(2):
                S.op("dve", lambda e, hh=hh, q=q, r=r: e.scalar_tensor_tensor(
                    mkT[:, 2 * hh + q, :], mkf[:, 2 * hh + q, :], gmk[:, q:q + 1], r[:, 0:256], ALU.mult, ALU.mult),
                    ["mkf", rk], ["mkT"])
        for m2 in range(4):
            wv, wk = wload(w_mkv, 0, DC, 1024 + m2 * 256, 256)
            for j in range(2):
                pb, pbk = pbank(0, 4)
                S.mm([(pb[:, 0:256], memn[:, k, j * 128:(j + 1) * 128], wv[:, k, :]) for k in range(DC)],
                     [wk, "memn"], [pbk])
                S.op("act", lambda e, pb=pb, j=j, m2=m2: e.activation(
                    out=mv[:, j, m2 * 256:(m2 + 1) * 256], in_=pb[:, 0:256], func=AF.Copy), [pbk], ["mv"])
        S.fence()

    esA = ExitStack()
    xtl = esA.enter_context(nc.sbuf_tensor("xtl", [128, DC, 512], BF16))
    for m2 in range(FC // 2):
        for gu in range(2):
            b_ = 2 * m2 + gu
            wv, wk = wload(w1gu, 0, DC, gu * DFF + m2 * 256, 256)
            S.dma("sp", wbf_gu[b_], wv.rearrange("p k n -> p (k n)"), [wk], [f"wbfg{b_}"], f"wst{b_ % 3}")
    for mo in range(DC):
        wv, wk = wload(w1d, 0, FC, mo * 128, 128)
        S.dma("sp", wbf_d[mo], wv.rearrange("p k n -> p (k n)"), [wk], [f"wbfd{mo}"], f"wst{mo % 3}")
    S.fence(("pe", "act", "dve", "sp", "pool"))
    if STAGE == "P":
        S.final()
        return nc
    for g in range(NGA if STAGE != "A1" else 2):
        jsup, own = g // 2, (g % 2 == 0)
        with ExitStack() as es:
            sbt = lambda name, shape, dt: es.enter_context(nc.sbuf_tensor(f"{name}_a{g}", shape, dt))
            xT = sbt("xT", [128, DC, G], F32)
            xn = sbt("xn", [128, DC, G], BF16)
            load_cs(g)
            with ExitStack() as es0:
                xtok = es0.enter_context(nc.sbuf_tensor(f"xtok_a{g}", [128, 2, D], F32))
                for j in range(4):
                    S.dma("sp", xtok[:, j % 2, :], x_d[4 * g + j], [], [f"xtok{j % 2}"], f"ld{j % 2}")
                    for c4 in range(4):
                        pb, pbk = pbank(0, 4)
                        S.tr([(pb[:, q * 128:(q + 1) * 128], xtok[:, j % 2, (c4 * 4 + q) * 128:(c4 * 4 + q + 1) * 128])
                              for q in range(4)], ident[:], [f"xtok{j % 2}"], [pbk])
                        if c4 % 2:
                            S.op("act", lambda e, pb=pb, j=j, c4=c4: e.activation(
                                out=xT[:, c4 * 4:c4 * 4 + 4, j * 128:(j + 1) * 128],
                                in_=pb[:].rearrange("p (q t) -> p q t", q=4), func=AF.Copy), [pbk], ["xT"])
                        else:
                            S.op("dve", lambda e, pb=pb, j=j, c4=c4: e.tensor_copy(
                                xT[:, c4 * 4:c4 * 4 + 4, j * 128:(j + 1) * 128],
                                pb[:].rearrange("p (q t) -> p q t", q=4)), [pbk], ["xT"])
                S.fence()
            rmsnorm_fm(xT, "xT", DC, g1, 2048, xn, "xn")
            with ExitStack() as es0:
                hff = es0.enter_context(nc.sbuf_tensor(f"hff_a{g}", [128, FC, G], BF16))
                ffn(xT, "xT", xn, "xn", hff, w1gu, w1d, pre=(wbf_gu, wbf_d))
                S.fence()
            if own:
                S.dma("sp", x1d[:, :, jsup * 128:(jsup + 1) * 128], xT[:, :, 0:128], ["xT"], [f"x1d{jsup}"], "st0")
            rmsnorm_fm(xT, "xT", DC, gm, 2048, xn, "xn")
            with ExitStack() as es2:
                sb2 = lambda name, shape, dt: es2.enter_context(nc.sbuf_tensor(f"{name}_a{g}", shape, dt))
                ckv = sb2("ckv", [128, 4, G], F32)
                ckvn = sb2("ckvn", [128, 4, G], BF16)
                kpe = sb2("kpe", [64, G], F32)
                kper = sb2("kper", [64, G], F32)
                t1 = sb2("t1", [64, G], F32)
                t2 = sb2("t2", [64, G], F32)
                knf = sb2("knf", [128, 2, G], F32)
                kout = sb2("kout", [128, 2, G], BF16)
                krout = sb2("krout", [64, 2, G], BF16)
                vsb = sb2("vsb", [128, 4, H, 128], BF16)
                tail = sb2("tail", [128, CC, 128], F32)
                sgt = sb2("sgt", [128, 2, 128], F32)
                for m2 in range(2):
                    wv, wk = wload(w_in, 0, DC, O_CKV + m2 * 256, 256)
                    for q in range(2):
                        pb, pbk = pbank(0, 4)
                        S.mm([(pb[:], wv[:, k, q * 128:(q + 1) * 128], xn[:, k, :]) for k in range(DC)],
                             [wk, "xn"], [pbk])
                        S.op("act", lambda e, pb=pb, m=m2 * 2 + q: e.activation(out=ckv[:, m, :], in_=pb[:],
                                                                               func=AF.Copy), [pbk], ["ckv"])
                wv, wk = wload(w_in, 0, DC, O_KPE, 64)
                pb, pbk = pbank(0, 4)
                S.mm([(pb[0:64, :], wv[:, k, :], xn[:, k, :]) for k in range(DC)], [wk, "xn"], [pbk])
                kpesq = sb2("kpesq", [64, G], BF16)
                S.op("act", lambda e: e.activation(out=kpesq[:], in_=pb[0:64, :], func=AF.Square), [pbk], ["kpesq2"])
                S.op("dve", lambda e: e.tensor_scalar(kpe[:], pb[0:64, :], gkr[:, 0:1], None, ALU.mult),
                     [pbk], ["kpe"])
                rope_apply(kpe[:], "kpe", G, kper[:], "kper", t1[:], t2[:])
                rmsnorm_fm(ckv, "ckv", 4, gkva, 512, ckvn, "ckvn")
                wv, wk = None, None
                for h in range(H):
                    if h % 3 == 0:
                        wv, wk = wload(w_ukv, 0, 4, h * 256, 768)
                    hq = h % 3
                    pb, pbk = pbank(0, 4)
                    S.mm([(pb[:], wv[:, k, hq * 256:hq * 256 + 128], ckvn[:, k, :]) for k in range(4)],
                         [wk, "ckvn"], [pbk])
                    knh, knk = knf[:, h % 2, :], f"knf{h % 2}"
                    S.op("act", lambda e, pb=pb, knh=knh: e.activation(out=knh, in_=pb[:], func=AF.Copy), [pbk], [knk])
                    S.op("act", lambda e, knh=knh: e.activation(out=sqbuf[1][:], in_=knh, func=AF.Square), [knk], ["sq1"])
                    i = rs["i"] % 2
                    rs["i"] += 1
                    sbank, skey = ps[4 + i], f"ps{4 + i}"
                    S.mm([(sbank[:], ones_bf[0:64, :], kpesq[:]), (sbank[:], ones_bf[:], sqbuf[1][:])],
                         ["kpesq2", "sq1"], [skey])
                    r, rk = rstd[i], f"rstd{i}"
                    S.op("act", lambda e, r=r, sbank=sbank: e.activation(out=r[:], in_=sbank[:], func=AF.Sqrt,
                                                                         bias=epst[:, 0:1], scale=1.0 / 192), [skey], [rk])
                    S.op("dve", lambda e, r=r: e.reciprocal(r[:], r[:]), [rk], [rk])
                    S.op("dve", lambda e, r=r, h=h, knh=knh: e.scalar_tensor_tensor(kout[:, h % 2, :], knh, gkn[:, 0:1], r[:],
                                                                                  ALU.mult, ALU.mult),
                         [knk, rk], [f"kout{h % 2}"])
                    S.op("dve", lambda e, r=r, h=h: e.tensor_tensor(krout[:, h % 2, :], kper[:], r[0:64, :], ALU.mult),
                         ["kper", rk], [f"krout{h % 2}"])
                    S.dma("sp", kloc[h * 192:h * 192 + 128, g * G:(g + 1) * G], kout[:, h % 2, :],
                          [f"kout{h % 2}"], [f"kloc.{g}.{h}"], f"stk{h % 2}")
                    S.dma("sp", kloc[h * 192 + 128:h * 192 + 192, g * G:(g + 1) * G], krout[:, h % 2, :],
                          [f"krout{h % 2}"], [f"klocr.{g}.{h}"], f"stk{h % 2}")
                for h3 in range(4):
                    wv, wk = wload(w_ukv, 0, 4, h3 * 768, 768)
                    for j in range(4):
                        pb, pbk = pbank(0, 4)
                        S.mm([(pb[:, 0:384].rearrange("p (a d) -> p a d", a=3),
                               ckvn[:, k, j * 128:(j + 1) * 128],
                               wv[:, k, :].rearrange("p (a d) -> p a d", a=3)[:, :, 128:256]) for k in range(4)],
                             [wk, "ckvn"], [pbk])
                        S.op("act", lambda e, pb=pb, j=j, h3=h3: e.activation(
                            out=vsb[:, j, h3 * 3:h3 * 3 + 3, :],
                            in_=pb[:, 0:384].rearrange("p (a d) -> p a d", a=3), func=AF.Copy),
                            [pbk], [f"vsb{j}"])
                        if h3 == 3:
                            tok0 = (4 * g + j) * 128
                            S.dma("sp", vloc.rearrange("(h t) d -> t h d", h=H)[tok0:tok0 + 128, :, :],
                                  vsb[:, j, :, :], [f"vsb{j}"], [f"vloc.{g}.{j}"], f"stv{j % 2}")
                if not own:
                    S.op("dve", lambda e: e.tensor_copy(xtl[:, :, jsup * 32:(jsup + 1) * 32], xn[:, :, 480:512]),
                         ["xn"], ["xtl"])
                S.fence()
            S.fence()

    if STAGE == "A":
        S.final()
        return nc
    S.fence(("pe", "act", "dve", "sp", "pool"))
    with ExitStack() as est:
        tailf = est.enter_context(nc.sbuf_tensor("tailf", [128, CC, 512], F32))
        sgt2 = [est.enter_context(nc.sbuf_tensor(f"sgt2_{i}", [128, 512], F32)) for i in range(2)]
        for c2 in range(CC // 2):
            wv, wk = wload(w_in, 0, DC, O_CV + c2 * 256, 256)
            wg_, wgk = wload(w_in, 0, DC, O_CG + c2 * 256, 256)
            for q in range(2):
                cc = c2 * 2 + q
                pv, pvk = pbank(0, 4)
                pg, pgk = pbank(0, 4)
                S.mm([(pv[:], wv[:, k, q * 128:(q + 1) * 128], xtl[:, k, :]) for k in range(DC)], [wk, "xtl"], [pvk])
                S.mm([(pg[:], wg_[:, k, q * 128:(q + 1) * 128], xtl[:, k, :]) for k in range(DC)], [wgk, "xtl"], [pgk])
                S.op("act", lambda e, pg=pg, cc=cc: e.activation(out=sgt2[cc % 2][:], in_=pg[:], func=AF.Sigmoid),
                     [pgk], [f"sgt2_{cc % 2}"])
                S.op("dve", lambda e, pv=pv, cc=cc: e.tensor_tensor(tailf[:, cc, :], sgt2[cc % 2][:], pv[:], ALU.mult),
                     [pvk, f"sgt2_{cc % 2}"], ["tailf"])
        S.dma("sp", tloc.rearrange("(c p) n -> p c n", p=128), tailf[:], ["tailf"], ["tloc"], "st1")
        S.fence(("pe", "act", "dve", "sp", "pool"))
    esA.close()
    S.fence(("pe", "act", "dve", "sp", "pool"))
    with ExitStack() as es:
        hacc = es.enter_context(nc.sbuf_tensor("hacc", [128, CC, 512], F32))
        tb = es.enter_context(nc.sbuf_tensor("trk", [128, CC, 512], F32))
        S.dma("sp", tb[:], tloc.rearrange("(c p) n -> p c n", p=128), [], ["trk"], "ld0")
        S.op("dve", lambda e: e.tensor_scalar(hacc[:], tb[:], sel[:, 0:1], None, ALU.mult), ["trk"], ["hacc"])
        S.op("dve", lambda e: e.scalar_tensor_tensor(hacc[:, :, 32:512], tb[:, :, 0:480], sel[:, 1:2],
                                                     hacc[:, :, 32:512], ALU.mult, ALU.add), ["trk", "hacc"], ["hacc"])
        S.dma("sp", hd.rearrange("(c p) n -> p c n", p=128), hacc[:], ["hacc"], ["hd"], "st1")
        S.fence()
    if STAGE == "C":
        S.final()
        return nc
    hT = sb("hT", [128, DC, G], BF16)
    mT = sb("mT", [128, DC, G], BF16)
    for g in range(NG if STAGE == "full" else 1):
        J = g
        with ExitStack() as es:
            sbt = lambda name, shape, dt: es.enter_context(nc.sbuf_tensor(f"{name}_b{g}", shape, dt))
            load_cs_own(g)
            with ExitStack() as es1:
                x1T = es1.enter_context(nc.sbuf_tensor(f"x1Ta_b{g}", [128, DC, G], F32))
                S.dma("sp", x1T[:], x1d[:, :, g * G:(g + 1) * G], [], ["x1T"], "ld0")
                rmsnorm_fm(x1T, "x1T", DC, gm, 2048, hT, "hT")
                S.fence()
            merged = sbt("merged", [128, DC, G], F32)
            gsb = [sbt(f"gsb{i}", [128, G], F32) for i in range(2)]

            def branch_out(br, wy, KCy, act_in, akey):
                for m2 in range(DC // 2):
                    wv, wk = wload(wy, 0, KCy, m2 * 256, 256)
                    wg_, wgk = wload(w_in, 0, DC, O_GATE + br * D + m2 * 256, 256)
                    for q in range(2):
                        m = m2 * 2 + q
                        py, pyk = pbank(0, 4)
                        pg, pgk = pbank(0, 4)
                        S.mm([(py[:], wv[:, k, q * 128:(q + 1) * 128], act_in[:, k, :]) for k in range(KCy)],
                             [wk, akey], [pyk])
                        S.mm([(pg[:], wg_[:, k, q * 128:(q + 1) * 128], hT[:, k, :]) for k in range(DC)],
                             [wgk, "hT"], [pgk])
                        gb, gbk = gsb[m % 2], f"gsb{m % 2}"
                        S.op("act", lambda e, pg=pg, gb=gb, m=m: e.activation(
                            out=gb[:], in_=pg[:], func=AF.Sigmoid, bias=bg[:, br * DC + m:br * DC + m + 1]),
                            [pgk], [gbk])
                        if br == 0:
                            S.op("dve", lambda e, py=py, gb=gb, m=m: e.tensor_tensor(merged[:, m, :], gb[:], py[:],
                                                                                    ALU.mult), [pyk, gbk], ["merged"])
                        else:
                            S.op("dve", lambda e, py=py, gb=gb: e.tensor_tensor(gb[:], gb[:], py[:], ALU.mult),
                                 [pyk, gbk], [gbk])
                            S.op("dve", lambda e, gb=gb, m=m: e.tensor_tensor(merged[:, m, :], merged[:, m, :], gb[:],
                                                                             ALU.add), [gbk, "merged"], ["merged"])

            with ExitStack() as es2:
                sb2 = lambda name, shape, dt: es2.enter_context(nc.sbuf_tensor(f"{name}_b{g}", shape, dt))
                HC = CC // 2
                apad = sb2("apad", [128, HC, 4, 160], F32)
                cacc = sb2("cacc", [128, CC, G], F32)
                an = sb2("an", [128, CC, G], BF16)
                sgc = [sb2(f"sgc{i}", [128, G], F32) for i in range(2)]
                meansb = sb2("meansb", [128, G], F32)
                for half in range(2):
                    for cl in range(HC):
                        S.dma("sp", apad[:, cl, :, 0:32],
                              hd.rearrange("(c p) (j t) -> p c j t", p=128, t=32)[:, half * HC + cl, 4 * g:4 * g + 4, :],
                              ["hd"], [f"apadh{cl}"], "ld0")
                    for c2 in range(HC // 2):
                        wv, wk = wload(w_in, 0, DC, O_CV + half * HC * 128 + c2 * 256, 256)
                        wg_, wgk = wload(w_in, 0, DC, O_CG + half * HC * 128 + c2 * 256, 256)
                        for q in range(2):
                            cl = c2 * 2 + q
                            pv, pvk = pbank(0, 4)
                            pg, pgk = pbank(0, 4)
                            S.mm([(pv[:], wv[:, k, q * 128:(q + 1) * 128], hT[:, k, :]) for k in range(DC)], [wk, "hT"], [pvk])
                            S.mm([(pg[:], wg_[:, k, q * 128:(q + 1) * 128], hT[:, k, :]) for k in range(DC)], [wgk, "hT"], [pgk])
                            sg_, sgk = sgc[cl % 2], f"sgc{cl % 2}"
                            S.op("act", lambda e, pg=pg, sg_=sg_: e.activation(out=sg_[:], in_=pg[:], func=AF.Sigmoid),
                                 [pgk], [sgk])
                            S.op("dve", lambda e, pv=pv, sg_=sg_, cl=cl: e.tensor_tensor(
                                apad[:, cl, :, 32:160], sg_[:].rearrange("p (j t) -> p j t", j=4),
                                pv[:].rearrange("p (j t) -> p j t", j=4), ALU.mult), [pvk, sgk], [f"apad{cl}"])
                    for k in range(31):
                        for cl in range(HC):
                            cc = half * HC + cl
                            src = apad[:, cl, :, 2 + k:130 + k]
                            dst = cacc[:, cc, :].rearrange("p (j t) -> p j t", j=4)
                            if k == 0:
                                S.op("dve", lambda e, src=src, dst=dst, cc=cc: e.tensor_scalar(
                                    dst, src, cw[:, cc, 0:1], cb[:, cc:cc + 1], ALU.mult, ALU.add),
                                    [f"apad{cl}", f"apadh{cl}"], [f"cacc{cc}"])
                            else:
                                S.op("dve", lambda e, src=src, dst=dst, cc=cc, k=k: e.scalar_tensor_tensor(
                                    dst, src, cw[:, cc, k:k + 1], dst, ALU.mult, ALU.add),
                                    [f"apad{cl}", f"apadh{cl}", f"cacc{cc}"], [f"cacc{cc}"])
                pmean, pmk = ps[4], "ps4"
                pms, pmsk = ps[5], "ps5"
                S.mm([(pmean[:], inv[1536][:], cacc[:, cc, :]) for cc in range(CC)],
                     [f"cacc{cc}" for cc in range(CC)], [pmk])
                for cc in range(CC):
                    sq, sqk = sqbuf[cc % 2], f"sq{cc % 2}"
                    S.op("act", lambda e, sq=sq, cc=cc: e.activation(out=sq[:], in_=cacc[:, cc, :], func=AF.Square),
                         [f"cacc{cc}"], [sqk])
                    S.mm([(pms[:], ones_bf[:], sq[:])], [sqk], [pmsk], start=(cc == 0), stop=(cc == CC - 1))
                S.op("act", lambda e: e.activation(out=meansb[:], in_=pmean[:], func=AF.Copy), [pmk], ["meansb"])
                S.op("dve", lambda e: e.tensor_tensor(rstd[0][:], meansb[:], meansb[:], ALU.mult), ["meansb"], ["rstd0"])
                S.op("dve", lambda e: e.scalar_tensor_tensor(rstd[0][:], pms[:], 1.0 / 1536, rstd[0][:], ALU.mult, ALU.subtract),
                     [pmsk, "rstd0"], ["rstd0"])
                S.op("act", lambda e: e.activation(out=rstd[0][:], in_=rstd[0][:], func=AF.Sqrt, bias=epst[:, 0:1]),
                     ["rstd0"], ["rstd0"])
                S.op("dve", lambda e: e.reciprocal(rstd[0][:], rstd[0][:]), ["rstd0"], ["rstd0"])
                for cc in range(CC):
                    S.op("dve", lambda e, cc=cc: e.tensor_tensor(cacc[:, cc, :], cacc[:, cc, :], meansb[:], ALU.subtract),
                         [f"cacc{cc}", "meansb"], [f"cacc{cc}"])
                    S.op("dve", lambda e, cc=cc: e.tensor_tensor(cacc[:, cc, :], cacc[:, cc, :], rstd[0][:], ALU.mult),
                         [f"cacc{cc}", "rstd0"], [f"cacc{cc}"])
                    S.op("act", lambda e, cc=cc: e.activation(out=an[:, cc, :], in_=cacc[:, cc, :], func=AF.Silu,
                                                              bias=lnb[:, cc:cc + 1], scale=lng[:, cc:cc + 1]),
                         [f"cacc{cc}"], ["an"])
                branch_out(0, w_co, CC, an, "an")
                S.fence()

            with ExitStack() as es2:
                sb2 = lambda name, shape, dt: es2.enter_context(nc.sbuf_tensor(f"{name}_b{g}", shape, dt))
                QN = sb2("QN", [128, H, G], BF16)
                QR = sb2("QR", [64, H, G], BF16)
                oT = sb2("oT", [128, H, G], BF16)
                es3 = ExitStack()
                sb3 = lambda name, shape, dt: es3.enter_context(nc.sbuf_tensor(f"{name}_b{g}", shape, dt))
                cq = sb3("cq", [128, 4, G], F32)
                cqn = sb3("cqn", [128, 4, G], BF16)
                qnf = sb3("qnf", [128, G], F32)
                qrf = sb3("qrf", [64, G], F32)
                qrg = sb3("qrg", [64, G], F32)
                qrr = sb3("qrr", [64, G], F32)
                t1 = sb3("t1", [64, G], F32)
                t2 = sb3("t2", [64, G], F32)
                for m2 in range(2):
                    wv, wk = wload(w_in, 0, DC, O_CQ + m2 * 256, 256)
                    for q in range(2):
                        pb, pbk = pbank(0, 4)
                        S.mm([(pb[:], wv[:, k, q * 128:(q + 1) * 128], hT[:, k, :]) for k in range(DC)], [wk, "hT"], [pbk])
                        S.op("act", lambda e, pb=pb, m=m2 * 2 + q: e.activation(out=cq[:, m, :], in_=pb[:], func=AF.Copy),
                             [pbk], ["cq"])
                rmsnorm_fm(cq, "cq", 4, gqa, 512, cqn, "cqn")
                wv, wk = None, None
                for h in range(H):
                    if h % 4 == 0:
                        wv, wk = wload(w_uq, 0, 4, h * 192, 768)
                    hq = h % 4
                    pn, pnk = pbank(0, 4)
                    pr_, prk_ = pbank(0, 4)
                    S.mm([(pn[:], wv[:, k, hq * 192:hq * 192 + 128], cqn[:, k, :]) for k in range(4)], [wk, "cqn"], [pnk])
                    S.mm([(pr_[0:64, :], wv[:, k, hq * 192 + 128:hq * 192 + 192], cqn[:, k, :]) for k in range(4)],
                         [wk, "cqn"], [prk_])
                    S.op("act", lambda e, pn=pn: e.activation(out=qnf[:], in_=pn[:], func=AF.Copy), [pnk], ["qnf"])
                    S.op("act", lambda e, pr_=pr_: e.activation(out=qrf[:], in_=pr_[0:64, :], func=AF.Copy), [prk_], ["qrf"])
                    r, rk = stat_rstd([(qnf[:], 128), (qrf[:], 64)], 192, ["qnf", "qrf"])
                    S.op("dve", lambda e, r=r, h=h: e.scalar_tensor_tensor(QN[:, h, :], qnf[:], gqn[:, 0:1], r[:],
                                                                         ALU.mult, ALU.mult), ["qnf", rk], ["QN"])
                    S.op("dve", lambda e: e.tensor_scalar(qrg[:], qrf[:], gqr[:, 0:1], None, ALU.mult),
                         ["qrf"], ["qrg"])
                    rope_apply(qrg[:], "qrg", G, qrr[:], "qrr", t1[:], t2[:])
                    S.op("dve", lambda e, r=r, h=h: e.tensor_tensor(QR[:, h, :], qrr[:], r[0:64, :], ALU.mult),
                         ["qrr", rk], ["QR"])
                S.fence()
                es3.close()
                es3 = ExitStack()
                rden = sb3("rden", [128, G], F32)
                kn = [sb3(f"kn{i}", [128, 8, 128], BF16) for i in range(3)]
                kr = [sb3(f"kr{i}", [64, 8, 128], BF16) for i in range(3)]
                vv = [sb3(f"vv{i}", [128, 8, 128], BF16) for i in range(3)]
                PT = [sb3(f"PT{i}", [128, G], BF16) for i in range(3)]
                nsc = 4 * J + 4
                blocks = [(h, sc, kt) for h in range(H) for sc in range(nsc) for kt in range(8)]
                slot_of = {}
                kvs = {"i": 0}

                def ensure_load(h, sc):
                    if (h, sc) in slot_of:
                        return
                    i = kvs["i"] % 3
                    kvs["i"] += 1
                    slot_of[(h, sc)] = i
                    kkey = f"kv{i}"
                    S.dma("sp", kn[i][:].rearrange("p r t -> p (r t)"),
                          kloc[h * 192:h * 192 + 128, sc * 1024:(sc + 1) * 1024], [], [kkey], f"kv{i}")
                    S.dma("sp", kr[i][:].rearrange("p r t -> p (r t)"),
                          kloc[h * 192 + 128:h * 192 + 192, sc * 1024:(sc + 1) * 1024], [], [kkey], f"kv{i}")
                    S.dma("sp", vv[i][:], vloc.rearrange("(h n s t) d -> t h n s d", h=H, s=8, t=128)[:, h, sc, :, :],
                          [], [kkey], f"kv{i}")

                def emit_qk(b):
                    h, sc, kt = blocks[b]
                    ensure_load(h, sc)
                    if kt == 0:
                        if sc + 1 < nsc:
                            ensure_load(h, sc + 1)
                        elif h + 1 < H:
                            ensure_load(h + 1, 0)
                    i = slot_of[(h, sc)]
                    q0 = max(0, sc - 4 * J) * 128
                    N = G - q0
                    pS, pSk = pbank(0, 4)
                    S.mm([(pS[:, 0:N], kn[i][:, kt, :], QN[:, h, q0:G]), (pS[:, 0:N], kr[i][:, kt, :], QR[:, h, q0:G])],
                         [f"kv{i}", "QN", "QR"], [pSk])
                    return pS, pSk, q0, N, i

                pend = emit_qk(0)
                for b, (h, sc, kt) in enumerate(blocks):
                    pS, pSk, q0, N, i = pend
                    pend = emit_qk(b + 1) if b + 1 < len(blocks) else None
                    if h % 2 == 0:
                        po, pok, pd, pdk = ps[4], "ps4", ps[5], "ps5"
                    else:
                        po, pok, pd, pdk = ps[6], "ps6", ps[7], "ps7"
                    P_, Pk = PT[b % 3], f"PT{b % 3}"
                    S.op("act", lambda e, pS=pS, P_=P_, N=N: e.activation(out=P_[:, 0:N], in_=pS[:, 0:N], func=AF.Exp,
                                                                         scale=MLA_SCALE), [pSk], [Pk])
                    if sc >= 4 * J:
                        S.op("dve", lambda e, P_=P_, kt=kt: e.tensor_tensor(P_[:, 0:128], P_[:, 0:128], mask[:, kt, :],
                                                                            ALU.mult), [Pk], [Pk])
                    first = (sc == 0 and kt == 0)
                    last = (sc == nsc - 1 and kt == 7)
                    S.mm([(po[:, q0:G], vv[i][:, kt, :], P_[:, 0:N])], [f"kv{i}", Pk], [pok], start=first, stop=last)
                    S.mm([(pd[:, q0:G], ones_bf[:], P_[:, 0:N])], [Pk], [pdk], start=first, stop=last)
                    if last:
                        S.op("dve", lambda e, pd=pd: e.reciprocal(rden[:], pd[:]), [pdk], ["rden"])
                        S.op("dve", lambda e, h=h, po=po: e.tensor_tensor(oT[:, h, :], po[:], rden[:], ALU.mult),
                             [pok, "rden"], ["oT"])
                S.fence()
                es3.close()
                branch_out(1, w_mo, H, oT, "oT")
                S.fence()

            with ExitStack() as es2:
                sb2 = lambda name, shape, dt: es2.enter_context(nc.sbuf_tensor(f"{name}_b{g}", shape, dt))
                mq = sb2("mq", [128, 8, G], F32)
                mqn = sb2("mqn", [128, 8, G], BF16)
                om = sb2("om", [128, 8, G], BF16)
                PM = [sb2(f"PM{i}", [128, G], BF16) for i in range(2)]
                rden = sb2("rdenm", [128, G], F32)
                for m2 in range(4):
                    wv, wk = wload(w_in, 0, DC, O_MQ + m2 * 256, 256)
                    for q in range(2):
                        pb, pbk = pbank(0, 4)
                        S.mm([(pb[:], wv[:, k, q * 128:(q + 1) * 128], hT[:, k, :]) for k in range(DC)], [wk, "hT"], [pbk])
                        S.op("act", lambda e, pb=pb, m=m2 * 2 + q: e.activation(out=mq[:, m, :], in_=pb[:], func=AF.Copy),
                             [pbk], ["mq"])
                for hh in range(4):
                    r, rk = stat_rstd([(mq[:, 2 * hh + q, :], 128) for q in range(2)], 256, ["mq"])
                    for q in range(2):
                        S.op("dve", lambda e, hh=hh, q=q, r=r: e.scalar_tensor_tensor(
                            mqn[:, 2 * hh + q, :], mq[:, 2 * hh + q, :], gmq[:, q:q + 1], r[:], ALU.mult, ALU.mult),
                            ["mq", rk], ["mqn"])
                    for kt in range(2):
                        pS, pSk = pbank(0, 4)
                        S.mm([(pS[:], mkT[:, 2 * hh + q, kt * 128:(kt + 1) * 128], mqn[:, 2 * hh + q, :]) for q in range(2)],
                             ["mkT", "mqn"], [pSk])
                        S.op("act", lambda e, pS=pS, kt=kt: e.activation(out=PM[kt][:], in_=pS[:], func=AF.Exp, scale=MEM_SCALE),
                             [pSk], [f"PM{kt}"])
                    pd, pdk = ps[7], "ps7"
                    S.mm([(pd[:], ones_bf[:], PM[kt][:]) for kt in range(2)], ["PM0", "PM1"], [pdk])
                    S.op("dve", lambda e: e.reciprocal(rden[:], pd[:]), [pdk], ["rdenm"])
                    for e_ in range(2):
                        po, pok = ps[6], "ps6"
                        S.mm([(po[:], mv[:, kt, hh * 256 + e_ * 128:hh * 256 + (e_ + 1) * 128], PM[kt][:]) for kt in range(2)],
                             ["PM0", "PM1", "mv"], [pok])
                        S.op("dve", lambda e, hh=hh, e_=e_: e.tensor_tensor(om[:, 2 * hh + e_, :], po[:], rden[:], ALU.mult),
                             [pok, "rdenm"], ["om"])
                branch_out(2, w_meo, 8, om, "om")
                S.fence()

            for m in range(DC):
                S.op("act", lambda e, m=m: e.activation(out=mT[:, m, :], in_=merged[:, m, :], func=AF.Copy), ["merged"], ["mT"])
            S.fence()
        with ExitStack() as es:
            sbt = lambda name, shape, dt: es.enter_context(nc.sbuf_tensor(f"{name}_c{g}", shape, dt))
            x1T = sbt("x1T", [128, DC, G], F32)
            hff = sbt("hff", [128, FC, G], BF16)
            otok = sbt("otok", [128, 2, D], F32)
            S.dma("sp", x1T[:], x1d[:, :, g * G:(g + 1) * G], [], ["x1T"], "ld0")
            for m2 in range(DC // 2):
                wv, wk = wload(w_out, 0, DC, m2 * 256, 256)
                for q in range(2):
                    m = m2 * 2 + q
                    pb, pbk = pbank(0, 4)
                    S.mm([(pb[:], wv[:, k, q * 128:(q + 1) * 128], mT[:, k, :]) for k in range(DC)], [wk, "mT"], [pbk])
                    S.op("dve", lambda e, pb=pb, m=m: e.tensor_tensor(x1T[:, m, :], x1T[:, m, :], pb[:], ALU.add),
                         [pbk, "x1T"], ["x1T"])
            rmsnorm_fm(x1T, "x1T", DC, g2, 2048, hT, "hT")
            ffn(x1T, "x1T", hT, "hT", hff, w2gu, w2d)
            for j in range(4):
                for c4 in range(4):
                    pb, pbk = pbank(0, 4)
                    S.tr([(pb[:, q * 128:(q + 1) * 128], x1T[:, c4 * 4 + q, j * 128:(j + 1) * 128]) for q in range(4)],
                         ident[:], ["x1T"], [pbk])
                    S.op("act" if c4 % 2 else "dve",
                         (lambda e, pb=pb, j=j, c4=c4: e.activation(out=otok[:, j % 2, c4 * 512:(c4 + 1) * 512], in_=pb[:],
                                                                    func=AF.Copy)) if c4 % 2 else
                         (lambda e, pb=pb, j=j, c4=c4: e.tensor_copy(otok[:, j % 2, c4 * 512:(c4 + 1) * 512], pb[:])),
                         [pbk], [f"otok{j % 2}"])
                S.dma("sp", out_d[4 * g + j], otok[:, j % 2, :], [f"otok{j % 2}"], [f"outd.{g}.{j}"], f"sto{j % 2}")
            S.fence()
    S.final()
    print('KERNEL STATS', S.cnt, 'waits', S.nwait, 'dma', {k: v[1] // 16 for k, v in S.dsem.items()}, flush=True)
    return nc


_CACHE = {}


def _fm(v, C):
    return np.ascontiguousarray(np.asarray(v, np.float32).reshape(C, 128).T)


def kernel(**inp):
    f32 = np.float32
    x = np.asarray(inp["x"], f32)[0]
    mem = np.ascontiguousarray(np.asarray(inp["mem"], f32)[0])
    pos = np.asarray(inp["positions"])[0].astype(np.int32)
    sq = lambda k: np.ascontiguousarray(np.asarray(inp[k], f32)[0])
    common = {
        "mem": mem,
        "g_ffn1": _fm(sq("g_ffn1"), DC), "g_mix": _fm(sq("g_mix"), DC), "g_ffn2": _fm(sq("g_ffn2"), DC),
        "g_mem": _fm(sq("g_mem"), DC), "b_gate": _fm(sq("b_gate"), 48),
        "conv_w": np.ascontiguousarray(sq("conv_w").T.reshape(CC, 128, 31).transpose(1, 0, 2)),
        "conv_b": _fm(sq("conv_b"), CC), "conv_ln_g": _fm(sq("conv_ln_g"), CC), "conv_ln_b": _fm(sq("conv_ln_b"), CC),
        "g_q_a": _fm(sq("g_q_a"), 4), "g_kv_a": _fm(sq("g_kv_a"), 4),
        "gqn": sq("g_qnorm")[0:128].reshape(128, 1).copy(), "gqr": sq("g_qnorm")[128:192].reshape(64, 1).copy(),
        "gkn": sq("g_knorm")[0:128].reshape(128, 1).copy(), "gkr": sq("g_knorm")[128:192].reshape(64, 1).copy(),
        "g_mqnorm": _fm(sq("g_mqnorm"), 2), "g_mknorm": _fm(sq("g_mknorm"), 2),
        "ident": np.eye(128, dtype=f32),
        "w_ffn1_gu": sq("w_ffn1_gu"), "w_ffn1_down": sq("w_ffn1_down"), "w_in": sq("w_in"),
        "w_conv_out": sq("w_conv_out"), "w_uq": sq("w_uq"), "w_ukv": sq("w_ukv"), "w_mla_out": sq("w_mla_out"),
        "w_mem_kv": sq("w_mem_kv"), "w_mem_out": sq("w_mem_out"), "w_out": sq("w_out"),
        "w_ffn2_gu": sq("w_ffn2_gu"), "w_ffn2_down": sq("w_ffn2_down"),
    }
    rot = np.zeros((64, 64), f32)
    for i in range(32):
        rot[i + 32, i] = 1.0
        rot[i, i + 32] = -1.0
    common["rot"] = rot
    invf = (10000.0 ** (-np.arange(0, 64, 2, dtype=f32) / f32(64))).astype(f32)
    common["invf"] = np.concatenate([invf, invf]).reshape(64, 1).astype(f32)
    tri = (np.arange(128)[:, None] <= np.arange(128)[None, :]).astype(f32)
    in_maps = []
    xg = x.reshape(NT, NCORE, 128, D)
    pg = pos.reshape(NT, NCORE, 128)
    for c in range(NCORE):
        m = dict(common)
        order = [(c + s_) % NCORE for s_ in range(NCORE)]
        m["x"] = np.ascontiguousarray(xg[:, order]).reshape(NTA, 128, D)
        pl = np.ascontiguousarray(pg[:, order]).reshape(1, TA)
        m["pos"] = np.ascontiguousarray(np.broadcast_to(pl, (64, TA))).astype(np.int32)
        mk = np.zeros((128, 8, 128), f32)
        mk[:, 0, :] = tri
        if c >= 1:
            mk[:, 8 - c:, :] = 1.0
        m["mask"] = mk
        sel = np.zeros((128, 8), f32)
        sel[:, 0] = 1.0 if c >= 1 else 0.0
        sel[:, 1] = 1.0 if c == 0 else 0.0
        m["sel"] = sel
        in_maps.append(m)
    if "nc" not in _CACHE:
        _CACHE["nc"] = build()
    res = run_bass_kernel_spmd(_CACHE["nc"], in_maps, core_ids=list(range(NCORE)))
    out = np.empty((NT, NCORE, 128, D), f32)
    for c in range(NCORE):
        out[:, c] = np.asarray(res.results[c]["out"], f32).reshape(NT, 128, D)
    return out.reshape(1, NT * NCORE * 128, D)
```
